# Optimizing a Trainium2 kernel written in Bass

```python
import jax, jax.numpy as jnp
from jax import lax
import numpy as np

D_MODEL = 1024
BATCH = 16
SEQ = 2048
DEPTH = 2
DEC_BATCH = 32
DEC_SEQ = 16
PAST_LEN = 2048

CHUNK = 64
SSD_HEADS = 8
SSD_HEAD_DIM = 64
D_SSD = SSD_HEADS * SSD_HEAD_DIM
SSD_GROUPS = 2
D_STATE = 64
SSD_CONV = 4
D_XBC = D_SSD + 2 * SSD_GROUPS * D_STATE
D_SSD_IN = D_SSD + D_XBC + SSD_HEADS
ATTN_HEADS = 8
KV_HEADS = 2
HEAD_DIM = 64
D_ATTN = ATTN_HEADS * HEAD_DIM
D_KV = KV_HEADS * HEAD_DIM
WINDOW = 128
WIN_CHUNKS = WINDOW // CHUNK
D_MIX = D_SSD + D_ATTN
D_IN = D_SSD_IN + D_ATTN + 2 * D_KV
D_FF = 2816
FFN_CONV = 3
EPS = 1e-6

kernel_name = 'hybrid_ssd_swa_streaming_encoder'


def rms_norm(x, g):
    xf = x.astype(jnp.float32)
    y = xf * lax.rsqrt(jnp.mean(xf * xf, axis=-1, keepdims=True) + EPS)
    return (y * g.astype(jnp.float32)).astype(x.dtype)


def causal_dwconv(u, prev, w, b):
    K = w.shape[0]
    L = u.shape[1]
    up = jnp.concatenate([prev.astype(u.dtype), u], axis=1)
    y = b
    for k in range(K):
        y = y + up[:, k:k + L] * w[k]
    return y, up[:, up.shape[1] - (K - 1):]


def ssd_scan(x, dt, a, bm, cm, h0):
    f32 = jnp.float32
    Bsz, L, H, P = x.shape
    N = bm.shape[-1]
    Q = min(CHUNK, L)
    nc = L // Q
    xc = x.reshape(Bsz, nc, Q, H, P).astype(f32)
    dtc = dt.reshape(Bsz, nc, Q, H).astype(f32)
    bc = bm.reshape(Bsz, nc, Q, H, N).astype(f32)
    cc = cm.reshape(Bsz, nc, Q, H, N).astype(f32)
    acs = jnp.cumsum(dtc * a, axis=2)
    seg = acs[:, :, :, None, :] - acs[:, :, None, :, :]
    causal = jnp.tril(jnp.ones((Q, Q), dtype=bool))[None, None, :, :, None]
    decay = jnp.exp(jnp.where(causal, seg, -jnp.inf))
    scores = jnp.einsum('bcthn,bcshn->bctsh', cc, bc) * decay
    y_diag = jnp.einsum('bctsh,bcsh,bcshp->bcthp', scores, dtc, xc)
    decay_end = jnp.exp(acs[:, :, -1:, :] - acs)
    chunk_states = jnp.einsum('bcsh,bcshn,bcshp->bchpn', decay_end * dtc, bc, xc)
    chunk_decay = jnp.exp(acs[:, :, -1, :])

    def step(h, inp):
        s, d = inp
        return d[:, :, None, None] * h + s, h

    h_last, h_prev = lax.scan(step, h0.astype(f32),
                              (jnp.moveaxis(chunk_states, 1, 0), jnp.moveaxis(chunk_decay, 1, 0)))
    h_prev = jnp.moveaxis(h_prev, 0, 1)
    y_off = jnp.einsum('bcthn,bchpn->bcthp', cc, h_prev) * jnp.exp(acs)[..., None]
    y = (y_diag + y_off).reshape(Bsz, L, H, P)
    return y, h_last.astype(h0.dtype)


def ssd_mixer(zxbcdt, conv_prev, h0, conv_w, conv_b, dt_bias, a_log, d_skip, norm_g):
    Bsz, L, _ = zxbcdt.shape
    z = zxbcdt[..., :D_SSD]
    xbc = zxbcdt[..., D_SSD:D_SSD + D_XBC]
    dt_raw = zxbcdt[..., D_SSD + D_XBC:]
    xbc, conv_new = causal_dwconv(xbc, conv_prev, conv_w, conv_b)
    xbc = jax.nn.silu(xbc)
    gn = SSD_GROUPS * D_STATE
    xs = xbc[..., :D_SSD].reshape(Bsz, L, SSD_HEADS, SSD_HEAD_DIM)
    rep = SSD_HEADS // SSD_GROUPS
    bm = jnp.repeat(xbc[..., D_SSD:D_SSD + gn].reshape(Bsz, L, SSD_GROUPS, D_STATE), rep, axis=2)
    cm = jnp.repeat(xbc[..., D_SSD + gn:].reshape(Bsz, L, SSD_GROUPS, D_STATE), rep, axis=2)
    dt = jax.nn.softplus(dt_raw.astype(jnp.float32) + dt_bias.astype(jnp.float32))
    a = -jnp.exp(a_log.astype(jnp.float32))
    y, h_new = ssd_scan(xs, dt, a, bm, cm, h0)
    y = y + d_skip.astype(jnp.float32)[:, None] * xs.astype(jnp.float32)
    y = y.reshape(Bsz, L, D_SSD) * jax.nn.silu(z.astype(jnp.float32))
    y = rms_norm(y, norm_g)
    return y.astype(zxbcdt.dtype), conv_new, h_new


def alibi_slopes():
    return jnp.asarray(2.0 ** (-8.0 * np.arange(1, ATTN_HEADS + 1) / ATTN_HEADS), dtype=jnp.float32)


def banded_sink_attention(q, k, v, valid, sinks):
    Bsz, N, Tq, H, D = q.shape
    Tk = k.shape[2]
    rep = H // KV_HEADS
    qg = q.reshape(Bsz, N, Tq, KV_HEADS, rep, D)
    s = jnp.einsum('bnqgrd,bnkgd->bngrqk', qg, k).astype(jnp.float32) * (D ** -0.5)
    dist = jnp.abs(jnp.arange(Tq)[:, None] + WINDOW - jnp.arange(Tk)[None, :]).astype(jnp.float32)
    s = s - alibi_slopes().reshape(KV_HEADS, rep)[:, :, None, None] * dist
    s = jnp.where(valid[None, :, None, None, None, :], s, -jnp.inf)
    sink = sinks.astype(jnp.float32).reshape(KV_HEADS, rep, 1, 1)
    m = jnp.maximum(jnp.max(s, axis=-1, keepdims=True), sink)
    p = jnp.exp(s - m)
    p = p / (jnp.sum(p, axis=-1, keepdims=True) + jnp.exp(sink - m))
    o = jnp.einsum('bngrqk,bnkgd->bnqgrd', p.astype(v.dtype), v)
    return o.reshape(Bsz, N, Tq, H * D)


def swa_prompt(q, k, v, sinks):
    Bsz, L, H, D = q.shape
    nc = L // CHUNK
    qb = q.reshape(Bsz, nc, CHUNK, H, D)
    pad = jnp.zeros((Bsz, WINDOW, KV_HEADS, D), k.dtype)
    kp = jnp.concatenate([pad, k], axis=1).reshape(Bsz, nc + WIN_CHUNKS, CHUNK, KV_HEADS, D)
    vp = jnp.concatenate([pad, v], axis=1).reshape(Bsz, nc + WIN_CHUNKS, CHUNK, KV_HEADS, D)
    kb = jnp.concatenate([kp[:, i:i + nc] for i in range(WIN_CHUNKS + 1)], axis=2)
    vb = jnp.concatenate([vp[:, i:i + nc] for i in range(WIN_CHUNKS + 1)], axis=2)
    band = (WIN_CHUNKS + 1) * CHUNK
    valid = (jnp.arange(nc)[:, None] - WIN_CHUNKS + jnp.arange(band)[None, :] // CHUNK) >= 0
    o = banded_sink_attention(qb, kb, vb, valid, sinks).reshape(Bsz, L, D_ATTN)
    return o, k[:, L - WINDOW:], v[:, L - WINDOW:]


def swa_sample(q, k, v, k_cache, v_cache, sinks):
    kb = jnp.concatenate([k_cache.astype(k.dtype), k], axis=1)
    vb = jnp.concatenate([v_cache.astype(v.dtype), v], axis=1)
    valid = jnp.ones((1, kb.shape[1]), dtype=bool)
    o = banded_sink_attention(q[:, None], kb[:, None], vb[:, None], valid, sinks)[:, 0]
    return o, kb[:, kb.shape[1] - WINDOW:], vb[:, vb.shape[1] - WINDOW:]


def trunk_layer(x, c, lw, conv_prev, h0, k_cache, v_cache, ffn_prev):
    (w_ada, b_ada, norm_mix, w_in, conv_ssd_w, conv_ssd_b, dt_bias, a_log, d_skip, ssd_norm,
     q_norm, k_norm, sinks, w_out, norm_ffn, w_up, conv_ffn_w, conv_ffn_b, w_down) = lw
    Bsz, L, _ = x.shape
    mod = (jax.nn.silu(c) @ w_ada + b_ada)[:, None, :]
    sh1, sc1, g1, sh2, sc2, g2 = jnp.split(mod, 6, axis=-1)
    h = rms_norm(x, norm_mix) * (1 + sc1) + sh1
    proj = h @ w_in
    y_ssd, conv_new, h_new = ssd_mixer(proj[..., :D_SSD_IN], conv_prev, h0, conv_ssd_w, conv_ssd_b,
                                       dt_bias, a_log, d_skip, ssd_norm)
    q = proj[..., D_SSD_IN:D_SSD_IN + D_ATTN].reshape(Bsz, L, ATTN_HEADS, HEAD_DIM)
    k = proj[..., D_SSD_IN + D_ATTN:D_SSD_IN + D_ATTN + D_KV].reshape(Bsz, L, KV_HEADS, HEAD_DIM)
    v = proj[..., D_SSD_IN + D_ATTN + D_KV:].reshape(Bsz, L, KV_HEADS, HEAD_DIM)
    q = rms_norm(q, q_norm)
    k = rms_norm(k, k_norm)
    if k_cache is None:
        o, k_new, v_new = swa_prompt(q, k, v, sinks)
    else:
        o, k_new, v_new = swa_sample(q, k, v, k_cache, v_cache, sinks)
    mix = jnp.concatenate([y_ssd, o], axis=-1) @ w_out
    x = x + g1 * mix
    h = rms_norm(x, norm_ffn) * (1 + sc2) + sh2
    up, ffn_new = causal_dwconv(h @ w_up, ffn_prev, conv_ffn_w, conv_ffn_b)
    u, gt = jnp.split(up, 2, axis=-1)
    x = x + g2 * ((jax.nn.silu(gt) * u) @ w_down)
    return x, h_new, conv_new, k_new, v_new, ffn_new


def setup_inputs(seed: int = 0) -> dict:
    key = jax.random.key(seed)
    ks = jax.random.split(key, 32)
    nrm = lambda k, shape, s: jax.random.normal(k, shape, jnp.float32) * s
    dt0 = jnp.exp(jax.random.uniform(ks[17], (DEPTH, SSD_HEADS), jnp.float32,
                                     minval=np.log(1e-3), maxval=np.log(1e-1)))
    return {
        'x_prompt': nrm(ks[0], (BATCH, SEQ, D_MODEL), 1.0),
        'x_sample': nrm(ks[1], (DEC_BATCH, DEC_SEQ, D_MODEL), 1.0),
        'c_prompt': nrm(ks[2], (BATCH, D_MODEL), 1.0),
        'c_sample': nrm(ks[3], (DEC_BATCH, D_MODEL), 1.0),
        'state_ssm': nrm(ks[4], (DEPTH, DEC_BATCH, SSD_HEADS, SSD_HEAD_DIM, D_STATE), 0.1),
        'cache_conv_ssd': nrm(ks[5], (DEPTH, DEC_BATCH, SSD_CONV - 1, D_XBC), 1.0),
        'cache_attn_k': nrm(ks[6], (DEPTH, DEC_BATCH, WINDOW, KV_HEADS, HEAD_DIM), 1.0),
        'cache_attn_v': nrm(ks[7], (DEPTH, DEC_BATCH, WINDOW, KV_HEADS, HEAD_DIM), 1.0),
        'cache_conv_ffn': nrm(ks[8], (DEPTH, DEC_BATCH, FFN_CONV - 1, 2 * D_FF), 0.5),
        'w_ada': nrm(ks[9], (DEPTH, D_MODEL, 6 * D_MODEL), 0.5 * D_MODEL ** -0.5),
        'b_ada': nrm(ks[10], (DEPTH, 6 * D_MODEL), 0.02),
        'norm_mix': 1.0 + nrm(ks[11], (DEPTH, D_MODEL), 0.02),
        'w_in': nrm(ks[12], (DEPTH, D_MODEL, D_IN), D_MODEL ** -0.5),
        'conv_ssd_w': nrm(ks[13], (DEPTH, SSD_CONV, D_XBC), SSD_CONV ** -0.5),
        'conv_ssd_b': nrm(ks[14], (DEPTH, D_XBC), 0.02),
        'dt_bias': dt0 + jnp.log(-jnp.expm1(-dt0)),
        'a_log': jnp.log(jax.random.uniform(ks[15], (DEPTH, SSD_HEADS), jnp.float32, minval=1.0, maxval=16.0)),
        'd_skip': 1.0 + nrm(ks[16], (DEPTH, SSD_HEADS), 0.1),
        'ssd_norm': 1.0 + nrm(ks[18], (DEPTH, D_SSD), 0.02),
        'q_norm': 1.0 + nrm(ks[19], (DEPTH, HEAD_DIM), 0.02),
        'k_norm': 1.0 + nrm(ks[20], (DEPTH, HEAD_DIM), 0.02),
        'sinks': nrm(ks[21], (DEPTH, ATTN_HEADS), 1.0),
        'w_out': nrm(ks[22], (DEPTH, D_MIX, D_MODEL), D_MIX ** -0.5),
        'norm_ffn': 1.0 + nrm(ks[23], (DEPTH, D_MODEL), 0.02),
        'w_up': nrm(ks[24], (DEPTH, D_MODEL, 2 * D_FF), D_MODEL ** -0.5),
        'conv_ffn_w': nrm(ks[25], (DEPTH, FFN_CONV, 2 * D_FF), FFN_CONV ** -0.5),
        'conv_ffn_b': nrm(ks[26], (DEPTH, 2 * D_FF), 0.02),
        'w_down': nrm(ks[27], (DEPTH, D_FF, D_MODEL), D_FF ** -0.5),
    }


def reference(x_prompt, x_sample, c_prompt, c_sample, state_ssm, cache_conv_ssd, cache_attn_k,
              cache_attn_v, cache_conv_ffn, w_ada, b_ada, norm_mix, w_in, conv_ssd_w, conv_ssd_b,
              dt_bias, a_log, d_skip, ssd_norm, q_norm, k_norm, sinks, w_out, norm_ffn, w_up,
              conv_ffn_w, conv_ffn_b, w_down):
    bp = x_prompt.shape[0]
    dtp = x_prompt.dtype
    conv0 = jnp.zeros((bp, SSD_CONV - 1, D_XBC), dtp)
    h00 = jnp.zeros((bp, SSD_HEADS, SSD_HEAD_DIM, D_STATE), dtp)
    ffn0 = jnp.zeros((bp, FFN_CONV - 1, 2 * D_FF), dtp)
    xp, xs = x_prompt, x_sample
    ssm_p, ssm_s, cs_p, cs_s, k_p, k_s, v_p, v_s, cf_p, cf_s = ([] for _ in range(10))
    for l in range(DEPTH):
        lw = (w_ada[l], b_ada[l], norm_mix[l], w_in[l], conv_ssd_w[l], conv_ssd_b[l], dt_bias[l],
              a_log[l], d_skip[l], ssd_norm[l], q_norm[l], k_norm[l], sinks[l], w_out[l],
              norm_ffn[l], w_up[l], conv_ffn_w[l], conv_ffn_b[l], w_down[l])
        xp, h1, c1, k1, v1, f1 = trunk_layer(xp, c_prompt, lw, conv0, h00, None, None, ffn0)
        xs, h2, c2, k2, v2, f2 = trunk_layer(xs, c_sample, lw, cache_conv_ssd[l], state_ssm[l],
                                             cache_attn_k[l], cache_attn_v[l], cache_conv_ffn[l])
        ssm_p.append(h1); cs_p.append(c1); k_p.append(k1); v_p.append(v1); cf_p.append(f1)
        ssm_s.append(h2); cs_s.append(c2); k_s.append(k2); v_s.append(v2); cf_s.append(f2)
    return (xp, xs, jnp.stack(ssm_p), jnp.stack(ssm_s), jnp.stack(cs_p), jnp.stack(cs_s),
            jnp.stack(k_p), jnp.stack(k_s), jnp.stack(v_p), jnp.stack(v_s),
            jnp.stack(cf_p), jnp.stack(cf_s))
```

```python
import contextlib
import numpy as np
import concourse.bass as bass
import concourse.mybir as mybir
from concourse.bass_utils import run_bass_kernel_spmd

F32 = mybir.dt.float32
BF16 = mybir.dt.bfloat16
I32 = mybir.dt.int32
AF = mybir.ActivationFunctionType
ALU = mybir.AluOpType

NCORES = 8
NTOK = 4160
NP = 1088
EPS = 1e-6
NEG = -30000.0
COMPUTE = ("pe", "act", "dve", "pool")
NDMA_SEMS = 24
TAGS = False


def UCONV_DVE(f):
    return True


class Res:
    __slots__ = ("name", "w", "r")

    def __init__(self, name):
        self.name = name
        self.w = None
        self.r = []


class Prog:
    def __init__(self, nc, stack):
        self.nc = nc
        self.streams = {e: [] for e in ("pe", "act", "dve", "pool", "sp")}
        self.cnt = {e: 0 for e in COMPUTE}
        self.sems = {}
        for e in COMPUTE:
            self.sems[e] = stack.enter_context(nc.semaphore("sem_" + e))
        for q in ("sp", "pool"):
            for i in range(NDMA_SEMS):
                k = "d_%s_%d" % (q, i)
                self.sems[k] = stack.enter_context(nc.semaphore(k))
        self.dma_cnt = {}
        self.dma_rr = {"sp": 0, "pool": 0}
        self.waited = {e: {} for e in self.streams}
        self.n_ops = 0
        self.tag = ""
        self.tagmap = {}

    def res(self, name):
        return Res(name)

    def _deps(self, eng, reads, writes):
        out = {}
        for r in reads:
            if r.w is not None:
                k, v, e = r.w
                if not (e == "pe" and eng == "pe") and out.get(k, 0) < v:
                    out[k] = v
        for r in writes:
            if r.w is not None:
                k, v, e = r.w
                if not (e == "pe" and eng == "pe") and out.get(k, 0) < v:
                    out[k] = v
            for (k, v, e) in r.r:
                if not (e == "pe" and eng == "pe") and out.get(k, 0) < v:
                    out[k] = v
        w = self.waited[eng]
        res = []
        for k, v in out.items():
            if w.get(k, 0) < v:
                w[k] = v
                res.append((k, v))
        return res

    def _commit(self, tok, reads, writes):
        for r in writes:
            r.w = tok
            r.r = []
        for r in reads:
            r.r = [t for t in r.r if t[0] != tok[0]] + [tok]

    def op(self, eng, fn, reads=(), writes=()):
        ex = [r for r in reads if r.name.startswith("bk")]
        if ex:
            reads = [r for r in reads if not r.name.startswith("bk")]
            writes = list(writes) + ex
        waits = self._deps(eng, reads, writes)
        self.cnt[eng] += 1
        tok = (eng, self.cnt[eng], eng)
        self.streams[eng].append((waits, fn, (eng, 1), self.tag))
        self._commit(tok, reads, writes)
        self.n_ops += 1

    def dma(self, queue, out, in_, reads=(), writes=()):
        i = self.dma_rr[queue]
        self.dma_rr[queue] = (i + 1) % NDMA_SEMS
        k = "d_%s_%d" % (queue, i)
        prev = self.dma_cnt.get(k, 0)
        waits = self._deps(queue, reads, writes)
        w = self.waited[queue]
        if prev > 0 and w.get(k, 0) < prev * 16:
            w[k] = prev * 16
            waits.append((k, prev * 16))
        self.dma_cnt[k] = prev + 1
        tok = (k, (prev + 1) * 16, queue)

        def fn(eng, out=out, in_=in_):
            return eng.dma_start(out=out, in_=in_)
        self.streams[queue].append((waits, fn, (k, 16), self.tag))
        self._commit(tok, reads, writes)
        self.n_ops += 1

    def wait_all(self, eng, resources):
        out = {}
        for r in resources:
            toks = list(r.r)
            if r.w is not None:
                toks.append(r.w)
            for (k, v, e) in toks:
                if out.get(k, 0) < v:
                    out[k] = v
        self.streams[eng].append((list(out.items()), None, None, ""))

    def emit(self):
        nc = self.nc
        sems = self.sems
        streams = self.streams

        def run(engh, lst):
            for waits, fn, inc, tag in lst:
                for k, v in waits:
                    engh.wait_ge(sems[k], v)
                if fn is not None:
                    ins = fn(engh)
                    ins.then_inc(sems[inc[0]], inc[1])
                    if TAGS:
                        try:
                            self.tagmap[ins.ins.name] = (tag, [k for k, v in waits])
                        except Exception:
                            pass

        with nc.Block() as block:
            @block.tensor
            def _(e):
                run(e, streams["pe"])

            @block.scalar
            def _(e):
                run(e, streams["act"])

            @block.vector
            def _(e):
                run(e, streams["dve"])

            @block.gpsimd
            def _(e):
                run(e, streams["pool"])

            @block.sync
            def _(e):
                run(e, streams["sp"])


class Tile:
    pass


def build():
    nc = bass.Bass("TRN2", target_bir_lowering=False)

    def din(name, shape):
        return nc.dram_tensor(name, shape, F32, kind="ExternalInput").ap()

    def dout(name, shape):
        return nc.dram_tensor(name, shape, F32, kind="ExternalOutput").ap()

    xT_d = din("xT", [8, 128, NTOK])
    cT_d = din("cT", [128, 8, 6])
    wada_d = din("wada", [2, 128, 8, 6144])
    vec_d = din("vec", [2, 128, 272])
    tv_d = din("tv", [2, 544])
    win_d = din("win", [2, 128, 8, 2056])
    wout_d = din("wout", [2, 128, 8, 1024])
    wup_d = din("wup", [2, 128, 22, 8, 256])
    wdn_d = din("wdn", [2, 128, 2, 8, 11, 128])
    hT0_d = din("hT0", [2, 4, 128, 256])
    cs0_d = din("cs0", [2, 4, 128, 18])
    kcT_d = din("kcT", [2, 4, 128, 128])
    vc_d = din("vc", [2, 4, 128, 128])
    cf0_d = din("cf0", [2, 4, 128, 88])
    yT_d = dout("yT", [8, 128, NTOK])
    ssm_o = dout("ssmT", [2, 6, 128, 256])
    cs_o = dout("cso", [2, 6, 128, 18])
    k_o = dout("kTo", [2, 6, 128, 128])
    v_o = dout("vo", [2, 6, 128, 128])
    cf_o = dout("cfo", [2, 6, 128, 88])

    with contextlib.ExitStack() as st:
        P = Prog(nc, st)
        RO = P.res("outputs")

        def sb(name, shape, dt=F32):
            return st.enter_context(nc.sbuf_tensor(name, shape, dt))

        def MM(out, lhsT, rhs, start, stop, reads, writes, skip=False):
            P.op("pe", lambda e: e.matmul(out, lhsT=lhsT, rhs=rhs, start=start, stop=stop,
                                          skip_group_check=skip), reads, writes)

        def TR(out, in_, ident, reads, writes):
            P.op("pe", lambda e: e.transpose(out, in_, ident), reads, writes)

        def ACT(out, in_, func, reads, writes, scale=1.0, bias=0.0, accum=None):
            if accum is None:
                P.op("act", lambda e: e.activation(out=out, in_=in_, func=func, bias=bias, scale=scale),
                     reads, writes)
            else:
                P.op("act", lambda e: e.activation(out=out, in_=in_, func=func, bias=bias, scale=scale,
                                                   accum_out=accum), reads, writes)

        def ACP(out, in_, reads, writes):
            P.op("act", lambda e: e.copy(out=out, in_=in_), reads, writes)

        def VCP(out, in_, reads, writes):
            P.op("dve", lambda e: e.tensor_copy(out=out, in_=in_), reads, writes)

        def TT(out, in0, in1, op, reads, writes):
            P.op("dve", lambda e: e.tensor_tensor(out=out, in0=in0, in1=in1, op=op), reads, writes)

        def STT(out, in0, scalar, in1, op0, op1, reads, writes):
            P.op("dve", lambda e: e.scalar_tensor_tensor(out=out, in0=in0, scalar=scalar, in1=in1,
                                                         op0=op0, op1=op1), reads, writes)

        def TS(out, in0, s1, s2, op0, op1, reads, writes, eng="dve"):
            if s2 is None:
                P.op(eng, lambda e: e.tensor_scalar(out=out, in0=in0, scalar1=s1, scalar2=None, op0=op0),
                     reads, writes)
            else:
                P.op(eng, lambda e: e.tensor_scalar(out=out, in0=in0, scalar1=s1, scalar2=s2, op0=op0, op1=op1),
                     reads, writes)

        def MSET(eng, ap, val, writes):
            P.op(eng, lambda e: e.memset(ap, val), (), writes)

        BK = []
        BKr = []
        for i in range(8):
            BK.append(st.enter_context(nc.psum_tensor("bk%d" % i, [128, 512], F32)))
            BKr.append(P.res("bk%d" % i))
        B_D0, B_D1, B_GS, B_SM, B_Y, B_ST, B_T1, B_T2 = range(8)
        BT1v = BK[B_T1][:].bitcast(BF16)
        BT2v = BK[B_T2][:].bitcast(BF16)
        BSMv = BK[B_SM][:].bitcast(BF16)

        XT = sb("XT", [128, 8, NP])
        HTB = sb("HTB", [128, 8, NP], BF16)
        WIN = sb("WIN", [128, 8, 2056], BF16)
        WO = [sb("WO%d" % i, [128, 8, 128], BF16) for i in range(3)]
        WU = [sb("WU%d" % i, [128, 8, 256], BF16) for i in range(3)]
        WD = [sb("WD%d" % i, [128, 11, 128], BF16) for i in range(2)]
        UNI = sb("UNI", [128, 11 * NP], BF16)
        ACTT = UNI[:].rearrange("p (f t) -> p f t", f=11)
        XS = UNI[:, 0:3072].rearrange("p (c t) -> p c t", c=6)
        QN = UNI[:, 3072:5120].rearrange("p (c t) -> p c t", c=4)
        XBCh = UNI[:, 5120:8210].rearrange("p (c t) -> p c t", c=6)
        LT = UNI[:, 8212:9236].rearrange("p (h t) -> p h t", h=8)
        MT = UNI[:, 9236:10260].rearrange("p (h t) -> p h t", h=8)
        PT = UNI[:, 10260:11284].rearrange("p (g a j q) -> p g a j q", g=2, a=2, j=4)
        UN2 = sb("UN2", [128, 4096])

        def bfv(a, b):
            return UN2[:, a:b].bitcast(BF16)
        UG = [bfv(0, 1100).rearrange("p (u t) -> p u t", u=2), bfv(1100, 2200).rearrange("p (u t) -> p u t", u=2)]
        SG = [UN2[:, 2200:2712], UN2[:, 2712:3224]]
        DG = [bfv(3224, 3608).rearrange("p (a c) -> p a c", a=6), bfv(3608, 3992).rearrange("p (a c) -> p a c", a=6)]
        SZ = [UN2[:, 0:512], UN2[:, 512:1024]]
        KN32 = UN2[:, 1024:1536]
        SQ2 = [bfv(1536, 1792), bfv(1792, 2048)]
        XDT = [bfv(2048, 2304), bfv(3584, 3840)]
        XW = [bfv(2304, 2560), bfv(3840, 4096)]
        XD = [bfv(2560, 2816), None]
        Y4 = [bfv(2816, 3072), UNI[:, 11284:11796]]
        MIX = sb("MIX", [128, 8, 512], BF16)
        ACCU = [MIX[:, 0:2, :].rearrange("p a t -> p (a t)").bitcast(F32), MIX[:, 2:4, :].rearrange("p a t -> p (a t)").bitcast(F32)]
        SQN = sb("SQN", [128, 8, 512], BF16)
        PT1 = bfv(3072, 3584).rearrange("p (g a j q) -> p g a j q", g=2, a=2, j=4)
        KT = [[sb("KT%d_%d" % (l, g), [128, 640], BF16) for g in range(2)] for l in range(2)]
        VB = [sb("VB%d" % l, [128, 5, 2, 64], BF16) for l in range(2)]
        V32 = sb("V32", [128, 128])
        TMPF = [sb("TMPF%d" % i, [128, 512]) for i in range(3)]
        BTOK = [sb("BTOK", [128, 128], BF16), UNI[:, 11796:11924]]
        XD[1] = sb("XD1", [128, 512], BF16)
        MT = [MT, sb("MT1", [128, 8, 128], BF16)]
        HT32 = [sb("HT32%d" % l, [128, 256]) for l in range(2)]
        HTBF = [sb("HTBF%d" % l, [128, 512], BF16) for l in range(2)]
        CH = [sb("CH%d" % l, [128, 6, 3], BF16) for l in range(2)]
        CONVO = sb("CONVO", [128, 6, 3])
        CFFO = [sb("CFFO%d" % i, [128, 22, 2, 2]) for i in range(2)]
        CFS = sb("CFS", [128, 4, 22, 2, 2])
        CFFO = CFFO + [CFS[:, j_] for j_ in range(4)]
        FH = [sb("FH%d" % l, [128, 22, 2, 2], BF16) for l in range(2)]
        CFH = sb("CFH", [128, 2, 4, 88], BF16)
        DGX = [sb("DGX%d" % i, [128, 4, 128], BF16) for i in range(2)]
        FEN = sb("FEN", [128, 2])
        GSB = sb("GSB", [128, 2, 128], BF16)
        DTR = sb("DTR", [128, 32]); DTA = sb("DTA", [128, 32]); DT = sb("DT", [128, 32])
        ADT = sb("ADT", [128, 32]); ACS = sb("ACS", [128, 32]); NACS = sb("NACS", [128, 32])
        ACSH = sb("ACSH", [128, 32], BF16); ACSL = sb("ACSL", [128, 32], BF16)
        NACSH = sb("NACSH", [128, 32], BF16); NACSL = sb("NACSL", [128, 32], BF16)
        WDEC = sb("WDEC", [128, 32]); DTW = sb("DTW", [128, 32]); DECF = sb("DECF", [128, 32])
        EACS = sb("EACS", [128, 32]); DECSEL = sb("DECSEL", [128, 4, 4]); SSQ = sb("SSQ", [128, 2])
        IDF = sb("IDF", [128, 128]); IDB = sb("IDB", [128, 128], BF16)
        ONESB = sb("ONESB", [128, 128], BF16)
        BLK1 = sb("BLK1", [128, 128], BF16); TRI = sb("TRI", [128, 128])
        MASKN = sb("MASKN", [128, 128], BF16)
        SEL128 = sb("SEL128", [128, 128]); SEL16 = sb("SEL16", [128, 128])
        BIAS = [sb("BIAS%d" % i, [128, 8, 64], BF16) for i in range(4)]
        NSL = sb("NSL", [128, 8])
        VEC = sb("VEC", [128, 2, 272]); TVB = sb("TVB", [128, 2, 544])
        ANEG = sb("ANEG", [128, 2, 8]); ESK = sb("ESK", [128, 2, 8]); QG8 = sb("QG8", [128, 2])
        MOD = sb("MOD", [128, 2, 48, 6]); CS = sb("CS", [128, 8, 6]); CSB = sb("CSB", [128, 8, 6], BF16)

        R = {}
        for n in ("WIN GSB SQN PT1 MIX XS QN KN32 V32 LT PT CONVO CFH "
                  "DTR DTA DT ADT ACS NACS WDEC DTW DECF EACS DECSEL SSQ CONST VEC TVB MOD CS CSB").split():
            R[n] = P.res(n)
        for l in range(2):
            for n in ("KT", "VB", "HT32", "HTBF", "CH", "FH"):
                R["%s%d" % (n, l)] = P.res("%s%d" % (n, l))
        R_WO = [P.res("WO%d" % i) for i in range(3)]
        R_WU = [P.res("WU%d" % i) for i in range(3)]
        R_WD = [P.res("WD%d" % i) for i in range(2)]
        R_SZ = [P.res("SZ%d" % i) for i in range(2)]
        R_SQ2 = [P.res("SQ2%d" % i) for i in range(2)]
        R_TMPF = [P.res("TMPF%d" % i) for i in range(3)]
        R_UG = [P.res("UG%d" % i) for i in range(2)]
        R_SG = [P.res("SG%d" % i) for i in range(2)]
        R_DG = [P.res("DG%d" % i) for i in range(2)]
        R_XB = [P.res("XBCh%d" % c) for c in range(6)]
        R_X = [[P.res("X_%d_%d" % (i, k)) for k in range(8)] for i in range(6)]
        R_H = [P.res("H_%d" % i) for i in range(6)]
        R_AT = [P.res("AT_%d" % i) for i in range(6)]
        R_CFFO = [P.res("CFFO%d" % i) for i in range(6)]
        R_WAB = [P.res("WAB0"), P.res("WAB1")]
        R_ACCU = [P.res("ACCU0"), P.res("ACCU1")]
        R_XDT = [P.res("XDT%d" % i) for i in range(2)]
        R_XW = [P.res("XW%d" % i) for i in range(2)]
        R_XD = [P.res("XD%d" % i) for i in range(2)]
        R_Y4 = [P.res("Y4%d" % i) for i in range(2)]
        R_BTOK = [P.res("BTOK%d" % i) for i in range(2)]
        R_MT = [P.res("MT%d" % i) for i in range(2)]
        R_DGX = [P.res("DGX%d" % i) for i in range(2)]
        R_UGT = [[P.res("UG%d_%d" % (i, j)) for j in range(6)] for i in range(2)]
        ALIASED = ([R["XS"], R["QN"], R["LT"], R["PT"], R["PT1"], R["KN32"]] + R_MT + R_XDT + R_XW + R_XD + R_Y4 + R_BTOK
                   + R_XB + R_AT + R_SZ + R_SQ2 + R_UG + R_SG + R_DG + R_WAB + R_UGT[0] + R_UGT[1] + R_ACCU + [R["MIX"]])

        def fence():
            MSET("dve", FEN[:, 0:1], 0.0, ALIASED)

        RC = [R["CONST"]]
        MSET("pool", IDF[:], 1.0, RC)
        P.op("pool", lambda e: e.affine_select(out=IDF[:], in_=IDF[:], pattern=[[-1, 128]],
                                               compare_op=ALU.is_equal, fill=0.0, base=0, channel_multiplier=1), RC, RC)
        MSET("pool", TRI[:], 1.0, RC)
        MSET("pool", SEL128[:], 1.0, RC)
        MSET("pool", SEL16[:], 1.0, RC)
        MSET("pool", MASKN[:], 0.0, RC)
        P.op("pool", lambda e: e.affine_select(out=TRI[:], in_=TRI[:], pattern=[[1, 128]],
                                               compare_op=ALU.is_ge, fill=0.0, base=0, channel_multiplier=-1), RC, RC)
        P.op("pool", lambda e: e.affine_select(out=MASKN[:], in_=MASKN[:], pattern=[[1, 128]],
                                               compare_op=ALU.is_ge, fill=NEG, base=0, channel_multiplier=-1), RC, RC)
        P.op("pool", lambda e: e.affine_select(out=SEL128[:], in_=SEL128[:], pattern=[[0, 128]],
                                               compare_op=ALU.is_equal, fill=0.0, base=-127, channel_multiplier=1), RC, RC)
        P.op("pool", lambda e: e.affine_select(out=SEL16[:], in_=SEL16[:], pattern=[[0, 128]],
                                               compare_op=ALU.is_equal, fill=0.0, base=-15, channel_multiplier=1), RC, RC)
        VCP(IDB[:], IDF[:], RC, RC)
        MSET("dve", ONESB[:], 1.0, RC)
        MSET("dve", BLK1[:], 0.0, RC)
        MSET("dve", BLK1[0:64, 0:64], 1.0, RC)
        MSET("dve", BLK1[64:128, 64:128], 1.0, RC)
        for h in range(8):
            MSET("dve", NSL[:, h:h + 1], -(2.0 ** (-(h + 1))), RC)
        IOT = TMPF[1][:].bitcast(I32)
        for idx, c0 in enumerate((128, 0, 192, 64)):
            P.op("pool", lambda e, c0=c0: e.iota(IOT.rearrange("p (h i) -> p h i", h=8), pattern=[[0, 8], [1, 64]],
                                                 base=c0, channel_multiplier=-1), RC, RC)
            VCP(TMPF[0][:], IOT, RC, RC)
            STT(TMPF[2][:], TMPF[0][:], -1.0, TMPF[0][:], ALU.mult, ALU.max, RC, RC)
            TT(BIAS[idx][:], TMPF[2][:].rearrange("p (h i) -> p h i", h=8),
               NSL[:].unsqueeze(2).to_broadcast([128, 8, 64]), ALU.mult, RC, RC)
        for l_ in range(2):
            for g_ in range(2):
                MSET("pool", KT[l_][g_][:], 0.0, [R["KT%d" % l_]])
        MSET("dve", BIAS[1][64:128, :, :], NEG, RC)
        MSET("dve", BIAS[2][0:64, :, :], NEG, RC)
        P.dma("sp", VEC[:], vec_d.rearrange("l p n -> p l n"), (), [R["VEC"]])
        for l in range(2):
            P.dma("sp", TVB[:, l, :], tv_d[l].partition_broadcast(128), (), [R["TVB"]])
        P.dma("sp", CS[:], cT_d, (), [R["CS"]])
        RV = [R["VEC"], R["TVB"]]
        ACT(ANEG[:], TVB[:, :, 8:16], AF.Exp, RV, RV)
        TS(ANEG[:], ANEG[:], -1.0, None, ALU.mult, None, RV, RV)
        ACT(ESK[:], TVB[:, :, 24:32], AF.Exp, RV, RV)
        TS(QG8[:], VEC[:, :, 270], 0.125, None, ALU.mult, None, RV, RV)
        ACT(CS[:], CS[:], AF.Silu, [R["CS"]], [R["CS"]])
        VCP(CSB[:], CS[:], [R["CS"]], [R["CSB"]])

        passes = []
        for s in range(2):
            for half in range(2):
                tl = []
                for i in range(2):
                    t = Tile()
                    t.seq = s; t.NT = 512; t.QB = 128; t.QA = 64; t.sample = False
                    t.tok0 = half * 1024 + i * 512
                    t.first = (t.tok0 == 0); t.last = (t.tok0 == 1536)
                    t.c0 = i * 512; t.idx = i; t.slot = i * 514
                    tl.append(t)
                passes.append(tl)
        for j in range(4):
            t = Tile()
            t.seq = 2 + j; t.sj = j; t.NT = 16; t.QB = 16; t.QA = 16; t.sample = True
            t.first = True; t.last = True; t.tok0 = 0
            t.c0 = 1024 + 16 * j; t.idx = 2 + j; t.slot = 2 * 514 + 18 * j
            passes[3].append(t)
        pass_dram0 = [0, 1024, 2048, 3072]

        def load_win_piece(l, k):
            P.dma("pool", WIN[:, k, :], win_d[l, :, k, :], (), [R["WIN"]])

        for k in range(8):
            load_win_piece(0, k)

        P.tag = "mod"
        WAB = [UNI[:, 0:4096].rearrange("p (k n) -> p k n", k=8), UNI[:, 4096:8192].rearrange("p (k n) -> p k n", k=8)]
        pi = 0
        for l in range(2):
            for piece in range(12):
                buf = pi % 2
                P.dma("pool", WAB[buf], wada_d[l, :, :, piece * 512:(piece + 1) * 512], (), [R_WAB[buf]])
                for cq in range(4):
                    cc = piece * 4 + cq
                    bank = B_D0 if cc % 2 == 0 else B_D1
                    for k in range(8):
                        MM(BK[bank][:, 0:6], WAB[buf][:, k, cq * 128:(cq + 1) * 128], CSB[:, k, :],
                           k == 0, k == 7, [R_WAB[buf], R["CSB"]], [BKr[bank]])
                    TS(MOD[:, l, cc, :], BK[bank][:, 0:6], VEC[:, l, 16 + cc:17 + cc], None, ALU.add, None,
                       [BKr[bank], R["VEC"]], [R["MOD"]])
                pi += 1
            for kind, nb in ((1, 0), (4, 8)):
                for k in range(8):
                    cc = kind * 8 + k
                    TS(MOD[:, l, cc, :], MOD[:, l, cc, :], 1.0, VEC[:, l, nb + k:nb + k + 1], ALU.add, ALU.mult,
                       [R["MOD"], R["VEC"]], [R["MOD"]])

        def load_wo(l, d):
            P.dma("pool", WO[d % 3][:], wout_d[l, :, d, :].rearrange("p (m c) -> p m c", m=8), (), [R_WO[d % 3]])

        def build_dgx(l):
            for c in range(6):
                for tap in range(4):
                    TS(DGX[:, c * 4 + tap, :], IDB[:], VEC[:, l, 64 + c * 4 + tap:65 + c * 4 + tap], None,
                       ALU.mult, None, [R["CONST"], R["VEC"]], [R["DGX"]])

        tmp_rr = [0]

        def tmpf():
            i = tmp_rr[0] % 3
            tmp_rr[0] += 1
            return TMPF[i], R_TMPF[i]

        def norm_stages(l, t, kA, kS):
            NT = t.NT
            cs = slice(t.c0, t.c0 + NT)
            s = t.seq
            tg = ("S:" if t.sample else "P:") + ("norm%d" % (1 if kA == 1 else 2))
            NB = B_ST

            def st1():
                old = P.tag; P.tag = tg
                for k in range(8):
                    ACT(SQN[:, k, :NT], XT[:, k, cs], AF.Square, [R_X[t.idx][k]], [R["SQN"]])
                P.tag = old

            def st2():
                old = P.tag; P.tag = tg
                for k in range(8):
                    MM(BK[NB][:, :NT], ONESB[:], SQN[:, k, :NT], k == 0, k == 7, [R["SQN"], R["CONST"]], [BKr[NB]])
                ACT(BK[NB][:, :NT], BK[NB][:, :NT], AF.Ln, [BKr[NB]], [BKr[NB]], scale=1.0 / 1024, bias=EPS)
                ACT(BK[NB][:, :NT], BK[NB][:, :NT], AF.Exp, [BKr[NB]], [BKr[NB]], scale=-0.5)
                P.tag = old

            def st3():
                old = P.tag; P.tag = tg
                for k in range(8):
                    tb, tr = tmpf()
                    STT(tb[:, :NT], BK[NB][:, :NT], MOD[:, l, kA * 8 + k, s:s + 1], XT[:, k, cs], ALU.mult, ALU.mult,
                        [BKr[NB], R["MOD"], R_X[t.idx][k]], [tr])
                    ACT(HTB[:, k, cs], tb[:, :NT], AF.Identity, [tr, R["MOD"]], [R_H[t.idx]],
                        bias=MOD[:, l, kS * 8 + k, s:s + 1])
                P.tag = old
            return [st1, st2, st3]

        def mixer(l, t, inj):
            NT, QB, QA = t.NT, t.QB, t.QA
            nblk = NT // QB
            nch = NT // QA
            nb8 = nblk * 8
            cs = slice(t.c0, t.c0 + NT)
            s = t.seq
            RH = R_H[t.idx]
            kt, vb = KT[l], VB[l]
            RKT, RVB = R["KT%d" % l], R["VB%d" % l]
            RHT, RHB = R["HT32%d" % l], R["HTBF%d" % l]
            mo = 16 * t.sj if t.sample else 0
            if t.sample:
                j = t.sj
                P.dma("pool", XBCh[:, :, 0:3], cs0_d[l, j].rearrange("p (c k) -> p c k", c=6), (), R_XB)
                P.dma("sp", HT32[l][:], hT0_d[l, j], (), [RHT])
                MSET("dve", HTBF[l][:], 0.0, [RHB])
                VCP(HTBF[l][0:64, 0:256], HT32[l][0:64, :], [RHT], [RHB])
                VCP(HTBF[l][64:128, 256:512], HT32[l][64:128, :], [RHT], [RHB])
                P.dma("pool", kt[0][0:64, 0:128], kcT_d[l, j, 0:64, :], (), [RKT])
                P.dma("pool", kt[1][64:128, 0:128], kcT_d[l, j, 64:128, :], (), [RKT])
                P.dma("pool", vb[:, 0, :, :], vc_d[l, j].rearrange("p (g d) -> p g d", g=2), (), [RVB])
                P.dma("sp", k_o[l, s, :, 0:112], kcT_d[l, j, :, 16:128], (), [RO])
                P.dma("sp", v_o[l, s, 0:112, :], vc_d[l, j, 16:128, :], (), [RO])
            elif t.first:
                MSET("dve", XBCh[:, :, 0:3], 0.0, R_XB)
                MSET("dve", HT32[l][:], 0.0, [RHT])
                MSET("dve", HTBF[l][:], 0.0, [RHB])
            else:
                VCP(XBCh[:, :, 0:3], CH[l][:], [R["CH%d" % l]], R_XB)

            if (not t.sample) or t.sj == 0:
                load_wo(l, 0)
                load_wo(l, 1)
                load_wo(l, 2)

            pre = "S:" if t.sample else "P:"

            def ph_xbc():
                P.tag = pre + "xbc"
                mi = 0
                pend = []
                for c in range(6):
                    bank = B_D0 if mi % 2 == 0 else B_D1
                    mi += 1
                    dgx, dgxr = DGX[c % 2], R_DGX[c % 2]
                    for tap in range(4):
                        TS(dgx[:, tap, :], IDB[:], VEC[:, l, 64 + c * 4 + tap:65 + c * 4 + tap], None,
                           ALU.mult, None, [R["CONST"], R["VEC"]], [dgxr])
                    for k in range(8):
                        MM(BK[bank][:, :NT], WIN[:, k, c * 128:(c + 1) * 128], HTB[:, k, cs], k == 0, k == 7,
                           [R["WIN"], RH], [BKr[bank]])
                    ACP(XBCh[:, c, 3:3 + NT], BK[bank][:, :NT], [BKr[bank]], [R_XB[c]])
                    if t.last:
                        VCP(CONVO[:, c, :], BK[bank][:, NT - 3:NT], [BKr[bank]], [R["CONVO"]])

                    def conv_part(c=c, dgx=dgx, dgxr=dgxr):
                        cb = B_GS if c % 2 == 0 else B_SM
                        for tap in range(4):
                            MM(BK[cb][:, :NT], dgx[:, tap, :], XBCh[:, c, tap:tap + NT], tap == 0, tap == 3,
                               [dgxr, R_XB[c]], [BKr[cb]])
                        ACT(XS[:, c, :NT], BK[cb][:, :NT], AF.Silu, [BKr[cb], R["VEC"]], [R["XS"]],
                            bias=VEC[:, l, 88 + c:89 + c])
                    if pend:
                        pend.pop(0)()
                    pend.append(conv_part)
                while pend:
                    pend.pop(0)()
                if not t.last:
                    VCP(CH[l][:], XBCh[:, :, NT:NT + 3], R_XB, [R["CH%d" % l]])
                else:
                    P.dma("sp", cs_o[l, s], CONVO[:].rearrange("p c k -> p (c k)"), [R["CONVO"]], [RO])

            def ph_qk():
                P.tag = pre + "qk"
                pend = []
                for j in range(5):
                    bank = (B_D0, B_D1, B_GS, B_SM)[j % 4]
                    col = 768 + j * 128
                    for k in range(8):
                        MM(BK[bank][:, :NT], WIN[:, k, col:col + 128], HTB[:, k, cs], k == 0, k == 7,
                           [R["WIN"], RH], [BKr[bank]])
                    sq, sqr = SQ2[j % 2], R_SQ2[j % 2]
                    ACT(sq[:, :NT], BK[bank][:, :NT], AF.Square, [BKr[bank]], [sqr])

                    def stat_part(j=j, bank=bank, sq=sq, sqr=sqr):
                        MM(BK[B_Y][:, :NT], BLK1[:], sq[:, :NT], True, True, [sqr, R["CONST"]], [BKr[B_Y]])
                        tb, tr = tmpf()
                        ACT(tb[:, :NT], BK[B_Y][:, :NT], AF.Ln, [BKr[B_Y]], [tr], scale=1.0 / 64, bias=EPS)
                        ACT(tb[:, :NT], tb[:, :NT], AF.Exp, [tr], [tr], scale=-0.5)
                        if j < 4:
                            STT(QN[:, j, :NT], BK[bank][:, :NT], QG8[:, l:l + 1], tb[:, :NT], ALU.mult, ALU.mult,
                                [BKr[bank], tr, R["VEC"]], [R["QN"]])
                        else:
                            STT(KN32[:, :NT], BK[bank][:, :NT], VEC[:, l, 271:272], tb[:, :NT], ALU.mult, ALU.mult,
                                [BKr[bank], tr, R["VEC"]], [R["KN32"]])
                            ACP(kt[0][0:64, 128:128 + NT], KN32[0:64, :NT], [R["KN32"]], [RKT])
                            VCP(kt[1][64:128, 128:128 + NT], KN32[64:128, :NT], [R["KN32"]], [RKT])
                            if t.last:
                                if t.sample:
                                    P.dma("sp", k_o[l, s, :, 112:128], KN32[:, 0:16], [R["KN32"]], [RO])
                                else:
                                    P.dma("sp", k_o[l, s], KN32[:, NT - 128:NT], [R["KN32"]], [RO])
                    if pend:
                        pend.pop(0)()
                    pend.append(stat_part)
                while pend:
                    pend.pop(0)()

            def ph_vdt():
                P.tag = pre + "vdt"
                for b in range(nblk):
                    bank = (B_Y, B_ST, B_GS, B_T1)[b % 4]
                    for k in range(8):
                        MM(BK[bank][:QB, 0:136], HTB[:, k, t.c0 + b * QB:t.c0 + (b + 1) * QB], WIN[:, k, 1920:2056],
                           k == 0, k == 7, [R["WIN"], RH], [BKr[bank]])
                    ACP(vb[:QB, 1 + b, :, :], BK[bank][:QB, 0:128].rearrange("p (g d) -> p g d", g=2), [BKr[bank]], [RVB])
                    if t.last and b == nblk - 1:
                        VCP(V32[:QB, :], BK[bank][:QB, 0:128], [BKr[bank]], [R["V32"]])
                        if t.sample:
                            P.dma("sp", v_o[l, s, 112:128, :], V32[0:16, :], [R["V32"]], [RO])
                        else:
                            P.dma("sp", v_o[l, s], V32[:, :], [R["V32"]], [RO])
                    TT(DTR[:QB, b * 8:(b + 1) * 8], BK[bank][:QB, 128:136], TVB[:QB, l, 0:8], ALU.add,
                       [BKr[bank], R["TVB"]], [R["DTR"]])

            def ph_dtp1():
                P.tag = pre + "dtp"
                STT(DTA[:QB, :nb8], DTR[:QB, :nb8], -1.0, DTR[:QB, :nb8], ALU.mult, ALU.max, [R["DTR"]], [R["DTA"]])
                ACT(DTA[:QB, :nb8], DTA[:QB, :nb8], AF.Exp, [R["DTA"]], [R["DTA"]], scale=-1.0)
                ACT(DTA[:QB, :nb8], DTA[:QB, :nb8], AF.Ln, [R["DTA"]], [R["DTA"]], bias=1.0)
                STT(DT[:QB, :nb8], DTR[:QB, :nb8], 0.0, DTA[:QB, :nb8], ALU.max, ALU.add, [R["DTR"], R["DTA"]], [R["DT"]])
                TT(ADT[:QB, :nb8].rearrange("p (b h) -> p b h", h=8), DT[:QB, :nb8].rearrange("p (b h) -> p b h", h=8),
                   ANEG[:QB, l, :].unsqueeze(1).to_broadcast([QB, nblk, 8]), ALU.mult, [R["DT"], R["TVB"]], [R["ADT"]])

            def ph_dtp2():
                P.tag = pre + "dtp"
                MM(BK[B_SM][:QB, 0:nb8], TRI[:QB, :QB], ADT[:QB, :nb8], True, True, [R["ADT"], R["CONST"]], [BKr[B_SM]])
                VCP(ACS[:QB, :nb8], BK[B_SM][:QB, 0:nb8], [BKr[B_SM]], [R["ACS"]])
                TS(NACS[:QB, :nb8], ACS[:QB, :nb8], -1.0, None, ALU.mult, None, [R["ACS"]], [R["NACS"]])
                VCP(ACSH[:QB, :nb8], ACS[:QB, :nb8], [R["ACS"]], [R["NACS"]])
                TT(ACSL[:QB, :nb8], ACS[:QB, :nb8], ACSH[:QB, :nb8], ALU.subtract, [R["ACS"], R["NACS"]], [R["NACS"]])
                TS(NACSH[:QB, :nb8], ACSH[:QB, :nb8], -1.0, None, ALU.mult, None, [R["NACS"]], [R["NACS"]])
                TS(NACSL[:QB, :nb8], ACSL[:QB, :nb8], -1.0, None, ALU.mult, None, [R["NACS"]], [R["NACS"]])
                SEL = SEL128 if QB == 128 else SEL16
                MM(BK[B_SM][:, 32:32 + nb8], SEL[:QB, :], ACS[:QB, :nb8], True, True, [R["ACS"], R["CONST"]], [BKr[B_SM]])
                TT(WDEC[:QB, :nb8], BK[B_SM][:QB, 32:32 + nb8], ACS[:QB, :nb8], ALU.subtract, [BKr[B_SM], R["ACS"]], [R["WDEC"]])
                ACT(DECF[:, :nb8], BK[B_SM][:, 32:32 + nb8], AF.Exp, [BKr[B_SM]], [R["DECF"]])

            def ph_dtp3():
                P.tag = pre + "dtp"
                ACT(WDEC[:QB, :nb8], WDEC[:QB, :nb8], AF.Exp, [R["WDEC"]], [R["WDEC"]])
                TT(DTW[:QB, :nb8], DT[:QB, :nb8], WDEC[:QB, :nb8], ALU.mult, [R["DT"], R["WDEC"]], [R["DTW"]])
                ACT(EACS[:QB, :nb8], ACS[:QB, :nb8], AF.Exp, [R["ACS"]], [R["EACS"]])
                dv = DECF[:, :nb8].rearrange("p (b h) -> p b h", h=8)
                VCP(DECSEL[0:64, :nblk, :], dv[0:64, :, 0:4], [R["DECF"]], [R["DECSEL"]])
                VCP(DECSEL[64:128, :nblk, :], dv[64:128, :, 4:8], [R["DECF"]], [R["DECSEL"]])


            def hook(k_):
                for fn_ in inj[k_]:
                    fn_()
            ph_vdt()
            hook("a")
            ph_dtp1()
            ph_xbc()
            hook("b")
            ph_dtp2()
            ph_qk()
            hook("c")
            ph_dtp3()
            hook("d")
            P.tag = pre + "ssd"
            W4 = 4 * QA

            def pieces_of(c):
                if t.sample:
                    pcs = [(0, 0, 128, 0, 0, 0), (1, 128, 16, 0, 1, 1)]
                elif c % 2 == 0:
                    pcs = [(0, c * 64, 128, 0, c // 2, 0), (1, 128 + c * 64, 128, 0, c // 2 + 1, 1)]
                else:
                    pcs = [(0, 128 + (c - 3) * 64, 128, 0, (c - 1) // 2, 2),
                           (1, 128 + (c - 1) * 64, 128, 0, (c - 1) // 2 + 1, 3)]
                if t.first and (not t.sample) and c < 2:
                    pcs = pcs[1:]
                return pcs

            def attn_A(c):
                q0 = c * QA
                pieces = pieces_of(c)
                sb_ = (B_D0, B_D1) if c % 2 == 0 else (B_ST, B_T1)
                pt, ptr = (PT, R["PT"]) if c % 2 == 0 else (PT1, R["PT1"])
                for g in range(2):
                    bank = sb_[g]
                    for (pi_, ktc, nk, pb, vbi, bi) in pieces:
                        o = BK[bank][pb:pb + nk, pi_ * W4:(pi_ + 1) * W4]
                        MM(o, kt[g][:, ktc:ktc + nk], QN[:, :, q0:q0 + QA], True, False,
                           [RKT, R["QN"]], [BKr[bank]])
                        MM(o, IDB[:, pb:pb + nk], BIAS[bi][:, g * 4:(g + 1) * 4, 0:QA], False, True,
                           [R["CONST"]], [BKr[bank]])
                for g in range(2):
                    bank = sb_[g]
                    ACT(pt[:, g, :, :, :QA], BK[bank][:, 0:2 * W4].rearrange("p (a j q) -> p a j q", a=2, j=4),
                        AF.Exp, [BKr[bank]], [ptr])

            def attn_C(c):
                q0 = c * QA
                pieces = pieces_of(c)
                pt, ptr = (PT, R["PT"]) if c % 2 == 0 else (PT1, R["PT1"])
                bden = B_GS if c % 2 == 0 else B_SM
                bo = B_Y if c % 2 == 0 else B_T2
                for g in range(2):
                    o = BK[bden][:, g * W4:(g + 1) * W4]
                    for ii, (pi_, ktc, nk, pb, vbi, bi) in enumerate(pieces):
                        MM(o, ONESB[pb:pb + nk, :], pt[pb:pb + nk, g, pi_, :, :QA], ii == 0, False,
                           [ptr, R["CONST"]], [BKr[bden]])
                    MM(o, SEL128[:, :], ESK[:, l, g * 4:(g + 1) * 4].unsqueeze(2).to_broadcast([128, 4, QA]),
                       False, True, [R["TVB"], R["CONST"]], [BKr[bden]])
                rd, rdr = tmpf()
                ACT(rd[:, :8 * QA], BK[bden][:, 0:8 * QA], AF.Ln, [BKr[bden]], [rdr])
                ACT(rd[:, :8 * QA], rd[:, :8 * QA], AF.Exp, [rdr], [rdr], scale=-1.0)
                for h in range(8):
                    g, r = h // 4, h % 4
                    o = BK[bo][(h % 2) * 64:(h % 2 + 1) * 64, (h // 2) * QA:(h // 2 + 1) * QA]
                    for ii, (pi_, ktc, nk, pb, vbi, bi) in enumerate(pieces):
                        MM(o, vb[pb:pb + nk, vbi, g, :], pt[pb:pb + nk, g, pi_, r, :QA], ii == 0, ii == len(pieces) - 1,
                           [RVB, ptr], [BKr[bo]])
                rd3 = rd[:, :8 * QA].rearrange("p (h q) -> p h q", h=8)
                TT(MIX[0:64, 4:8, mo + q0:mo + q0 + QA], BK[bo][0:64, 0:4 * QA].rearrange("p (m q) -> p m q", m=4),
                   rd3[0:64, 0:8:2, :], ALU.mult, [BKr[bo], rdr], [R["MIX"]])
                TT(MIX[64:128, 4:8, mo + q0:mo + q0 + QA], BK[bo][64:128, 0:4 * QA].rearrange("p (m q) -> p m q", m=4),
                   rd3[64:128, 1:8:2, :], ALU.mult, [BKr[bo], rdr], [R["MIX"]])

            def _attn_first():
                old = P.tag; P.tag = pre + "attn"
                attn_A(0)
                P.tag = old
            attn_first = [_attn_first]
            def ssd_E(b):
                bc = slice(b * QB, (b + 1) * QB)
                szb, szr = SZ[b % 2], R_SZ[b % 2]
                i2 = b % 2
                for k in range(8):
                    MM(BK[B_T2][:QB, :], HTB[:, k, t.c0 + b * QB:t.c0 + (b + 1) * QB], WIN[:, k, 1408:1920],
                       k == 0, k == 7, [R["WIN"], RH], [BKr[B_T2]])
                ACT(szb[:QB, :], BK[B_T2][:QB, :], AF.Exp, [BKr[B_T2]], [szr], scale=-1.0)
                ACT(szb[:QB, :], szb[:QB, :], AF.Ln, [szr], [szr], bias=1.0)
                ACT(szb[:QB, :], szb[:QB, :], AF.Exp, [szr], [szr], scale=-1.0)
                TT(szb[:QB, :], BK[B_T2][:QB, :], szb[:QB, :], ALU.mult, [BKr[B_T2], szr], [szr])
                MM(BK[B_T1][:QB, 384:384 + QB], XS[0:64, 4, bc], XS[0:64, 5, bc], True, True, [R["XS"]], [BKr[B_T1]])
                MM(BK[B_SM][:QB, 64:64 + QB], XS[64:128, 4, bc], XS[64:128, 5, bc], True, True, [R["XS"]], [BKr[B_SM]])
                for c in range(4):
                    TR(BT1v[:QB, c * 128:(c + 1) * 128], XS[:, c, bc], IDB[:], [R["XS"], R["CONST"]], [BKr[B_T1]])
                TR(BT1v[:QB, 512:640], XS[:, 4, bc], IDB[:], [R["XS"], R["CONST"]], [BKr[B_T1]])
                ACP(GSB[:QB, 0, :QB], BK[B_T1][:QB, 384:384 + QB], [BKr[B_T1]], [R["GSB"]])
                ACP(GSB[:QB, 1, :QB], BK[B_SM][:QB, 64:64 + QB], [BKr[B_SM]], [R["GSB"]])
                xt3 = BT1v[:QB, 0:512].rearrange("p (h d) -> p h d", h=8)

                def bc8(tile_):
                    return tile_[:QB, b * 8:(b + 1) * 8].unsqueeze(2).to_broadcast([QB, 8, 64])
                TT(XDT[i2][:QB, :].rearrange("p (h d) -> p h d", h=8), xt3, bc8(DT), ALU.mult, [BKr[B_T1], R["DT"]], [R_XDT[i2]])
                TT(XW[i2][:QB, :].rearrange("p (h d) -> p h d", h=8), xt3, bc8(DTW), ALU.mult, [BKr[B_T1], R["DTW"]], [R_XW[i2]])
                TT(XD[i2][:QB, :].rearrange("p (h d) -> p h d", h=8), xt3,
                   TVB[:QB, l, 16:24].unsqueeze(2).to_broadcast([QB, 8, 64]), ALU.mult, [BKr[B_T1], R["TVB"]], [R_XD[i2]])
                ACP(BTOK[i2][:QB, :], BT1v[:QB, 512:640], [BKr[B_T1]], [R_BTOK[i2]])
                for h in range(8):
                    bank = B_D0 if h < 4 else B_D1
                    r = h % 4
                    o = BK[bank][:QB, r * QB:(r + 1) * QB]
                    col = b * 8 + h
                    MM(o, ACSH[:QB, col:col + 1].to_broadcast([QB, QB]), IDB[:QB, :QB], True, False,
                       [R["NACS"], R["CONST"]], [BKr[bank]])
                    MM(o, ACSL[:QB, col:col + 1].to_broadcast([QB, QB]), IDB[:QB, :QB], False, False,
                       [R["NACS"], R["CONST"]], [BKr[bank]])
                    MM(o, IDB[:QB, :QB], NACSH[:QB, col:col + 1].to_broadcast([QB, QB]), False, False,
                       [R["NACS"], R["CONST"]], [BKr[bank]])
                    MM(o, IDB[:QB, :QB], NACSL[:QB, col:col + 1].to_broadcast([QB, QB]), False, False,
                       [R["NACS"], R["CONST"]], [BKr[bank]])
                    MM(o, IDB[:QB, :QB], MASKN[:QB, :QB], False, True, [R["CONST"]], [BKr[bank]])
                for g in range(2):
                    bank = B_D0 if g == 0 else B_D1
                    ACT(LT[:QB, 4 * g:4 * g + 4, :QB], BK[bank][:QB, 0:4 * QB].rearrange("p (h t) -> p h t", h=4),
                        AF.Exp, [BKr[bank]], [R["LT"]])
                for g in range(2):
                    TT(MT[i2][:QB, 4 * g:4 * g + 4, :QB], LT[:QB, 4 * g:4 * g + 4, :QB],
                       GSB[:QB, g, :QB].unsqueeze(1).to_broadcast([QB, 4, QB]), ALU.mult, [R["LT"], R["GSB"]], [R_MT[i2]])

            def ssd_M(b):
                bc = slice(b * QB, (b + 1) * QB)
                szb, szr = SZ[b % 2], R_SZ[b % 2]
                i2 = b % 2
                MM(BK[B_Y][:QB, :], IDB[:QB, :QB], XD[i2][:QB, :], True, False, [R_XD[i2], R["CONST"]], [BKr[B_Y]], skip=True)
                for h in range(8):
                    MM(BK[B_Y][:QB, h * 64:(h + 1) * 64], MT[i2][:QB, h, :QB], XDT[i2][:QB, h * 64:(h + 1) * 64], False, h == 7,
                       [R_MT[i2], R_XDT[i2]], [BKr[B_Y]], skip=True)
                MM(BK[B_ST][:QB, :], XS[:, 5, bc], HTBF[l][:, :], True, True, [R["XS"], RHB], [BKr[B_ST]])
                for g in range(2):
                    MM(BK[B_GS][g * 64:(g + 1) * 64, 256:512], BTOK[i2][:QB, g * 64:(g + 1) * 64],
                       XW[i2][:QB, g * 256:(g + 1) * 256], True, True, [R_BTOK[i2], R_XW[i2]], [BKr[B_GS]])
                hv = HT32[l][:].rearrange("p (h d) -> p h d", h=4)
                P.op("pool", lambda e, a=hv, b_=DECSEL[:, b, :].unsqueeze(2).to_broadcast([128, 4, 64]):
                     e.tensor_tensor(out=a, in0=a, in1=b_, op=ALU.mult), [RHT, R["DECSEL"]], [RHT])
                TT(HT32[l][:], HT32[l][:], BK[B_GS][:, 256:512], ALU.add, [RHT, BKr[B_GS]], [RHT])
                ACP(HTBF[l][0:64, 0:256], HT32[l][0:64, :], [RHT], [RHB])
                ACP(HTBF[l][64:128, 256:512], HT32[l][64:128, :], [RHT], [RHB])
                t1, t1r = tmpf()
                ev = EACS[:QB, b * 8:(b + 1) * 8]
                TT(t1[:QB, :].rearrange("p (h d) -> p h d", h=8), BK[B_ST][:QB, :].rearrange("p (h d) -> p h d", h=8),
                   ev.unsqueeze(2).to_broadcast([QB, 8, 64]), ALU.mult, [BKr[B_ST], R["EACS"]], [t1r])
                TT(t1[:QB, :], BK[B_Y][:QB, :], t1[:QB, :], ALU.add, [BKr[B_Y], t1r], [t1r])
                P.op("pool", lambda e, a=t1[:QB, :], b_=szb[:QB, :]: e.tensor_tensor(out=a, in0=a, in1=b_, op=ALU.mult),
                     [t1r, szr], [t1r])
                ACT(Y4[i2][:QB, :], t1[:QB, :], AF.Square, [t1r], [R_Y4[i2], R["SSQ"]], accum=SSQ[:QB, 0:1])
                ACT(SSQ[:QB, 1:2], SSQ[:QB, 0:1], AF.Ln, [R["SSQ"]], [R["SSQ"]], scale=1.0 / 512, bias=EPS)
                ACT(SSQ[:QB, 1:2], SSQ[:QB, 1:2], AF.Exp, [R["SSQ"]], [R["SSQ"]], scale=-0.5)
                STT(Y4[i2][:QB, :], t1[:QB, :], SSQ[:QB, 1:2], TVB[:QB, l, 32:544], ALU.mult, ALU.mult,
                    [t1r, R["SSQ"], R["TVB"]], [R_Y4[i2]])

            def ssd_L(b, t2=False):
                i2 = b % 2
                old_tag = P.tag
                P.tag = pre + "ssd"
                vw, off, br = (BT2v, 0, BKr[B_T2]) if t2 else (BSMv, 512, BKr[B_SM])
                for c in range(4):
                    TR(vw[:, off + c * QB:off + (c + 1) * QB], Y4[i2][:QB, c * 128:(c + 1) * 128], IDB[:QB, :QB],
                       [R_Y4[i2], R["CONST"]], [br])
                ACP(MIX[:, 0:4, mo + b * QB:mo + (b + 1) * QB], vw[:, off:off + 4 * QB].rearrange("p (c t) -> p c t", c=4),
                    [br], [R["MIX"]])
                P.tag = old_tag
            deferL = [nblk - 2, nblk - 1] if nblk >= 3 else []

            for step in range(nblk + 2):
                if step < nblk:
                    ssd_E(step)
                if step == nblk:
                    attn_first[0]()
                if 0 <= step - 1 < nblk:
                    ssd_M(step - 1)
                if 0 <= step - 2 < nblk and (step - 2) not in deferL:
                    ssd_L(step - 2)
            if t.last:
                P.dma("sp", ssm_o[l, s], HT32[l][:], [RHT], [RO])

            P.tag = pre + "attn"
            for c in range(nch):
                if c + 1 < nch:
                    attn_A(c + 1)
                if c == 0 and deferL:
                    ssd_L(deferL[0], t2=True)
                attn_C(c)
                if c == 0 and deferL:
                    ssd_L(deferL[1], t2=True)
            if not t.last:
                VCP(kt[0][0:64, 0:128], kt[0][0:64, NT:NT + 128], [RKT], [RKT])
                VCP(kt[1][64:128, 0:128], kt[1][64:128, NT:NT + 128], [RKT], [RKT])
                VCP(vb[:, 0, :, :], vb[:, nblk, :, :], [RVB], [RVB])
            if not t.sample:
                wout_phase(l, [t], inj.get("w"))

        def wout_phase(l, tl, nstages=None):
            P.tag = ("S:" if tl[0].sample else "P:") + "wout"
            wtag = P.tag
            c_lo = tl[0].c0
            ncol = sum(t_.NT for t_ in tl)
            for d in range(8):
                wo, wor = WO[d % 3], R_WO[d % 3]
                bank = B_D0 if d % 2 == 0 else B_D1
                for m in range(8):
                    MM(BK[bank][:, :ncol], wo[:, m, :], MIX[:, m, :ncol], m == 0, m == 7, [wor, R["MIX"]], [BKr[bank]])
                if d + 3 < 8:
                    load_wo(l, d + 3)
                if nstages is not None and d in (0, 2, 4):
                    nstages[d // 2]()
                    P.tag = wtag
                for t_ in tl:
                    o0 = t_.c0 - c_lo
                    cs_ = slice(t_.c0, t_.c0 + t_.NT)
                    STT(XT[:, d, cs_], BK[bank][:, o0:o0 + t_.NT], MOD[:, l, 2 * 8 + d, t_.seq:t_.seq + 1], XT[:, d, cs_],
                        ALU.mult, ALU.add, [BKr[bank], R["MOD"], R_X[t_.idx][d]], [R_X[t_.idx][d]])

        def load_wu(l, f):
            P.dma("pool", WU[f % 3][:], wup_d[l, :, f], (), [R_WU[f % 3]])

        def load_wd(l, n):
            P.dma("pool", WD[n % 2][:], wdn_d[l, :, n // 8, n % 8], (), [R_WD[n % 2]])

        JT = Tile()
        JT.joint = True; JT.sample = True; JT.NT = 64; JT.c0 = 1024; JT.slot = 2 * 514; JT.idx = 2
        JT.first = True; JT.last = True; JT.seq = -1

        def ffn(l, tiles_all, prefetch):
            fence()
            pending = []
            tiles = [t_ for t_ in tiles_all if not t_.sample]
            stl = [t_ for t_ in tiles_all if t_.sample]
            if stl:
                tiles = tiles + [JT]
            wd_seq = [(0, d_) for d_ in range(8)] + [(1, d_) for d_ in range(8)]
            wd_step = [-1]
            it = 0
            P.tag = "ffn_up"
            for hf in range(2):
                P.tag = "ffn_up"
                for fi in range(11):
                    f = hf * 11 + fi
                    if f + 2 < 22:
                        load_wu(l, f + 2)
                    if prefetch:
                        prefetch.pop(0)()
                    wu, wur = WU[f % 3], R_WU[f % 3]
                    dg, dgr = DG[f % 2], R_DG[f % 2]
                    for ug in range(2):
                        for tap in range(3):
                            col = 94 + (ug * 22 + f) * 3 + tap
                            TS(dg[:, ug * 3 + tap, :], IDB[:], VEC[:, l, col:col + 1], None, ALU.mult, None,
                               [R["CONST"], R["VEC"]], [dgr])
                    ugb = UG[f % 2]
                    for ti, t in enumerate(tiles):
                        NT = t.NT
                        cs = slice(t.c0, t.c0 + NT)
                        so = t.slot
                        bu, bg, bcu, bcg = (B_D0, B_D1, B_GS, B_SM) if it % 2 == 0 else (B_Y, B_ST, B_T1, B_T2)
                        sg, sgr = SG[it % 2], R_SG[it % 2]
                        ugr = R_UGT[f % 2][t.idx]
                        it += 1
                        if getattr(t, "joint", False):
                            RHj = [R_H[i_] for i_ in range(2, 6)]
                            uv = [ugb[:, u_, so:so + 72].rearrange("p (j w) -> p j w", w=18) for u_ in range(2)]
                            VCP(ugb[:, :, so:so + 72].rearrange("p u (j w) -> p u j w", w=18)[:, :, :, 0:2],
                                CFH[:, l, :, :].rearrange("p j (f u k) -> p f u j k", f=22, u=2)[:, f], [R["CFH"]], [ugr])
                            for k in range(8):
                                MM(BK[bu][:, :NT], wu[:, k, 0:128], HTB[:, k, cs], k == 0, k == 7, [wur] + RHj, [BKr[bu]])
                            for k in range(8):
                                MM(BK[bg][:, :NT], wu[:, k, 128:256], HTB[:, k, cs], k == 0, k == 7, [wur] + RHj, [BKr[bg]])
                            pu = BK[bu][:, 0:64].rearrange("p (j w) -> p j w", w=16)
                            pg = BK[bg][:, 0:64].rearrange("p (j w) -> p j w", w=16)
                            ACP(uv[0][:, :, 2:18], pu, [BKr[bu]], [ugr])
                            ACP(uv[1][:, :, 2:18], pg, [BKr[bg]], [ugr])
                            VCP(CFS[:, :, f, 0, :], pu[:, :, 14:16], [BKr[bu]], [R_CFFO[2]])
                            VCP(CFS[:, :, f, 1, :], pg[:, :, 14:16], [BKr[bg]], [R_CFFO[2]])

                            def part2j(NT=NT, cs=cs, bcu=bcu, bcg=bcg, sg=sg, sgr=sgr, dg=dg, dgr=dgr, uv=uv, ugr=ugr, f=f, fi=fi):
                                for tap in range(3):
                                    MM(BK[bcu][:, :NT], dg[:, tap, :], uv[0][:, :, tap:tap + 16], tap == 0, tap == 2,
                                       [dgr, ugr], [BKr[bcu]])
                                for tap in range(3):
                                    MM(BK[bcg][:, :NT], dg[:, 3 + tap, :], uv[1][:, :, tap:tap + 16], tap == 0, tap == 2,
                                       [dgr, ugr], [BKr[bcg]])
                                ACT(sg[:, :NT], BK[bcg][:, :NT], AF.Silu, [BKr[bcg], R["VEC"]], [sgr],
                                    bias=VEC[:, l, 226 + 22 + f:227 + 22 + f])
                                STT(ACTT[:, fi, cs], BK[bcu][:, :NT], VEC[:, l, 226 + f:227 + f], sg[:, :NT], ALU.add, ALU.mult,
                                    [BKr[bcu], R["VEC"], sgr], [R_AT[2]])
                            if pending:
                                pending.pop(0)()
                            pending.append(part2j)
                            continue
                        if t.sample:
                            VCP(ugb[:, :, so:so + 2],
                                CFH[:, l, t.sj, :].rearrange("p (f u k) -> p f u k", f=22, u=2)[:, f, :, :], [R["CFH"]], [ugr])
                        elif t.first:
                            MSET("dve", ugb[:, :, so:so + 2], 0.0, [ugr])
                        elif ti == 0:
                            VCP(ugb[:, :, so:so + 2], FH[l][:, f, :, :], [R["FH%d" % l]], [ugr])
                        else:
                            pso = tiles[ti - 1].slot + tiles[ti - 1].NT
                            VCP(ugb[:, :, so:so + 2], ugb[:, :, pso:pso + 2], [R_UGT[f % 2][tiles[ti - 1].idx]], [ugr])
                        for k in range(8):
                            MM(BK[bu][:, :NT], wu[:, k, 0:128], HTB[:, k, cs], k == 0, k == 7, [wur, R_H[t.idx]], [BKr[bu]])
                        for k in range(8):
                            MM(BK[bg][:, :NT], wu[:, k, 128:256], HTB[:, k, cs], k == 0, k == 7, [wur, R_H[t.idx]], [BKr[bg]])
                        ACP(ugb[:, 0, so + 2:so + 2 + NT], BK[bu][:, :NT], [BKr[bu]], [ugr])
                        ACP(ugb[:, 1, so + 2:so + 2 + NT], BK[bg][:, :NT], [BKr[bg]], [ugr])
                        dve_conv = UCONV_DVE(f)
                        acc, accr = ACCU[it % 2], R_ACCU[it % 2]
                        if dve_conv:
                            cu = 94 + f * 3
                            ACT(acc[:, :NT], BK[bu][:, :NT], AF.Identity, [BKr[bu], R["VEC"]], [accr],
                                scale=VEC[:, l, cu + 2:cu + 3], bias=VEC[:, l, 226 + f:227 + f])
                        if t.last:
                            VCP(CFFO[t.idx][:, f, 0, :], BK[bu][:, NT - 2:NT], [BKr[bu]], [R_CFFO[t.idx]])
                            VCP(CFFO[t.idx][:, f, 1, :], BK[bg][:, NT - 2:NT], [BKr[bg]], [R_CFFO[t.idx]])
                        elif ti == len(tiles) - 1 or tiles[ti + 1].sample:
                            VCP(FH[l][:, f, :, :], ugb[:, :, so + NT:so + NT + 2], [ugr], [R["FH%d" % l]])
                        def part2(NT=NT, cs=cs, so=so, bcu=bcu, bcg=bcg, sg=sg, sgr=sgr, dg=dg, dgr=dgr, ugb=ugb, ugr=ugr,
                                  f=f, fi=fi, t=t, dve_conv=dve_conv, acc=acc, accr=accr):
                            if dve_conv:
                                cu = 94 + f * 3
                                for tap in (1, 0):
                                    STT(acc[:, :NT], ugb[:, 0, so + tap:so + tap + NT], VEC[:, l, cu + tap:cu + tap + 1], acc[:, :NT],
                                        ALU.mult, ALU.add, [ugr, R["VEC"], accr], [accr])
                            else:
                                for tap in range(3):
                                    MM(BK[bcu][:, :NT], dg[:, tap, :], ugb[:, 0, so + tap:so + tap + NT], tap == 0, tap == 2,
                                       [dgr, ugr], [BKr[bcu]])
                            for tap in range(3):
                                MM(BK[bcg][:, :NT], dg[:, 3 + tap, :], ugb[:, 1, so + tap:so + tap + NT], tap == 0, tap == 2,
                                   [dgr, ugr], [BKr[bcg]])
                            ACT(sg[:, :NT], BK[bcg][:, :NT], AF.Silu, [BKr[bcg], R["VEC"]], [sgr],
                                bias=VEC[:, l, 226 + 22 + f:227 + 22 + f])
                            if dve_conv:
                                TT(ACTT[:, fi, cs], acc[:, :NT], sg[:, :NT], ALU.mult, [accr, sgr], [R_AT[t.idx]])
                            else:
                                STT(ACTT[:, fi, cs], BK[bcu][:, :NT], VEC[:, l, 226 + f:227 + f], sg[:, :NT], ALU.add, ALU.mult,
                                    [BKr[bcu], R["VEC"], sgr], [R_AT[t.idx]])
                        if pending:
                            pending.pop(0)()
                        pending.append(part2)
                while pending:
                    pending.pop(0)()
                P.tag = "ffn_down"
                order = [(d_, t_) for d_ in range(8) for t_ in tiles]
                prev_piece = None
                for (d, t) in order:
                    piece = (hf, d)
                    if piece != prev_piece:
                        wd_step[0] += 1
                        nxt_i = wd_step[0] + 1
                        if nxt_i < len(wd_seq):
                            hf_n, d_n = wd_seq[nxt_i]
                            P.dma("pool", WD[nxt_i % 2][:], wdn_d[l, :, hf_n, d_n], (), [R_WD[nxt_i % 2]])
                        prev_piece = piece
                    n = wd_step[0]
                    wd, wdr = WD[n % 2], R_WD[n % 2]
                    NT = t.NT
                    cs = slice(t.c0, t.c0 + NT)
                    bank = it % 8
                    it += 1
                    for fi in range(11):
                        MM(BK[bank][:, :NT], wd[:, fi, :], ACTT[:, fi, cs], fi == 0, fi == 10,
                           [wdr, R_AT[t.idx]], [BKr[bank]])
                    if getattr(t, "joint", False):
                        for t_ in stl:
                            cs_ = slice(t_.c0, t_.c0 + 16)
                            o0 = t_.c0 - 1024
                            STT(XT[:, d, cs_], BK[bank][:, o0:o0 + 16], MOD[:, l, 5 * 8 + d, t_.seq:t_.seq + 1], XT[:, d, cs_],
                                ALU.mult, ALU.add, [BKr[bank], R["MOD"], R_X[t_.idx][d]], [R_X[t_.idx][d]])
                        continue
                    STT(XT[:, d, cs], BK[bank][:, :NT], MOD[:, l, 5 * 8 + d, t.seq:t.seq + 1], XT[:, d, cs],
                        ALU.mult, ALU.add, [BKr[bank], R["MOD"], R_X[t.idx][d]], [R_X[t.idx][d]])
            while prefetch:
                prefetch.pop(0)()
            for t in tiles_all:
                if t.last:
                    rr = R_CFFO[2] if t.sample else R_CFFO[t.idx]
                    P.dma("sp", cf_o[l, t.seq], CFFO[t.idx].rearrange("p f u k -> p (f u k)"), [rr], [RO])

        for pi_, tiles in enumerate(passes):
            c0d = pass_dram0[pi_]
            ncol = sum(t.NT for t in tiles)
            for t in tiles:
                P.dma("sp", XT[:, :, t.c0:t.c0 + t.NT],
                      xT_d[:, :, c0d + t.c0:c0d + t.c0 + t.NT].rearrange("c p t -> p c t"), (),
                      [R_X[t.idx][k] for k in range(8)])
            if pi_ == 3:
                P.dma("pool", CFH[:], cf0_d.rearrange("l j p n -> p l j n"), (), [R["CFH"]])
            for l in range(2):
                fence()
                load_wu(l, 0)
                load_wu(l, 1)
                load_wd(l, 0)
                n1 = norm_stages(l, tiles[0], 1, 0)
                for fn_ in n1:
                    fn_()
                n2prev = None
                for i, t in enumerate(tiles):
                    inj = {"a": [], "b": [], "c": [], "d": []}
                    if n2prev is not None:
                        inj["a"].append(n2prev[1])
                        inj["b"].append(n2prev[2])
                    if i + 1 < len(tiles):
                        n1n = norm_stages(l, tiles[i + 1], 1, 0)
                        if t.sample:
                            inj["b"].append(n1n[0])
                            inj["c"].append(n1n[1])
                            inj["d"].append(n1n[2])
                        else:
                            inj["w"] = n1n
                    mixer(l, t, inj)
                    if t.sample:
                        n2prev = None
                    else:
                        n2prev = norm_stages(l, t, 4, 3)
                        n2prev[0]()
                if n2prev is not None:
                    n2prev[1]()
                    n2prev[2]()
                stl = [t for t in tiles if t.sample]
                if stl:
                    wout_phase(l, stl)
                    for t in stl:
                        for fn_ in norm_stages(l, t, 4, 3):
                            fn_()
                nxt = pi_ * 2 + l + 1
                pf = []
                if nxt < 8:
                    pf = [(lambda k=k, ln=nxt % 2: load_win_piece(ln, k)) for k in range(8)]
                ffn(l, tiles, pf)
            for t in tiles:
                P.dma("sp", yT_d[:, :, c0d + t.c0:c0d + t.c0 + t.NT].rearrange("c p t -> p c t"),
                      XT[:, :, t.c0:t.c0 + t.NT], [R_X[t.idx][k] for k in range(8)], [RO])
        P.wait_all("sp", [RO])
        P.emit()
    return nc


_NC_CACHE = {}


def _fm(v):
    sh = v.shape
    c = sh[-1] // 128
    a = v.reshape(sh[:-1] + (c, 128))
    return np.moveaxis(a, -1, 0)


def kernel(x_prompt, x_sample, c_prompt, c_sample, state_ssm, cache_conv_ssd, cache_attn_k,
           cache_attn_v, cache_conv_ffn, w_ada, b_ada, norm_mix, w_in, conv_ssd_w, conv_ssd_b,
           dt_bias, a_log, d_skip, ssd_norm, q_norm, k_norm, sinks, w_out, norm_ffn, w_up,
           conv_ffn_w, conv_ffn_b, w_down):
    f32 = np.float32
    A = lambda v: np.asarray(v, dtype=f32)
    x_prompt, x_sample, c_prompt, c_sample = A(x_prompt), A(x_sample), A(c_prompt), A(c_sample)
    state_ssm, cache_conv_ssd, cache_attn_k, cache_attn_v, cache_conv_ffn = (
        A(state_ssm), A(cache_conv_ssd), A(cache_attn_k), A(cache_attn_v), A(cache_conv_ffn))
    w_ada, b_ada, norm_mix, w_in, conv_ssd_w, conv_ssd_b = A(w_ada), A(b_ada), A(norm_mix), A(w_in), A(conv_ssd_w), A(conv_ssd_b)
    dt_bias, a_log, d_skip, ssd_norm, q_norm, k_norm, sinks = A(dt_bias), A(a_log), A(d_skip), A(ssd_norm), A(q_norm), A(k_norm), A(sinks)
    w_out, norm_ffn, w_up, conv_ffn_w, conv_ffn_b, w_down = A(w_out), A(norm_ffn), A(w_up), A(conv_ffn_w), A(conv_ffn_b), A(w_down)

    wada = np.ascontiguousarray(w_ada.reshape(2, 8, 128, 6144).transpose(0, 2, 1, 3))
    qcols = []
    for j in range(4):
        qcols += list(range(1288 + j * 64, 1288 + (j + 1) * 64))
        qcols += list(range(1288 + (4 + j) * 64, 1288 + (5 + j) * 64))
    perm = (list(range(512, 1280)) + qcols + list(range(1800, 1928)) + list(range(0, 512))
            + list(range(1928, 2056)) + list(range(1280, 1288)))
    win = np.ascontiguousarray(w_in[:, :, perm].reshape(2, 8, 128, 2056).transpose(0, 2, 1, 3))
    wout = np.ascontiguousarray(w_out.reshape(2, 8, 128, 8, 128).transpose(0, 2, 3, 1, 4)).reshape(2, 128, 8, 1024)
    wu4 = w_up.reshape(2, 8, 128, 2, 22, 128)
    wup = np.ascontiguousarray(wu4.transpose(0, 2, 4, 1, 3, 5)).reshape(2, 128, 22, 8, 256)
    wd6 = w_down.reshape(2, 2, 11, 128, 8, 128)
    wdn = np.ascontiguousarray(wd6.transpose(0, 3, 1, 4, 2, 5))
    vec = np.zeros((2, 128, 272), f32)
    vec[:, :, 0:8] = np.moveaxis(_fm(norm_mix), 0, 1)
    vec[:, :, 8:16] = np.moveaxis(_fm(norm_ffn), 0, 1)
    vec[:, :, 16:64] = np.moveaxis(_fm(b_ada), 0, 1)
    csw = _fm(conv_ssd_w)
    vec[:, :, 64:88] = csw.transpose(1, 0, 3, 2).reshape(2, 128, 24)
    vec[:, :, 88:94] = np.moveaxis(_fm(conv_ssd_b), 0, 1)
    cfw = _fm(conv_ffn_w)
    vec[:, :, 94:226] = cfw.transpose(1, 0, 3, 2).reshape(2, 128, 132)
    vec[:, :, 226:270] = np.moveaxis(_fm(conv_ffn_b), 0, 1)
    vec[:, :, 270] = np.concatenate([q_norm, q_norm], axis=1)
    vec[:, :, 271] = np.concatenate([k_norm, k_norm], axis=1)
    tv = np.ascontiguousarray(np.concatenate([dt_bias, a_log, d_skip, sinks, ssd_norm], axis=1))

    in_maps = []
    for c in range(NCORES):
        bp = [2 * c, 2 * c + 1]
        bs = [4 * c + j for j in range(4)]
        xs = np.concatenate([x_prompt[bp[0]], x_prompt[bp[1]], x_sample[bs].reshape(64, 1024)], axis=0)
        xT = np.ascontiguousarray(xs.T.reshape(8, 128, NTOK))
        call = np.concatenate([c_prompt[bp], c_sample[bs]], axis=0)
        cT = np.ascontiguousarray(call.T.reshape(8, 128, 6).transpose(1, 0, 2))
        st_ = state_ssm[:, bs].reshape(2, 4, 2, 4, 64, 64)
        hT0 = np.ascontiguousarray(st_.transpose(0, 1, 2, 5, 3, 4)).reshape(2, 4, 128, 256)
        cc_ = cache_conv_ssd[:, bs].reshape(2, 4, 3, 6, 128)
        cs0 = np.ascontiguousarray(cc_.transpose(0, 1, 4, 3, 2)).reshape(2, 4, 128, 18)
        kc_ = cache_attn_k[:, bs].reshape(2, 4, 128, 128)
        kcT = np.ascontiguousarray(kc_.transpose(0, 1, 3, 2))
        vc = np.ascontiguousarray(cache_attn_v[:, bs].reshape(2, 4, 128, 128))
        cf_ = cache_conv_ffn[:, bs].reshape(2, 4, 2, 2, 22, 128)
        cf0 = np.ascontiguousarray(cf_.transpose(0, 1, 5, 4, 3, 2)).reshape(2, 4, 128, 88)
        in_maps.append(dict(xT=xT, cT=cT, wada=wada, vec=vec, tv=tv, win=win, wout=wout, wup=wup, wdn=wdn,
                            hT0=hT0, cs0=cs0, kcT=kcT, vc=vc, cf0=cf0))

    if "nc" not in _NC_CACHE:
        _NC_CACHE["nc"] = build()
    nc = _NC_CACHE["nc"]
    res = run_bass_kernel_spmd(nc, in_maps, core_ids=list(range(NCORES)))
    rs = res.results

    y_p = np.zeros((16, 2048, 1024), f32); y_s = np.zeros((32, 16, 1024), f32)
    ssm_p = np.zeros((2, 16, 8, 64, 64), f32); ssm_s = np.zeros((2, 32, 8, 64, 64), f32)
    cs_p = np.zeros((2, 16, 3, 768), f32); cs_s = np.zeros((2, 32, 3, 768), f32)
    k_p = np.zeros((2, 16, 128, 2, 64), f32); k_s = np.zeros((2, 32, 128, 2, 64), f32)
    v_p = np.zeros((2, 16, 128, 2, 64), f32); v_s = np.zeros((2, 32, 128, 2, 64), f32)
    cf_p = np.zeros((2, 16, 2, 5632), f32); cf_s = np.zeros((2, 32, 2, 5632), f32)
    for c in range(NCORES):
        r = rs[c]
        yt = np.asarray(r["yT"]).reshape(1024, NTOK).T
        ssmT = np.asarray(r["ssmT"]).reshape(2, 6, 2, 64, 4, 64)
        ssm = ssmT.transpose(0, 1, 2, 4, 5, 3).reshape(2, 6, 8, 64, 64)
        cso = np.asarray(r["cso"]).reshape(2, 6, 128, 6, 3).transpose(0, 1, 4, 3, 2).reshape(2, 6, 3, 768)
        kTo = np.asarray(r["kTo"]).reshape(2, 6, 128, 128).transpose(0, 1, 3, 2).reshape(2, 6, 128, 2, 64)
        vo = np.asarray(r["vo"]).reshape(2, 6, 128, 2, 64)
        cfo = np.asarray(r["cfo"]).reshape(2, 6, 128, 22, 2, 2).transpose(0, 1, 5, 4, 3, 2).reshape(2, 6, 2, 5632)
        for i in range(2):
            b = 2 * c + i
            y_p[b] = yt[i * 2048:(i + 1) * 2048]
            ssm_p[:, b] = ssm[:, i]; cs_p[:, b] = cso[:, i]; k_p[:, b] = kTo[:, i]; v_p[:, b] = vo[:, i]; cf_p[:, b] = cfo[:, i]
        for j in range(4):
            b = 4 * c + j
            y_s[b] = yt[4096 + 16 * j:4096 + 16 * (j + 1)]
            ssm_s[:, b] = ssm[:, 2 + j]; cs_s[:, b] = cso[:, 2 + j]; k_s[:, b] = kTo[:, 2 + j]; v_s[:, b] = vo[:, 2 + j]
            cf_s[:, b] = cfo[:, 2 + j]
    return (y_p, y_s, ssm_p, ssm_s, cs_p, cs_s, k_p, k_s, v_p, v_s, cf_p, cf_s)
```

```python
import contextlib
import numpy as np
import concourse.bass as bass
import concourse.mybir as mybir
from concourse.bass_utils import run_bass_kernel_spmd

F32 = mybir.dt.float32
BF16 = mybir.dt.bfloat16
I32 = mybir.dt.int32
AF = mybir.ActivationFunctionType
ALU = mybir.AluOpType

NCORES = 8
NTOK = 4160
NP = 1088
EPS = 1e-6
NEG = -30000.0
COMPUTE = ("pe", "act", "dve", "pool")
NDMA_SEMS = 24
TAGS = False


class Res:
    __slots__ = ("name", "w", "r")

    def __init__(self, name):
        self.name = name
        self.w = None
        self.r = []


class Prog:
    def __init__(self, nc, stack):
        self.nc = nc
        self.streams = {e: [] for e in ("pe", "act", "dve", "pool", "sp")}
        self.cnt = {e: 0 for e in COMPUTE}
        self.sems = {}
        for e in COMPUTE:
            self.sems[e] = stack.enter_context(nc.semaphore("sem_" + e))
        for q in ("sp", "pool"):
            for i in range(NDMA_SEMS):
                k = "d_%s_%d" % (q, i)
                self.sems[k] = stack.enter_context(nc.semaphore(k))
        self.dma_cnt = {}
        self.dma_rr = {"sp": 0, "pool": 0}
        self.waited = {e: {} for e in self.streams}
        self.n_ops = 0
        self.tag = ""
        self.tagmap = {}

    def res(self, name):
        return Res(name)

    def _deps(self, eng, reads, writes):
        out = {}
        for r in reads:
            if r.w is not None:
                k, v, e = r.w
                if not (e == "pe" and eng == "pe") and out.get(k, 0) < v:
                    out[k] = v
        for r in writes:
            if r.w is not None:
                k, v, e = r.w
                if not (e == "pe" and eng == "pe") and out.get(k, 0) < v:
                    out[k] = v
            for (k, v, e) in r.r:
                if not (e == "pe" and eng == "pe") and out.get(k, 0) < v:
                    out[k] = v
        w = self.waited[eng]
        res = []
        for k, v in out.items():
            if w.get(k, 0) < v:
                w[k] = v
                res.append((k, v))
        return res

    def _commit(self, tok, reads, writes):
        for r in writes:
            r.w = tok
            r.r = []
        for r in reads:
            r.r = [t for t in r.r if t[0] != tok[0]] + [tok]

    def op(self, eng, fn, reads=(), writes=()):
        ex = [r for r in reads if r.name.startswith("bk")]
        if ex:
            reads = [r for r in reads if not r.name.startswith("bk")]
            writes = list(writes) + ex
        waits = self._deps(eng, reads, writes)
        self.cnt[eng] += 1
        tok = (eng, self.cnt[eng], eng)
        self.streams[eng].append((waits, fn, (eng, 1), self.tag))
        self._commit(tok, reads, writes)
        self.n_ops += 1

    def dma(self, queue, out, in_, reads=(), writes=()):
        i = self.dma_rr[queue]
        self.dma_rr[queue] = (i + 1) % NDMA_SEMS
        k = "d_%s_%d" % (queue, i)
        prev = self.dma_cnt.get(k, 0)
        waits = self._deps(queue, reads, writes)
        w = self.waited[queue]
        if prev > 0 and w.get(k, 0) < prev * 16:
            w[k] = prev * 16
            waits.append((k, prev * 16))
        self.dma_cnt[k] = prev + 1
        tok = (k, (prev + 1) * 16, queue)

        def fn(eng, out=out, in_=in_):
            return eng.dma_start(out=out, in_=in_)
        self.streams[queue].append((waits, fn, (k, 16), self.tag))
        self._commit(tok, reads, writes)
        self.n_ops += 1

    def wait_all(self, eng, resources):
        out = {}
        for r in resources:
            toks = list(r.r)
            if r.w is not None:
                toks.append(r.w)
            for (k, v, e) in toks:
                if out.get(k, 0) < v:
                    out[k] = v
        self.streams[eng].append((list(out.items()), None, None, ""))

    def emit(self):
        nc = self.nc
        sems = self.sems
        streams = self.streams

        def run(engh, lst):
            for waits, fn, inc, tag in lst:
                for k, v in waits:
                    engh.wait_ge(sems[k], v)
                if fn is not None:
                    ins = fn(engh)
                    ins.then_inc(sems[inc[0]], inc[1])
                    if TAGS:
                        try:
                            self.tagmap[ins.ins.name] = (tag, [k for k, v in waits])
                        except Exception:
                            pass

        with nc.Block() as block:
            @block.tensor
            def _(e):
                run(e, streams["pe"])

            @block.scalar
            def _(e):
                run(e, streams["act"])

            @block.vector
            def _(e):
                run(e, streams["dve"])

            @block.gpsimd
            def _(e):
                run(e, streams["pool"])

            @block.sync
            def _(e):
                run(e, streams["sp"])


class Tile:
    pass


def build():
    nc = bass.Bass("TRN2", target_bir_lowering=False)

    def din(name, shape):
        return nc.dram_tensor(name, shape, F32, kind="ExternalInput").ap()

    def dout(name, shape):
        return nc.dram_tensor(name, shape, F32, kind="ExternalOutput").ap()

    xT_d = din("xT", [8, 128, NTOK])
    cT_d = din("cT", [128, 8, 6])
    wada_d = din("wada", [2, 128, 8, 6144])
    vec_d = din("vec", [2, 128, 272])
    tv_d = din("tv", [2, 544])
    win_d = din("win", [2, 128, 8, 2056])
    wout_d = din("wout", [2, 128, 8, 1024])
    wup_d = din("wup", [2, 128, 22, 8, 256])
    wdn_d = din("wdn", [2, 128, 2, 8, 11, 128])
    hT0_d = din("hT0", [2, 4, 128, 256])
    cs0_d = din("cs0", [2, 4, 128, 18])
    kcT_d = din("kcT", [2, 4, 128, 128])
    vc_d = din("vc", [2, 4, 128, 128])
    cf0_d = din("cf0", [2, 4, 128, 88])
    yT_d = dout("yT", [8, 128, NTOK])
    ssm_o = dout("ssmT", [2, 6, 128, 256])
    cs_o = dout("cso", [2, 6, 128, 18])
    k_o = dout("kTo", [2, 6, 128, 128])
    v_o = dout("vo", [2, 6, 128, 128])
    cf_o = dout("cfo", [2, 6, 128, 88])

    with contextlib.ExitStack() as st:
        P = Prog(nc, st)
        RO = P.res("outputs")

        def sb(name, shape, dt=F32):
            return st.enter_context(nc.sbuf_tensor(name, shape, dt))

        def MM(out, lhsT, rhs, start, stop, reads, writes, skip=False):
            P.op("pe", lambda e: e.matmul(out, lhsT=lhsT, rhs=rhs, start=start, stop=stop,
                                          skip_group_check=skip), reads, writes)

        def TR(out, in_, ident, reads, writes):
            P.op("pe", lambda e: e.transpose(out, in_, ident), reads, writes)

        def ACT(out, in_, func, reads, writes, scale=1.0, bias=0.0, accum=None):
            if accum is None:
                P.op("act", lambda e: e.activation(out=out, in_=in_, func=func, bias=bias, scale=scale),
                     reads, writes)
            else:
                P.op("act", lambda e: e.activation(out=out, in_=in_, func=func, bias=bias, scale=scale,
                                                   accum_out=accum), reads, writes)

        def ACP(out, in_, reads, writes):
            P.op("act", lambda e: e.copy(out=out, in_=in_), reads, writes)

        def VCP(out, in_, reads, writes):
            P.op("dve", lambda e: e.tensor_copy(out=out, in_=in_), reads, writes)

        def TT(out, in0, in1, op, reads, writes):
            P.op("dve", lambda e: e.tensor_tensor(out=out, in0=in0, in1=in1, op=op), reads, writes)

        def STT(out, in0, scalar, in1, op0, op1, reads, writes):
            P.op("dve", lambda e: e.scalar_tensor_tensor(out=out, in0=in0, scalar=scalar, in1=in1,
                                                         op0=op0, op1=op1), reads, writes)

        def TS(out, in0, s1, s2, op0, op1, reads, writes, eng="dve"):
            if s2 is None:
                P.op(eng, lambda e: e.tensor_scalar(out=out, in0=in0, scalar1=s1, scalar2=None, op0=op0),
                     reads, writes)
            else:
                P.op(eng, lambda e: e.tensor_scalar(out=out, in0=in0, scalar1=s1, scalar2=s2, op0=op0, op1=op1),
                     reads, writes)

        def MSET(eng, ap, val, writes):
            P.op(eng, lambda e: e.memset(ap, val), (), writes)

        BK = []
        BKr = []
        for i in range(8):
            BK.append(st.enter_context(nc.psum_tensor("bk%d" % i, [128, 512], F32)))
            BKr.append(P.res("bk%d" % i))
        B_D0, B_D1, B_GS, B_SM, B_Y, B_ST, B_T1, B_T2 = range(8)
        BT1v = BK[B_T1][:].bitcast(BF16)
        BT2v = BK[B_T2][:].bitcast(BF16)
        BSMv = BK[B_SM][:].bitcast(BF16)

        XT = sb("XT", [128, 8, NP])
        HTB = sb("HTB", [128, 8, NP], BF16)
        WIN = sb("WIN", [128, 8, 2056], BF16)
        WO = [sb("WO%d" % i, [128, 8, 128], BF16) for i in range(3)]
        WU = [sb("WU%d" % i, [128, 8, 256], BF16) for i in range(3)]
        WD = [sb("WD%d" % i, [128, 11, 128], BF16) for i in range(2)]
        UNI = sb("UNI", [128, 11 * NP], BF16)
        ACTT = UNI[:].rearrange("p (f t) -> p f t", f=11)
        XS = UNI[:, 0:3072].rearrange("p (c t) -> p c t", c=6)
        QN = UNI[:, 3072:5120].rearrange("p (c t) -> p c t", c=4)
        XBCh = UNI[:, 5120:8210].rearrange("p (c t) -> p c t", c=6)
        LT = UNI[:, 8212:9236].rearrange("p (h t) -> p h t", h=8)
        MT = UNI[:, 9236:10260].rearrange("p (h t) -> p h t", h=8)
        PT = UNI[:, 10260:11284].rearrange("p (g a j q) -> p g a j q", g=2, a=2, j=4)
        UN2 = sb("UN2", [128, 4096])

        def bfv(a, b):
            return UN2[:, a:b].bitcast(BF16)
        UG = [bfv(0, 1100).rearrange("p (u t) -> p u t", u=2), bfv(1100, 2200).rearrange("p (u t) -> p u t", u=2)]
        SG = [UN2[:, 2200:2712], UN2[:, 2712:3224]]
        DG = [bfv(3224, 3608).rearrange("p (a c) -> p a c", a=6), bfv(3608, 3992).rearrange("p (a c) -> p a c", a=6)]
        SZ = [UN2[:, 0:512], UN2[:, 512:1024]]
        KN32 = UN2[:, 1024:1536]
        SQ2 = [bfv(1536, 1792), bfv(1792, 2048)]
        XDT = [bfv(2048, 2304), bfv(3584, 3840)]
        XW = [bfv(2304, 2560), bfv(3840, 4096)]
        XD = [bfv(2560, 2816), None]
        Y4 = [bfv(2816, 3072), UNI[:, 11284:11796]]
        MIX = sb("MIX", [128, 8, 512], BF16)
        SQN = sb("SQN", [128, 8, 512], BF16)
        PT1 = bfv(3072, 3584).rearrange("p (g a j q) -> p g a j q", g=2, a=2, j=4)
        KT = [[sb("KT%d_%d" % (l, g), [128, 640], BF16) for g in range(2)] for l in range(2)]
        VB = [sb("VB%d" % l, [128, 5, 2, 64], BF16) for l in range(2)]
        V32 = sb("V32", [128, 128])
        TMPF = [sb("TMPF%d" % i, [128, 512]) for i in range(3)]
        BTOK = [sb("BTOK", [128, 128], BF16), UNI[:, 11796:11924]]
        XD[1] = sb("XD1", [128, 512], BF16)
        MT = [MT, sb("MT1", [128, 8, 128], BF16)]
        HT32 = [sb("HT32%d" % l, [128, 256]) for l in range(2)]
        HTBF = [sb("HTBF%d" % l, [128, 512], BF16) for l in range(2)]
        CH = [sb("CH%d" % l, [128, 6, 3], BF16) for l in range(2)]
        CONVO = sb("CONVO", [128, 6, 3])
        CFFO = [sb("CFFO%d" % i, [128, 22, 2, 2]) for i in range(2)]
        CFS = sb("CFS", [128, 4, 22, 2, 2])
        CFFO = CFFO + [CFS[:, j_] for j_ in range(4)]
        FH = [sb("FH%d" % l, [128, 22, 2, 2], BF16) for l in range(2)]
        CFH = sb("CFH", [128, 2, 4, 88], BF16)
        DGX = [sb("DGX%d" % i, [128, 4, 128], BF16) for i in range(2)]
        FEN = sb("FEN", [128, 2])
        GSB = sb("GSB", [128, 2, 128], BF16)
        DTR = sb("DTR", [128, 32]); DTA = sb("DTA", [128, 32]); DT = sb("DT", [128, 32])
        ADT = sb("ADT", [128, 32]); ACS = sb("ACS", [128, 32]); NACS = sb("NACS", [128, 32])
        ACSH = sb("ACSH", [128, 32], BF16); ACSL = sb("ACSL", [128, 32], BF16)
        NACSH = sb("NACSH", [128, 32], BF16); NACSL = sb("NACSL", [128, 32], BF16)
        WDEC = sb("WDEC", [128, 32]); DTW = sb("DTW", [128, 32]); DECF = sb("DECF", [128, 32])
        EACS = sb("EACS", [128, 32]); DECSEL = sb("DECSEL", [128, 4, 4]); SSQ = sb("SSQ", [128, 2])
        IDF = sb("IDF", [128, 128]); IDB = sb("IDB", [128, 128], BF16)
        ONESB = sb("ONESB", [128, 128], BF16)
        BLK1 = sb("BLK1", [128, 128], BF16); TRI = sb("TRI", [128, 128])
        MASKN = sb("MASKN", [128, 128], BF16)
        SEL128 = sb("SEL128", [128, 128]); SEL16 = sb("SEL16", [128, 128])
        BIAS = [sb("BIAS%d" % i, [128, 8, 64], BF16) for i in range(4)]
        NSL = sb("NSL", [128, 8])
        VEC = sb("VEC", [128, 2, 272]); TVB = sb("TVB", [128, 2, 544])
        ANEG = sb("ANEG", [128, 2, 8]); ESK = sb("ESK", [128, 2, 8]); QG8 = sb("QG8", [128, 2])
        MOD = sb("MOD", [128, 2, 48, 6]); CS = sb("CS", [128, 8, 6]); CSB = sb("CSB", [128, 8, 6], BF16)

        R = {}
        for n in ("WIN MOD1 GSB SQN PT1 MIX XS QN KN32 V32 LT PT CONVO CFH "
                  "DTR DTA DT ADT ACS NACS WDEC DTW DECF EACS DECSEL SSQ CONST VEC TVB MOD CS CSB").split():
            R[n] = P.res(n)
        for l in range(2):
            for n in ("KT", "VB", "HT32", "HTBF", "CH", "FH"):
                R["%s%d" % (n, l)] = P.res("%s%d" % (n, l))
        R_WO = [P.res("WO%d" % i) for i in range(3)]
        R_WU = [P.res("WU%d" % i) for i in range(3)]
        R_WD = [P.res("WD%d" % i) for i in range(2)]
        R_SZ = [P.res("SZ%d" % i) for i in range(2)]
        R_SQ2 = [P.res("SQ2%d" % i) for i in range(2)]
        R_TMPF = [P.res("TMPF%d" % i) for i in range(3)]
        R_UG = [P.res("UG%d" % i) for i in range(2)]
        R_SG = [P.res("SG%d" % i) for i in range(2)]
        R_DG = [P.res("DG%d" % i) for i in range(2)]
        R_XB = [P.res("XBCh%d" % c) for c in range(6)]
        R_X = [[P.res("X_%d_%d" % (i, k)) for k in range(8)] for i in range(6)]
        R_H = [P.res("H_%d" % i) for i in range(6)]
        R_AT = [P.res("AT_%d" % i) for i in range(6)]
        R_CFFO = [P.res("CFFO%d" % i) for i in range(6)]
        R_WAB = [P.res("WAB0"), P.res("WAB1")]
        RMOD = [R["MOD"], R["MOD1"]]
        R_XDT = [P.res("XDT%d" % i) for i in range(2)]
        R_XW = [P.res("XW%d" % i) for i in range(2)]
        R_XD = [P.res("XD%d" % i) for i in range(2)]
        R_Y4 = [P.res("Y4%d" % i) for i in range(2)]
        R_BTOK = [P.res("BTOK%d" % i) for i in range(2)]
        R_MT = [P.res("MT%d" % i) for i in range(2)]
        R_DGX = [P.res("DGX%d" % i) for i in range(2)]
        R_UGT = [[P.res("UG%d_%d" % (i, j)) for j in range(6)] for i in range(2)]
        ALIASED = ([R["XS"], R["QN"], R["LT"], R["PT"], R["PT1"], R["KN32"]] + R_MT + R_XDT + R_XW + R_XD + R_Y4 + R_BTOK
                   + R_XB + R_AT + R_SZ + R_SQ2 + R_UG + R_SG + R_DG + R_WAB + R_UGT[0] + R_UGT[1])

        def fence():
            MSET("dve", FEN[:, 0:1], 0.0, ALIASED)

        RC = [R["CONST"]]
        MSET("pool", IDF[:], 1.0, RC)
        P.op("pool", lambda e: e.affine_select(out=IDF[:], in_=IDF[:], pattern=[[-1, 128]],
                                               compare_op=ALU.is_equal, fill=0.0, base=0, channel_multiplier=1), RC, RC)
        MSET("pool", TRI[:], 1.0, RC)
        MSET("pool", SEL128[:], 1.0, RC)
        MSET("pool", SEL16[:], 1.0, RC)
        MSET("pool", MASKN[:], 0.0, RC)
        P.op("pool", lambda e: e.affine_select(out=TRI[:], in_=TRI[:], pattern=[[1, 128]],
                                               compare_op=ALU.is_ge, fill=0.0, base=0, channel_multiplier=-1), RC, RC)
        P.op("pool", lambda e: e.affine_select(out=MASKN[:], in_=MASKN[:], pattern=[[1, 128]],
                                               compare_op=ALU.is_ge, fill=NEG, base=0, channel_multiplier=-1), RC, RC)
        P.op("pool", lambda e: e.affine_select(out=SEL128[:], in_=SEL128[:], pattern=[[0, 128]],
                                               compare_op=ALU.is_equal, fill=0.0, base=-127, channel_multiplier=1), RC, RC)
        P.op("pool", lambda e: e.affine_select(out=SEL16[:], in_=SEL16[:], pattern=[[0, 128]],
                                               compare_op=ALU.is_equal, fill=0.0, base=-15, channel_multiplier=1), RC, RC)
        VCP(IDB[:], IDF[:], RC, RC)
        MSET("dve", ONESB[:], 1.0, RC)
        MSET("dve", BLK1[:], 0.0, RC)
        MSET("dve", BLK1[0:64, 0:64], 1.0, RC)
        MSET("dve", BLK1[64:128, 64:128], 1.0, RC)
        for h in range(8):
            MSET("dve", NSL[:, h:h + 1], -(2.0 ** (-(h + 1))), RC)
        IOT = TMPF[1][:].bitcast(I32)
        for idx, c0 in enumerate((128, 0, 192, 64)):
            P.op("pool", lambda e, c0=c0: e.iota(IOT.rearrange("p (h i) -> p h i", h=8), pattern=[[0, 8], [1, 64]],
                                                 base=c0, channel_multiplier=-1), RC, RC)
            VCP(TMPF[0][:], IOT, RC, RC)
            STT(TMPF[2][:], TMPF[0][:], -1.0, TMPF[0][:], ALU.mult, ALU.max, RC, RC)
            TT(BIAS[idx][:], TMPF[2][:].rearrange("p (h i) -> p h i", h=8),
               NSL[:].unsqueeze(2).to_broadcast([128, 8, 64]), ALU.mult, RC, RC)
        for l_ in range(2):
            for g_ in range(2):
                MSET("pool", KT[l_][g_][:], 0.0, [R["KT%d" % l_]])
        MSET("dve", BIAS[1][64:128, :, :], NEG, RC)
        MSET("dve", BIAS[2][0:64, :, :], NEG, RC)
        P.dma("sp", VEC[:], vec_d.rearrange("l p n -> p l n"), (), [R["VEC"]])
        for l in range(2):
            P.dma("sp", TVB[:, l, :], tv_d[l].partition_broadcast(128), (), [R["TVB"]])
        P.dma("sp", CS[:], cT_d, (), [R["CS"]])
        RV = [R["VEC"], R["TVB"]]
        ACT(ANEG[:], TVB[:, :, 8:16], AF.Exp, RV, RV)
        TS(ANEG[:], ANEG[:], -1.0, None, ALU.mult, None, RV, RV)
        ACT(ESK[:], TVB[:, :, 24:32], AF.Exp, RV, RV)
        TS(QG8[:], VEC[:, :, 270], 0.125, None, ALU.mult, None, RV, RV)
        ACT(CS[:], CS[:], AF.Silu, [R["CS"]], [R["CS"]])
        VCP(CSB[:], CS[:], [R["CS"]], [R["CSB"]])

        passes = []
        for s in range(2):
            for half in range(2):
                tl = []
                for i in range(2):
                    t = Tile()
                    t.seq = s; t.NT = 512; t.QB = 128; t.QA = 64; t.sample = False
                    t.tok0 = half * 1024 + i * 512
                    t.first = (t.tok0 == 0); t.last = (t.tok0 == 1536)
                    t.c0 = i * 512; t.idx = i; t.slot = i * 514
                    tl.append(t)
                passes.append(tl)
        for j in range(4):
            t = Tile()
            t.seq = 2 + j; t.sj = j; t.NT = 16; t.QB = 16; t.QA = 16; t.sample = True
            t.first = True; t.last = True; t.tok0 = 0
            t.c0 = 1024 + 16 * j; t.idx = 2 + j; t.slot = 2 * 514 + 18 * j
            passes[3].append(t)
        pass_dram0 = [0, 1024, 2048, 3072]

        def load_win_piece(l, k):
            P.dma("pool", WIN[:, k, :], win_d[l, :, k, :], (), [R["WIN"]])

        for k in range(8):
            load_win_piece(0, k)

        P.tag = "mod"
        WAB = [UNI[:, 0:4096].rearrange("p (k n) -> p k n", k=8), UNI[:, 4096:8192].rearrange("p (k n) -> p k n", k=8)]
        pi = 0
        for l in range(1):
            for piece in range(12):
                buf = pi % 2
                P.dma("pool", WAB[buf], wada_d[l, :, :, piece * 512:(piece + 1) * 512], (), [R_WAB[buf]])
                for cq in range(4):
                    cc = piece * 4 + cq
                    bank = B_D0 if cc % 2 == 0 else B_D1
                    for k in range(8):
                        MM(BK[bank][:, 0:6], WAB[buf][:, k, cq * 128:(cq + 1) * 128], CSB[:, k, :],
                           k == 0, k == 7, [R_WAB[buf], R["CSB"]], [BKr[bank]])
                    TS(MOD[:, l, cc, :], BK[bank][:, 0:6], VEC[:, l, 16 + cc:17 + cc], None, ALU.add, None,
                       [BKr[bank], R["VEC"]], [R["MOD"]])
                pi += 1
            for kind, nb in ((1, 0), (4, 8)):
                for k in range(8):
                    cc = kind * 8 + k
                    TS(MOD[:, l, cc, :], MOD[:, l, cc, :], 1.0, VEC[:, l, nb + k:nb + k + 1], ALU.add, ALU.mult,
                       [R["MOD"], R["VEC"]], [R["MOD"]])

        def load_wo(l, d):
            P.dma("pool", WO[d % 3][:], wout_d[l, :, d, :].rearrange("p (m c) -> p m c", m=8), (), [R_WO[d % 3]])

        def mod_deferred(l):
            cl = []
            for cc in range(48):
                def fn(cc=cc):
                    old_tag = P.tag
                    P.tag = "mod"
                    i = cc % 3
                    stg = TMPF[i][:].bitcast(BF16).rearrange("p (k n) -> p k n", k=8)
                    P.dma("pool", stg, wada_d[l, :, :, cc * 128:(cc + 1) * 128], (), [R_TMPF[i]])
                    bank = B_T2
                    for k in range(8):
                        MM(BK[bank][:, 0:6], stg[:, k, :], CSB[:, k, :], k == 0, k == 7, [R_TMPF[i], R["CSB"]], [BKr[bank]])
                    TS(MOD[:, l, cc, :], BK[bank][:, 0:6], VEC[:, l, 16 + cc:17 + cc], None, ALU.add, None,
                       [BKr[bank], R["VEC"]], [RMOD[l]])
                    P.tag = old_tag
                cl.append(fn)

            def fin():
                for kind, nb in ((1, 0), (4, 8)):
                    for k in range(8):
                        cc = kind * 8 + k
                        TS(MOD[:, l, cc, :], MOD[:, l, cc, :], 1.0, VEC[:, l, nb + k:nb + k + 1], ALU.add, ALU.mult,
                           [RMOD[l], R["VEC"]], [RMOD[l]])
            cl.append(fin)
            return cl

        def build_dgx(l):
            for c in range(6):
                for tap in range(4):
                    TS(DGX[:, c * 4 + tap, :], IDB[:], VEC[:, l, 64 + c * 4 + tap:65 + c * 4 + tap], None,
                       ALU.mult, None, [R["CONST"], R["VEC"]], [R["DGX"]])

        tmp_rr = [0]

        def tmpf():
            i = tmp_rr[0] % 3
            tmp_rr[0] += 1
            return TMPF[i], R_TMPF[i]

        def norm_stages(l, t, kA, kS):
            NT = t.NT
            cs = slice(t.c0, t.c0 + NT)
            s = t.seq
            tg = ("S:" if t.sample else "P:") + ("norm%d" % (1 if kA == 1 else 2))
            NB = B_ST

            def st1():
                old = P.tag; P.tag = tg
                for k in range(8):
                    ACT(SQN[:, k, :NT], XT[:, k, cs], AF.Square, [R_X[t.idx][k]], [R["SQN"]])
                P.tag = old

            def st2():
                old = P.tag; P.tag = tg
                for k in range(8):
                    MM(BK[NB][:, :NT], ONESB[:], SQN[:, k, :NT], k == 0, k == 7, [R["SQN"], R["CONST"]], [BKr[NB]])
                ACT(BK[NB][:, :NT], BK[NB][:, :NT], AF.Ln, [BKr[NB]], [BKr[NB]], scale=1.0 / 1024, bias=EPS)
                ACT(BK[NB][:, :NT], BK[NB][:, :NT], AF.Exp, [BKr[NB]], [BKr[NB]], scale=-0.5)
                P.tag = old

            def st3():
                old = P.tag; P.tag = tg
                for k in range(8):
                    tb, tr = tmpf()
                    STT(tb[:, :NT], BK[NB][:, :NT], MOD[:, l, kA * 8 + k, s:s + 1], XT[:, k, cs], ALU.mult, ALU.mult,
                        [BKr[NB], RMOD[l], R_X[t.idx][k]], [tr])
                    ACT(HTB[:, k, cs], tb[:, :NT], AF.Identity, [tr, RMOD[l]], [R_H[t.idx]],
                        bias=MOD[:, l, kS * 8 + k, s:s + 1])
                P.tag = old
            return [st1, st2, st3]

        def mixer(l, t, inj):
            NT, QB, QA = t.NT, t.QB, t.QA
            nblk = NT // QB
            nch = NT // QA
            nb8 = nblk * 8
            cs = slice(t.c0, t.c0 + NT)
            s = t.seq
            RH = R_H[t.idx]
            kt, vb = KT[l], VB[l]
            RKT, RVB = R["KT%d" % l], R["VB%d" % l]
            RHT, RHB = R["HT32%d" % l], R["HTBF%d" % l]
            mo = 16 * t.sj if t.sample else 0
            if t.sample:
                j = t.sj
                P.dma("pool", XBCh[:, :, 0:3], cs0_d[l, j].rearrange("p (c k) -> p c k", c=6), (), R_XB)
                P.dma("sp", HT32[l][:], hT0_d[l, j], (), [RHT])
                MSET("dve", HTBF[l][:], 0.0, [RHB])
                VCP(HTBF[l][0:64, 0:256], HT32[l][0:64, :], [RHT], [RHB])
                VCP(HTBF[l][64:128, 256:512], HT32[l][64:128, :], [RHT], [RHB])
                P.dma("pool", kt[0][0:64, 0:128], kcT_d[l, j, 0:64, :], (), [RKT])
                P.dma("pool", kt[1][64:128, 0:128], kcT_d[l, j, 64:128, :], (), [RKT])
                P.dma("pool", vb[:, 0, :, :], vc_d[l, j].rearrange("p (g d) -> p g d", g=2), (), [RVB])
                P.dma("sp", k_o[l, s, :, 0:112], kcT_d[l, j, :, 16:128], (), [RO])
                P.dma("sp", v_o[l, s, 0:112, :], vc_d[l, j, 16:128, :], (), [RO])
            elif t.first:
                MSET("dve", XBCh[:, :, 0:3], 0.0, R_XB)
                MSET("dve", HT32[l][:], 0.0, [RHT])
                MSET("dve", HTBF[l][:], 0.0, [RHB])
            else:
                VCP(XBCh[:, :, 0:3], CH[l][:], [R["CH%d" % l]], R_XB)

            if (not t.sample) or t.sj == 0:
                load_wo(l, 0)
                load_wo(l, 1)
                load_wo(l, 2)

            pre = "S:" if t.sample else "P:"

            def ph_xbc():
                P.tag = pre + "xbc"
                mi = 0
                pend = []
                for c in range(6):
                    bank = B_D0 if mi % 2 == 0 else B_D1
                    mi += 1
                    dgx, dgxr = DGX[c % 2], R_DGX[c % 2]
                    for tap in range(4):
                        TS(dgx[:, tap, :], IDB[:], VEC[:, l, 64 + c * 4 + tap:65 + c * 4 + tap], None,
                           ALU.mult, None, [R["CONST"], R["VEC"]], [dgxr])
                    for k in range(8):
                        MM(BK[bank][:, :NT], WIN[:, k, c * 128:(c + 1) * 128], HTB[:, k, cs], k == 0, k == 7,
                           [R["WIN"], RH], [BKr[bank]])
                    ACP(XBCh[:, c, 3:3 + NT], BK[bank][:, :NT], [BKr[bank]], [R_XB[c]])
                    if t.last:
                        VCP(CONVO[:, c, :], BK[bank][:, NT - 3:NT], [BKr[bank]], [R["CONVO"]])

                    def conv_part(c=c, dgx=dgx, dgxr=dgxr):
                        cb = B_GS if c % 2 == 0 else B_SM
                        for tap in range(4):
                            MM(BK[cb][:, :NT], dgx[:, tap, :], XBCh[:, c, tap:tap + NT], tap == 0, tap == 3,
                               [dgxr, R_XB[c]], [BKr[cb]])
                        ACT(XS[:, c, :NT], BK[cb][:, :NT], AF.Silu, [BKr[cb], R["VEC"]], [R["XS"]],
                            bias=VEC[:, l, 88 + c:89 + c])
                    if pend:
                        pend.pop(0)()
                    pend.append(conv_part)
                while pend:
                    pend.pop(0)()
                if not t.last:
                    VCP(CH[l][:], XBCh[:, :, NT:NT + 3], R_XB, [R["CH%d" % l]])
                else:
                    P.dma("sp", cs_o[l, s], CONVO[:].rearrange("p c k -> p (c k)"), [R["CONVO"]], [RO])

            def ph_qk():
                P.tag = pre + "qk"
                pend = []
                for j in range(5):
                    bank = (B_D0, B_D1, B_GS, B_SM)[j % 4]
                    col = 768 + j * 128
                    for k in range(8):
                        MM(BK[bank][:, :NT], WIN[:, k, col:col + 128], HTB[:, k, cs], k == 0, k == 7,
                           [R["WIN"], RH], [BKr[bank]])
                    sq, sqr = SQ2[j % 2], R_SQ2[j % 2]
                    ACT(sq[:, :NT], BK[bank][:, :NT], AF.Square, [BKr[bank]], [sqr])

                    def stat_part(j=j, bank=bank, sq=sq, sqr=sqr):
                        MM(BK[B_Y][:, :NT], BLK1[:], sq[:, :NT], True, True, [sqr, R["CONST"]], [BKr[B_Y]])
                        tb, tr = tmpf()
                        ACT(tb[:, :NT], BK[B_Y][:, :NT], AF.Ln, [BKr[B_Y]], [tr], scale=1.0 / 64, bias=EPS)
                        ACT(tb[:, :NT], tb[:, :NT], AF.Exp, [tr], [tr], scale=-0.5)
                        if j < 4:
                            STT(QN[:, j, :NT], BK[bank][:, :NT], QG8[:, l:l + 1], tb[:, :NT], ALU.mult, ALU.mult,
                                [BKr[bank], tr, R["VEC"]], [R["QN"]])
                        else:
                            STT(KN32[:, :NT], BK[bank][:, :NT], VEC[:, l, 271:272], tb[:, :NT], ALU.mult, ALU.mult,
                                [BKr[bank], tr, R["VEC"]], [R["KN32"]])
                            ACP(kt[0][0:64, 128:128 + NT], KN32[0:64, :NT], [R["KN32"]], [RKT])
                            VCP(kt[1][64:128, 128:128 + NT], KN32[64:128, :NT], [R["KN32"]], [RKT])
                            if t.last:
                                if t.sample:
                                    P.dma("sp", k_o[l, s, :, 112:128], KN32[:, 0:16], [R["KN32"]], [RO])
                                else:
                                    P.dma("sp", k_o[l, s], KN32[:, NT - 128:NT], [R["KN32"]], [RO])
                    if pend:
                        pend.pop(0)()
                    pend.append(stat_part)
                while pend:
                    pend.pop(0)()

            def ph_vdt():
                P.tag = pre + "vdt"
                for b in range(nblk):
                    bank = (B_Y, B_ST, B_GS, B_T1)[b % 4]
                    for k in range(8):
                        MM(BK[bank][:QB, 0:136], HTB[:, k, t.c0 + b * QB:t.c0 + (b + 1) * QB], WIN[:, k, 1920:2056],
                           k == 0, k == 7, [R["WIN"], RH], [BKr[bank]])
                    ACP(vb[:QB, 1 + b, :, :], BK[bank][:QB, 0:128].rearrange("p (g d) -> p g d", g=2), [BKr[bank]], [RVB])
                    if t.last and b == nblk - 1:
                        VCP(V32[:QB, :], BK[bank][:QB, 0:128], [BKr[bank]], [R["V32"]])
                        if t.sample:
                            P.dma("sp", v_o[l, s, 112:128, :], V32[0:16, :], [R["V32"]], [RO])
                        else:
                            P.dma("sp", v_o[l, s], V32[:, :], [R["V32"]], [RO])
                    TT(DTR[:QB, b * 8:(b + 1) * 8], BK[bank][:QB, 128:136], TVB[:QB, l, 0:8], ALU.add,
                       [BKr[bank], R["TVB"]], [R["DTR"]])

            def ph_dtp1():
                P.tag = pre + "dtp"
                STT(DTA[:QB, :nb8], DTR[:QB, :nb8], -1.0, DTR[:QB, :nb8], ALU.mult, ALU.max, [R["DTR"]], [R["DTA"]])
                ACT(DTA[:QB, :nb8], DTA[:QB, :nb8], AF.Exp, [R["DTA"]], [R["DTA"]], scale=-1.0)
                ACT(DTA[:QB, :nb8], DTA[:QB, :nb8], AF.Ln, [R["DTA"]], [R["DTA"]], bias=1.0)
                STT(DT[:QB, :nb8], DTR[:QB, :nb8], 0.0, DTA[:QB, :nb8], ALU.max, ALU.add, [R["DTR"], R["DTA"]], [R["DT"]])
                TT(ADT[:QB, :nb8].rearrange("p (b h) -> p b h", h=8), DT[:QB, :nb8].rearrange("p (b h) -> p b h", h=8),
                   ANEG[:QB, l, :].unsqueeze(1).to_broadcast([QB, nblk, 8]), ALU.mult, [R["DT"], R["TVB"]], [R["ADT"]])

            def ph_dtp2():
                P.tag = pre + "dtp"
                MM(BK[B_SM][:QB, 0:nb8], TRI[:QB, :QB], ADT[:QB, :nb8], True, True, [R["ADT"], R["CONST"]], [BKr[B_SM]])
                VCP(ACS[:QB, :nb8], BK[B_SM][:QB, 0:nb8], [BKr[B_SM]], [R["ACS"]])
                TS(NACS[:QB, :nb8], ACS[:QB, :nb8], -1.0, None, ALU.mult, None, [R["ACS"]], [R["NACS"]])
                VCP(ACSH[:QB, :nb8], ACS[:QB, :nb8], [R["ACS"]], [R["NACS"]])
                TT(ACSL[:QB, :nb8], ACS[:QB, :nb8], ACSH[:QB, :nb8], ALU.subtract, [R["ACS"], R["NACS"]], [R["NACS"]])
                TS(NACSH[:QB, :nb8], ACSH[:QB, :nb8], -1.0, None, ALU.mult, None, [R["NACS"]], [R["NACS"]])
                TS(NACSL[:QB, :nb8], ACSL[:QB, :nb8], -1.0, None, ALU.mult, None, [R["NACS"]], [R["NACS"]])
                SEL = SEL128 if QB == 128 else SEL16
                MM(BK[B_SM][:, 32:32 + nb8], SEL[:QB, :], ACS[:QB, :nb8], True, True, [R["ACS"], R["CONST"]], [BKr[B_SM]])
                TT(WDEC[:QB, :nb8], BK[B_SM][:QB, 32:32 + nb8], ACS[:QB, :nb8], ALU.subtract, [BKr[B_SM], R["ACS"]], [R["WDEC"]])
                ACT(DECF[:, :nb8], BK[B_SM][:, 32:32 + nb8], AF.Exp, [BKr[B_SM]], [R["DECF"]])

            def ph_dtp3():
                P.tag = pre + "dtp"
                ACT(WDEC[:QB, :nb8], WDEC[:QB, :nb8], AF.Exp, [R["WDEC"]], [R["WDEC"]])
                TT(DTW[:QB, :nb8], DT[:QB, :nb8], WDEC[:QB, :nb8], ALU.mult, [R["DT"], R["WDEC"]], [R["DTW"]])
                ACT(EACS[:QB, :nb8], ACS[:QB, :nb8], AF.Exp, [R["ACS"]], [R["EACS"]])
                dv = DECF[:, :nb8].rearrange("p (b h) -> p b h", h=8)
                VCP(DECSEL[0:64, :nblk, :], dv[0:64, :, 0:4], [R["DECF"]], [R["DECSEL"]])
                VCP(DECSEL[64:128, :nblk, :], dv[64:128, :, 4:8], [R["DECF"]], [R["DECSEL"]])


            def hook(k_):
                for fn_ in inj[k_]:
                    fn_()
            ph_vdt()
            hook("a")
            ph_dtp1()
            ph_xbc()
            hook("b")
            ph_dtp2()
            ph_qk()
            hook("c")
            ph_dtp3()
            hook("d")
            P.tag = pre + "ssd"
            W4 = 4 * QA

            def pieces_of(c):
                if t.sample:
                    pcs = [(0, 0, 128, 0, 0, 0), (1, 128, 16, 0, 1, 1)]
                elif c % 2 == 0:
                    pcs = [(0, c * 64, 128, 0, c // 2, 0), (1, 128 + c * 64, 128, 0, c // 2 + 1, 1)]
                else:
                    pcs = [(0, 128 + (c - 3) * 64, 128, 0, (c - 1) // 2, 2),
                           (1, 128 + (c - 1) * 64, 128, 0, (c - 1) // 2 + 1, 3)]
                if t.first and (not t.sample) and c < 2:
                    pcs = pcs[1:]
                return pcs

            def attn_A(c):
                q0 = c * QA
                pieces = pieces_of(c)
                sb_ = (B_D0, B_D1) if c % 2 == 0 else (B_ST, B_T1)
                pt, ptr = (PT, R["PT"]) if c % 2 == 0 else (PT1, R["PT1"])
                for g in range(2):
                    bank = sb_[g]
                    for (pi_, ktc, nk, pb, vbi, bi) in pieces:
                        o = BK[bank][pb:pb + nk, pi_ * W4:(pi_ + 1) * W4]
                        MM(o, kt[g][:, ktc:ktc + nk], QN[:, :, q0:q0 + QA], True, False,
                           [RKT, R["QN"]], [BKr[bank]])
                        MM(o, IDB[:, pb:pb + nk], BIAS[bi][:, g * 4:(g + 1) * 4, 0:QA], False, True,
                           [R["CONST"]], [BKr[bank]])
                for g in range(2):
                    bank = sb_[g]
                    ACT(pt[:, g, :, :, :QA], BK[bank][:, 0:2 * W4].rearrange("p (a j q) -> p a j q", a=2, j=4),
                        AF.Exp, [BKr[bank]], [ptr])

            def attn_C(c):
                q0 = c * QA
                pieces = pieces_of(c)
                pt, ptr = (PT, R["PT"]) if c % 2 == 0 else (PT1, R["PT1"])
                bden = B_GS if c % 2 == 0 else B_SM
                bo = B_Y if c % 2 == 0 else B_T2
                for g in range(2):
                    o = BK[bden][:, g * W4:(g + 1) * W4]
                    for ii, (pi_, ktc, nk, pb, vbi, bi) in enumerate(pieces):
                        MM(o, ONESB[pb:pb + nk, :], pt[pb:pb + nk, g, pi_, :, :QA], ii == 0, False,
                           [ptr, R["CONST"]], [BKr[bden]])
                    MM(o, SEL128[:, :], ESK[:, l, g * 4:(g + 1) * 4].unsqueeze(2).to_broadcast([128, 4, QA]),
                       False, True, [R["TVB"], R["CONST"]], [BKr[bden]])
                rd, rdr = tmpf()
                ACT(rd[:, :8 * QA], BK[bden][:, 0:8 * QA], AF.Ln, [BKr[bden]], [rdr])
                ACT(rd[:, :8 * QA], rd[:, :8 * QA], AF.Exp, [rdr], [rdr], scale=-1.0)
                for h in range(8):
                    g, r = h // 4, h % 4
                    o = BK[bo][(h % 2) * 64:(h % 2 + 1) * 64, (h // 2) * QA:(h // 2 + 1) * QA]
                    for ii, (pi_, ktc, nk, pb, vbi, bi) in enumerate(pieces):
                        MM(o, vb[pb:pb + nk, vbi, g, :], pt[pb:pb + nk, g, pi_, r, :QA], ii == 0, ii == len(pieces) - 1,
                           [RVB, ptr], [BKr[bo]])
                rd3 = rd[:, :8 * QA].rearrange("p (h q) -> p h q", h=8)
                TT(MIX[0:64, 4:8, mo + q0:mo + q0 + QA], BK[bo][0:64, 0:4 * QA].rearrange("p (m q) -> p m q", m=4),
                   rd3[0:64, 0:8:2, :], ALU.mult, [BKr[bo], rdr], [R["MIX"]])
                TT(MIX[64:128, 4:8, mo + q0:mo + q0 + QA], BK[bo][64:128, 0:4 * QA].rearrange("p (m q) -> p m q", m=4),
                   rd3[64:128, 1:8:2, :], ALU.mult, [BKr[bo], rdr], [R["MIX"]])

            def _attn_first():
                old = P.tag; P.tag = pre + "attn"
                attn_A(0)
                P.tag = old
            attn_first = [_attn_first]
            def ssd_E(b):
                bc = slice(b * QB, (b + 1) * QB)
                szb, szr = SZ[b % 2], R_SZ[b % 2]
                i2 = b % 2
                for k in range(8):
                    MM(BK[B_T2][:QB, :], HTB[:, k, t.c0 + b * QB:t.c0 + (b + 1) * QB], WIN[:, k, 1408:1920],
                       k == 0, k == 7, [R["WIN"], RH], [BKr[B_T2]])
                ACT(szb[:QB, :], BK[B_T2][:QB, :], AF.Exp, [BKr[B_T2]], [szr], scale=-1.0)
                ACT(szb[:QB, :], szb[:QB, :], AF.Ln, [szr], [szr], bias=1.0)
                ACT(szb[:QB, :], szb[:QB, :], AF.Exp, [szr], [szr], scale=-1.0)
                TT(szb[:QB, :], BK[B_T2][:QB, :], szb[:QB, :], ALU.mult, [BKr[B_T2], szr], [szr])
                MM(BK[B_T1][:QB, 384:384 + QB], XS[0:64, 4, bc], XS[0:64, 5, bc], True, True, [R["XS"]], [BKr[B_T1]])
                MM(BK[B_SM][:QB, 64:64 + QB], XS[64:128, 4, bc], XS[64:128, 5, bc], True, True, [R["XS"]], [BKr[B_SM]])
                for c in range(4):
                    TR(BT1v[:QB, c * 128:(c + 1) * 128], XS[:, c, bc], IDB[:], [R["XS"], R["CONST"]], [BKr[B_T1]])
                TR(BT1v[:QB, 512:640], XS[:, 4, bc], IDB[:], [R["XS"], R["CONST"]], [BKr[B_T1]])
                ACP(GSB[:QB, 0, :QB], BK[B_T1][:QB, 384:384 + QB], [BKr[B_T1]], [R["GSB"]])
                ACP(GSB[:QB, 1, :QB], BK[B_SM][:QB, 64:64 + QB], [BKr[B_SM]], [R["GSB"]])
                xt3 = BT1v[:QB, 0:512].rearrange("p (h d) -> p h d", h=8)

                def bc8(tile_):
                    return tile_[:QB, b * 8:(b + 1) * 8].unsqueeze(2).to_broadcast([QB, 8, 64])
                TT(XDT[i2][:QB, :].rearrange("p (h d) -> p h d", h=8), xt3, bc8(DT), ALU.mult, [BKr[B_T1], R["DT"]], [R_XDT[i2]])
                TT(XW[i2][:QB, :].rearrange("p (h d) -> p h d", h=8), xt3, bc8(DTW), ALU.mult, [BKr[B_T1], R["DTW"]], [R_XW[i2]])
                TT(XD[i2][:QB, :].rearrange("p (h d) -> p h d", h=8), xt3,
                   TVB[:QB, l, 16:24].unsqueeze(2).to_broadcast([QB, 8, 64]), ALU.mult, [BKr[B_T1], R["TVB"]], [R_XD[i2]])
                ACP(BTOK[i2][:QB, :], BT1v[:QB, 512:640], [BKr[B_T1]], [R_BTOK[i2]])
                for h in range(8):
                    bank = B_D0 if h < 4 else B_D1
                    r = h % 4
                    o = BK[bank][:QB, r * QB:(r + 1) * QB]
                    col = b * 8 + h
                    MM(o, ACSH[:QB, col:col + 1].to_broadcast([QB, QB]), IDB[:QB, :QB], True, False,
                       [R["NACS"], R["CONST"]], [BKr[bank]])
                    MM(o, ACSL[:QB, col:col + 1].to_broadcast([QB, QB]), IDB[:QB, :QB], False, False,
                       [R["NACS"], R["CONST"]], [BKr[bank]])
                    MM(o, IDB[:QB, :QB], NACSH[:QB, col:col + 1].to_broadcast([QB, QB]), False, False,
                       [R["NACS"], R["CONST"]], [BKr[bank]])
                    MM(o, IDB[:QB, :QB], NACSL[:QB, col:col + 1].to_broadcast([QB, QB]), False, False,
                       [R["NACS"], R["CONST"]], [BKr[bank]])
                    MM(o, IDB[:QB, :QB], MASKN[:QB, :QB], False, True, [R["CONST"]], [BKr[bank]])
                for g in range(2):
                    bank = B_D0 if g == 0 else B_D1
                    ACT(LT[:QB, 4 * g:4 * g + 4, :QB], BK[bank][:QB, 0:4 * QB].rearrange("p (h t) -> p h t", h=4),
                        AF.Exp, [BKr[bank]], [R["LT"]])
                for g in range(2):
                    TT(MT[i2][:QB, 4 * g:4 * g + 4, :QB], LT[:QB, 4 * g:4 * g + 4, :QB],
                       GSB[:QB, g, :QB].unsqueeze(1).to_broadcast([QB, 4, QB]), ALU.mult, [R["LT"], R["GSB"]], [R_MT[i2]])

            def ssd_M(b):
                bc = slice(b * QB, (b + 1) * QB)
                szb, szr = SZ[b % 2], R_SZ[b % 2]
                i2 = b % 2
                MM(BK[B_Y][:QB, :], IDB[:QB, :QB], XD[i2][:QB, :], True, False, [R_XD[i2], R["CONST"]], [BKr[B_Y]], skip=True)
                for h in range(8):
                    MM(BK[B_Y][:QB, h * 64:(h + 1) * 64], MT[i2][:QB, h, :QB], XDT[i2][:QB, h * 64:(h + 1) * 64], False, h == 7,
                       [R_MT[i2], R_XDT[i2]], [BKr[B_Y]], skip=True)
                MM(BK[B_ST][:QB, :], XS[:, 5, bc], HTBF[l][:, :], True, True, [R["XS"], RHB], [BKr[B_ST]])
                for g in range(2):
                    MM(BK[B_GS][g * 64:(g + 1) * 64, 256:512], BTOK[i2][:QB, g * 64:(g + 1) * 64],
                       XW[i2][:QB, g * 256:(g + 1) * 256], True, True, [R_BTOK[i2], R_XW[i2]], [BKr[B_GS]])
                hv = HT32[l][:].rearrange("p (h d) -> p h d", h=4)
                P.op("pool", lambda e, a=hv, b_=DECSEL[:, b, :].unsqueeze(2).to_broadcast([128, 4, 64]):
                     e.tensor_tensor(out=a, in0=a, in1=b_, op=ALU.mult), [RHT, R["DECSEL"]], [RHT])
                TT(HT32[l][:], HT32[l][:], BK[B_GS][:, 256:512], ALU.add, [RHT, BKr[B_GS]], [RHT])
                ACP(HTBF[l][0:64, 0:256], HT32[l][0:64, :], [RHT], [RHB])
                ACP(HTBF[l][64:128, 256:512], HT32[l][64:128, :], [RHT], [RHB])
                t1, t1r = tmpf()
                ev = EACS[:QB, b * 8:(b + 1) * 8]
                TT(t1[:QB, :].rearrange("p (h d) -> p h d", h=8), BK[B_ST][:QB, :].rearrange("p (h d) -> p h d", h=8),
                   ev.unsqueeze(2).to_broadcast([QB, 8, 64]), ALU.mult, [BKr[B_ST], R["EACS"]], [t1r])
                TT(t1[:QB, :], BK[B_Y][:QB, :], t1[:QB, :], ALU.add, [BKr[B_Y], t1r], [t1r])
                P.op("pool", lambda e, a=t1[:QB, :], b_=szb[:QB, :]: e.tensor_tensor(out=a, in0=a, in1=b_, op=ALU.mult),
                     [t1r, szr], [t1r])
                ACT(Y4[i2][:QB, :], t1[:QB, :], AF.Square, [t1r], [R_Y4[i2], R["SSQ"]], accum=SSQ[:QB, 0:1])
                ACT(SSQ[:QB, 1:2], SSQ[:QB, 0:1], AF.Ln, [R["SSQ"]], [R["SSQ"]], scale=1.0 / 512, bias=EPS)
                ACT(SSQ[:QB, 1:2], SSQ[:QB, 1:2], AF.Exp, [R["SSQ"]], [R["SSQ"]], scale=-0.5)
                STT(Y4[i2][:QB, :], t1[:QB, :], SSQ[:QB, 1:2], TVB[:QB, l, 32:544], ALU.mult, ALU.mult,
                    [t1r, R["SSQ"], R["TVB"]], [R_Y4[i2]])

            def ssd_L(b, t2=False):
                i2 = b % 2
                old_tag = P.tag
                P.tag = pre + "ssd"
                vw, off, br = (BT2v, 0, BKr[B_T2]) if t2 else (BSMv, 512, BKr[B_SM])
                for c in range(4):
                    TR(vw[:, off + c * QB:off + (c + 1) * QB], Y4[i2][:QB, c * 128:(c + 1) * 128], IDB[:QB, :QB],
                       [R_Y4[i2], R["CONST"]], [br])
                ACP(MIX[:, 0:4, mo + b * QB:mo + (b + 1) * QB], vw[:, off:off + 4 * QB].rearrange("p (c t) -> p c t", c=4),
                    [br], [R["MIX"]])
                P.tag = old_tag
            deferL = [nblk - 2, nblk - 1] if nblk >= 3 else []

            for step in range(nblk + 2):
                if step < nblk:
                    ssd_E(step)
                if step == nblk:
                    attn_first[0]()
                if 0 <= step - 1 < nblk:
                    ssd_M(step - 1)
                if 0 <= step - 2 < nblk and (step - 2) not in deferL:
                    ssd_L(step - 2)
            if t.last:
                P.dma("sp", ssm_o[l, s], HT32[l][:], [RHT], [RO])

            P.tag = pre + "attn"
            for c in range(nch):
                if c + 1 < nch:
                    attn_A(c + 1)
                if c == 0 and deferL:
                    ssd_L(deferL[0], t2=True)
                attn_C(c)
                if c == 0 and deferL:
                    ssd_L(deferL[1], t2=True)
            if not t.last:
                VCP(kt[0][0:64, 0:128], kt[0][0:64, NT:NT + 128], [RKT], [RKT])
                VCP(kt[1][64:128, 0:128], kt[1][64:128, NT:NT + 128], [RKT], [RKT])
                VCP(vb[:, 0, :, :], vb[:, nblk, :, :], [RVB], [RVB])
            if not t.sample:
                wout_phase(l, [t], inj.get("w"))

        def wout_phase(l, tl, nstages=None):
            P.tag = ("S:" if tl[0].sample else "P:") + "wout"
            wtag = P.tag
            c_lo = tl[0].c0
            ncol = sum(t_.NT for t_ in tl)
            for d in range(8):
                wo, wor = WO[d % 3], R_WO[d % 3]
                bank = B_D0 if d % 2 == 0 else B_D1
                for m in range(8):
                    MM(BK[bank][:, :ncol], wo[:, m, :], MIX[:, m, :ncol], m == 0, m == 7, [wor, R["MIX"]], [BKr[bank]])
                if d + 3 < 8:
                    load_wo(l, d + 3)
                if nstages is not None and d in (0, 2, 4):
                    nstages[d // 2]()
                    P.tag = wtag
                for t_ in tl:
                    o0 = t_.c0 - c_lo
                    cs_ = slice(t_.c0, t_.c0 + t_.NT)
                    STT(XT[:, d, cs_], BK[bank][:, o0:o0 + t_.NT], MOD[:, l, 2 * 8 + d, t_.seq:t_.seq + 1], XT[:, d, cs_],
                        ALU.mult, ALU.add, [BKr[bank], RMOD[l], R_X[t_.idx][d]], [R_X[t_.idx][d]])

        def load_wu(l, f):
            P.dma("pool", WU[f % 3][:], wup_d[l, :, f], (), [R_WU[f % 3]])

        def load_wd(l, n):
            P.dma("pool", WD[n % 2][:], wdn_d[l, :, n // 8, n % 8], (), [R_WD[n % 2]])

        JT = Tile()
        JT.joint = True; JT.sample = True; JT.NT = 64; JT.c0 = 1024; JT.slot = 2 * 514; JT.idx = 2
        JT.first = True; JT.last = True; JT.seq = -1

        def ffn(l, tiles_all, prefetch, extra=None):
            fence()
            extra = extra or []
            pending = []
            tiles = [t_ for t_ in tiles_all if not t_.sample]
            stl = [t_ for t_ in tiles_all if t_.sample]
            if stl:
                tiles = tiles + [JT]
            wd_seq = [(0, d_) for d_ in range(8)] + [(1, d_) for d_ in range(8)]
            wd_step = [-1]
            it = 0
            P.tag = "ffn_up"
            for hf in range(2):
                P.tag = "ffn_up"
                for fi in range(11):
                    f = hf * 11 + fi
                    if f + 2 < 22:
                        load_wu(l, f + 2)
                    if prefetch:
                        prefetch.pop(0)()
                    for _ in range(3):
                        if extra:
                            extra.pop(0)()
                    wu, wur = WU[f % 3], R_WU[f % 3]
                    dg, dgr = DG[f % 2], R_DG[f % 2]
                    for ug in range(2):
                        for tap in range(3):
                            col = 94 + (ug * 22 + f) * 3 + tap
                            TS(dg[:, ug * 3 + tap, :], IDB[:], VEC[:, l, col:col + 1], None, ALU.mult, None,
                               [R["CONST"], R["VEC"]], [dgr])
                    ugb = UG[f % 2]
                    for ti, t in enumerate(tiles):
                        NT = t.NT
                        cs = slice(t.c0, t.c0 + NT)
                        so = t.slot
                        bu, bg, bcu, bcg = (B_D0, B_D1, B_GS, B_SM) if it % 2 == 0 else (B_Y, B_ST, B_T1, B_T2)
                        sg, sgr = SG[it % 2], R_SG[it % 2]
                        ugr = R_UGT[f % 2][t.idx]
                        it += 1
                        if getattr(t, "joint", False):
                            RHj = [R_H[i_] for i_ in range(2, 6)]
                            uv = [ugb[:, u_, so:so + 72].rearrange("p (j w) -> p j w", w=18) for u_ in range(2)]
                            VCP(ugb[:, :, so:so + 72].rearrange("p u (j w) -> p u j w", w=18)[:, :, :, 0:2],
                                CFH[:, l, :, :].rearrange("p j (f u k) -> p f u j k", f=22, u=2)[:, f], [R["CFH"]], [ugr])
                            for k in range(8):
                                MM(BK[bu][:, :NT], wu[:, k, 0:128], HTB[:, k, cs], k == 0, k == 7, [wur] + RHj, [BKr[bu]])
                            for k in range(8):
                                MM(BK[bg][:, :NT], wu[:, k, 128:256], HTB[:, k, cs], k == 0, k == 7, [wur] + RHj, [BKr[bg]])
                            pu = BK[bu][:, 0:64].rearrange("p (j w) -> p j w", w=16)
                            pg = BK[bg][:, 0:64].rearrange("p (j w) -> p j w", w=16)
                            ACP(uv[0][:, :, 2:18], pu, [BKr[bu]], [ugr])
                            ACP(uv[1][:, :, 2:18], pg, [BKr[bg]], [ugr])
                            VCP(CFS[:, :, f, 0, :], pu[:, :, 14:16], [BKr[bu]], [R_CFFO[2]])
                            VCP(CFS[:, :, f, 1, :], pg[:, :, 14:16], [BKr[bg]], [R_CFFO[2]])

                            def part2j(NT=NT, cs=cs, bcu=bcu, bcg=bcg, sg=sg, sgr=sgr, dg=dg, dgr=dgr, uv=uv, ugr=ugr, f=f, fi=fi):
                                for tap in range(3):
                                    MM(BK[bcu][:, :NT], dg[:, tap, :], uv[0][:, :, tap:tap + 16], tap == 0, tap == 2,
                                       [dgr, ugr], [BKr[bcu]])
                                for tap in range(3):
                                    MM(BK[bcg][:, :NT], dg[:, 3 + tap, :], uv[1][:, :, tap:tap + 16], tap == 0, tap == 2,
                                       [dgr, ugr], [BKr[bcg]])
                                ACT(sg[:, :NT], BK[bcg][:, :NT], AF.Silu, [BKr[bcg], R["VEC"]], [sgr],
                                    bias=VEC[:, l, 226 + 22 + f:227 + 22 + f])
                                STT(ACTT[:, fi, cs], BK[bcu][:, :NT], VEC[:, l, 226 + f:227 + f], sg[:, :NT], ALU.add, ALU.mult,
                                    [BKr[bcu], R["VEC"], sgr], [R_AT[2]])
                            if pending:
                                pending.pop(0)()
                            pending.append(part2j)
                            continue
                        if t.sample:
                            VCP(ugb[:, :, so:so + 2],
                                CFH[:, l, t.sj, :].rearrange("p (f u k) -> p f u k", f=22, u=2)[:, f, :, :], [R["CFH"]], [ugr])
                        elif t.first:
                            MSET("dve", ugb[:, :, so:so + 2], 0.0, [ugr])
                        elif ti == 0:
                            VCP(ugb[:, :, so:so + 2], FH[l][:, f, :, :], [R["FH%d" % l]], [ugr])
                        else:
                            pso = tiles[ti - 1].slot + tiles[ti - 1].NT
                            VCP(ugb[:, :, so:so + 2], ugb[:, :, pso:pso + 2], [R_UGT[f % 2][tiles[ti - 1].idx]], [ugr])
                        for k in range(8):
                            MM(BK[bu][:, :NT], wu[:, k, 0:128], HTB[:, k, cs], k == 0, k == 7, [wur, R_H[t.idx]], [BKr[bu]])
                        for k in range(8):
                            MM(BK[bg][:, :NT], wu[:, k, 128:256], HTB[:, k, cs], k == 0, k == 7, [wur, R_H[t.idx]], [BKr[bg]])
                        ACP(ugb[:, 0, so + 2:so + 2 + NT], BK[bu][:, :NT], [BKr[bu]], [ugr])
                        ACP(ugb[:, 1, so + 2:so + 2 + NT], BK[bg][:, :NT], [BKr[bg]], [ugr])
                        if t.last:
                            VCP(CFFO[t.idx][:, f, 0, :], BK[bu][:, NT - 2:NT], [BKr[bu]], [R_CFFO[t.idx]])
                            VCP(CFFO[t.idx][:, f, 1, :], BK[bg][:, NT - 2:NT], [BKr[bg]], [R_CFFO[t.idx]])
                        elif ti == len(tiles) - 1 or tiles[ti + 1].sample:
                            VCP(FH[l][:, f, :, :], ugb[:, :, so + NT:so + NT + 2], [ugr], [R["FH%d" % l]])
                        def part2(NT=NT, cs=cs, so=so, bcu=bcu, bcg=bcg, sg=sg, sgr=sgr, dg=dg, dgr=dgr, ugb=ugb, ugr=ugr,
                                  f=f, fi=fi, t=t):
                            for tap in range(3):
                                MM(BK[bcu][:, :NT], dg[:, tap, :], ugb[:, 0, so + tap:so + tap + NT], tap == 0, tap == 2,
                                   [dgr, ugr], [BKr[bcu]])
                            for tap in range(3):
                                MM(BK[bcg][:, :NT], dg[:, 3 + tap, :], ugb[:, 1, so + tap:so + tap + NT], tap == 0, tap == 2,
                                   [dgr, ugr], [BKr[bcg]])
                            ACT(sg[:, :NT], BK[bcg][:, :NT], AF.Silu, [BKr[bcg], R["VEC"]], [sgr],
                                bias=VEC[:, l, 226 + 22 + f:227 + 22 + f])
                            STT(ACTT[:, fi, cs], BK[bcu][:, :NT], VEC[:, l, 226 + f:227 + f], sg[:, :NT], ALU.add, ALU.mult,
                                [BKr[bcu], R["VEC"], sgr], [R_AT[t.idx]])
                        if pending:
                            pending.pop(0)()
                        pending.append(part2)
                while pending:
                    pending.pop(0)()
                P.tag = "ffn_down"
                order = [(d_, t_) for d_ in range(8) for t_ in tiles]
                prev_piece = None
                for (d, t) in order:
                    piece = (hf, d)
                    if piece != prev_piece:
                        wd_step[0] += 1
                        nxt_i = wd_step[0] + 1
                        if nxt_i < len(wd_seq):
                            hf_n, d_n = wd_seq[nxt_i]
                            P.dma("pool", WD[nxt_i % 2][:], wdn_d[l, :, hf_n, d_n], (), [R_WD[nxt_i % 2]])
                        prev_piece = piece
                    n = wd_step[0]
                    wd, wdr = WD[n % 2], R_WD[n % 2]
                    NT = t.NT
                    cs = slice(t.c0, t.c0 + NT)
                    bank = it % 8
                    it += 1
                    for fi in range(11):
                        MM(BK[bank][:, :NT], wd[:, fi, :], ACTT[:, fi, cs], fi == 0, fi == 10,
                           [wdr, R_AT[t.idx]], [BKr[bank]])
                    if getattr(t, "joint", False):
                        for t_ in stl:
                            cs_ = slice(t_.c0, t_.c0 + 16)
                            o0 = t_.c0 - 1024
                            STT(XT[:, d, cs_], BK[bank][:, o0:o0 + 16], MOD[:, l, 5 * 8 + d, t_.seq:t_.seq + 1], XT[:, d, cs_],
                                ALU.mult, ALU.add, [BKr[bank], RMOD[l], R_X[t_.idx][d]], [R_X[t_.idx][d]])
                        continue
                    STT(XT[:, d, cs], BK[bank][:, :NT], MOD[:, l, 5 * 8 + d, t.seq:t.seq + 1], XT[:, d, cs],
                        ALU.mult, ALU.add, [BKr[bank], RMOD[l], R_X[t.idx][d]], [R_X[t.idx][d]])
            while prefetch:
                prefetch.pop(0)()
            while extra:
                extra.pop(0)()
            for t in tiles_all:
                if t.last:
                    rr = R_CFFO[2] if t.sample else R_CFFO[t.idx]
                    P.dma("sp", cf_o[l, t.seq], CFFO[t.idx].rearrange("p f u k -> p (f u k)"), [rr], [RO])

        for pi_, tiles in enumerate(passes):
            c0d = pass_dram0[pi_]
            ncol = sum(t.NT for t in tiles)
            for t in tiles:
                P.dma("sp", XT[:, :, t.c0:t.c0 + t.NT],
                      xT_d[:, :, c0d + t.c0:c0d + t.c0 + t.NT].rearrange("c p t -> p c t"), (),
                      [R_X[t.idx][k] for k in range(8)])
            if pi_ == 3:
                P.dma("pool", CFH[:], cf0_d.rearrange("l j p n -> p l j n"), (), [R["CFH"]])
            for l in range(2):
                fence()
                load_wu(l, 0)
                load_wu(l, 1)
                load_wd(l, 0)
                n1 = norm_stages(l, tiles[0], 1, 0)
                for fn_ in n1:
                    fn_()
                n2prev = None
                for i, t in enumerate(tiles):
                    inj = {"a": [], "b": [], "c": [], "d": []}
                    if n2prev is not None:
                        inj["a"].append(n2prev[1])
                        inj["b"].append(n2prev[2])
                    if i + 1 < len(tiles):
                        n1n = norm_stages(l, tiles[i + 1], 1, 0)
                        if t.sample:
                            inj["b"].append(n1n[0])
                            inj["c"].append(n1n[1])
                            inj["d"].append(n1n[2])
                        else:
                            inj["w"] = n1n
                    mixer(l, t, inj)
                    if t.sample:
                        n2prev = None
                    else:
                        n2prev = norm_stages(l, t, 4, 3)
                        n2prev[0]()
                if n2prev is not None:
                    n2prev[1]()
                    n2prev[2]()
                stl = [t for t in tiles if t.sample]
                if stl:
                    wout_phase(l, stl)
                    for t in stl:
                        for fn_ in norm_stages(l, t, 4, 3):
                            fn_()
                nxt = pi_ * 2 + l + 1
                pf = []
                if nxt < 8:
                    pf = [(lambda k=k, ln=nxt % 2: load_win_piece(ln, k)) for k in range(8)]
                ffn(l, tiles, pf, mod_deferred(1) if (pi_ == 0 and l == 0) else None)
            for t in tiles:
                P.dma("sp", yT_d[:, :, c0d + t.c0:c0d + t.c0 + t.NT].rearrange("c p t -> p c t"),
                      XT[:, :, t.c0:t.c0 + t.NT], [R_X[t.idx][k] for k in range(8)], [RO])
        P.wait_all("sp", [RO])
        P.emit()
    return nc


_NC_CACHE = {}


def _fm(v):
    sh = v.shape
    c = sh[-1] // 128
    a = v.reshape(sh[:-1] + (c, 128))
    return np.moveaxis(a, -1, 0)


def kernel(x_prompt, x_sample, c_prompt, c_sample, state_ssm, cache_conv_ssd, cache_attn_k,
           cache_attn_v, cache_conv_ffn, w_ada, b_ada, norm_mix, w_in, conv_ssd_w, conv_ssd_b,
           dt_bias, a_log, d_skip, ssd_norm, q_norm, k_norm, sinks, w_out, norm_ffn, w_up,
           conv_ffn_w, conv_ffn_b, w_down):
    f32 = np.float32
    A = lambda v: np.asarray(v, dtype=f32)
    x_prompt, x_sample, c_prompt, c_sample = A(x_prompt), A(x_sample), A(c_prompt), A(c_sample)
    state_ssm, cache_conv_ssd, cache_attn_k, cache_attn_v, cache_conv_ffn = (
        A(state_ssm), A(cache_conv_ssd), A(cache_attn_k), A(cache_attn_v), A(cache_conv_ffn))
    w_ada, b_ada, norm_mix, w_in, conv_ssd_w, conv_ssd_b = A(w_ada), A(b_ada), A(norm_mix), A(w_in), A(conv_ssd_w), A(conv_ssd_b)
    dt_bias, a_log, d_skip, ssd_norm, q_norm, k_norm, sinks = A(dt_bias), A(a_log), A(d_skip), A(ssd_norm), A(q_norm), A(k_norm), A(sinks)
    w_out, norm_ffn, w_up, conv_ffn_w, conv_ffn_b, w_down = A(w_out), A(norm_ffn), A(w_up), A(conv_ffn_w), A(conv_ffn_b), A(w_down)

    wada = np.ascontiguousarray(w_ada.reshape(2, 8, 128, 6144).transpose(0, 2, 1, 3))
    qcols = []
    for j in range(4):
        qcols += list(range(1288 + j * 64, 1288 + (j + 1) * 64))
        qcols += list(range(1288 + (4 + j) * 64, 1288 + (5 + j) * 64))
    perm = (list(range(512, 1280)) + qcols + list(range(1800, 1928)) + list(range(0, 512))
            + list(range(1928, 2056)) + list(range(1280, 1288)))
    win = np.ascontiguousarray(w_in[:, :, perm].reshape(2, 8, 128, 2056).transpose(0, 2, 1, 3))
    wout = np.ascontiguousarray(w_out.reshape(2, 8, 128, 8, 128).transpose(0, 2, 3, 1, 4)).reshape(2, 128, 8, 1024)
    wu4 = w_up.reshape(2, 8, 128, 2, 22, 128)
    wup = np.ascontiguousarray(wu4.transpose(0, 2, 4, 1, 3, 5)).reshape(2, 128, 22, 8, 256)
    wd6 = w_down.reshape(2, 2, 11, 128, 8, 128)
    wdn = np.ascontiguousarray(wd6.transpose(0, 3, 1, 4, 2, 5))
    vec = np.zeros((2, 128, 272), f32)
    vec[:, :, 0:8] = np.moveaxis(_fm(norm_mix), 0, 1)
    vec[:, :, 8:16] = np.moveaxis(_fm(norm_ffn), 0, 1)
    vec[:, :, 16:64] = np.moveaxis(_fm(b_ada), 0, 1)
    csw = _fm(conv_ssd_w)
    vec[:, :, 64:88] = csw.transpose(1, 0, 3, 2).reshape(2, 128, 24)
    vec[:, :, 88:94] = np.moveaxis(_fm(conv_ssd_b), 0, 1)
    cfw = _fm(conv_ffn_w)
    vec[:, :, 94:226] = cfw.transpose(1, 0, 3, 2).reshape(2, 128, 132)
    vec[:, :, 226:270] = np.moveaxis(_fm(conv_ffn_b), 0, 1)
    vec[:, :, 270] = np.concatenate([q_norm, q_norm], axis=1)
    vec[:, :, 271] = np.concatenate([k_norm, k_norm], axis=1)
    tv = np.ascontiguousarray(np.concatenate([dt_bias, a_log, d_skip, sinks, ssd_norm], axis=1))

    in_maps = []
    for c in range(NCORES):
        bp = [2 * c, 2 * c + 1]
        bs = [4 * c + j for j in range(4)]
        xs = np.concatenate([x_prompt[bp[0]], x_prompt[bp[1]], x_sample[bs].reshape(64, 1024)], axis=0)
        xT = np.ascontiguousarray(xs.T.reshape(8, 128, NTOK))
        call = np.concatenate([c_prompt[bp], c_sample[bs]], axis=0)
        cT = np.ascontiguousarray(call.T.reshape(8, 128, 6).transpose(1, 0, 2))
        st_ = state_ssm[:, bs].reshape(2, 4, 2, 4, 64, 64)
        hT0 = np.ascontiguousarray(st_.transpose(0, 1, 2, 5, 3, 4)).reshape(2, 4, 128, 256)
        cc_ = cache_conv_ssd[:, bs].reshape(2, 4, 3, 6, 128)
        cs0 = np.ascontiguousarray(cc_.transpose(0, 1, 4, 3, 2)).reshape(2, 4, 128, 18)
        kc_ = cache_attn_k[:, bs].reshape(2, 4, 128, 128)
        kcT = np.ascontiguousarray(kc_.transpose(0, 1, 3, 2))
        vc = np.ascontiguousarray(cache_attn_v[:, bs].reshape(2, 4, 128, 128))
        cf_ = cache_conv_ffn[:, bs].reshape(2, 4, 2, 2, 22, 128)
        cf0 = np.ascontiguousarray(cf_.transpose(0, 1, 5, 4, 3, 2)).reshape(2, 4, 128, 88)
        in_maps.append(dict(xT=xT, cT=cT, wada=wada, vec=vec, tv=tv, win=win, wout=wout, wup=wup, wdn=wdn,
                            hT0=hT0, cs0=cs0, kcT=kcT, vc=vc, cf0=cf0))

    if "nc" not in _NC_CACHE:
        _NC_CACHE["nc"] = build()
    nc = _NC_CACHE["nc"]
    res = run_bass_kernel_spmd(nc, in_maps, core_ids=list(range(NCORES)))
    rs = res.results

    y_p = np.zeros((16, 2048, 1024), f32); y_s = np.zeros((32, 16, 1024), f32)
    ssm_p = np.zeros((2, 16, 8, 64, 64), f32); ssm_s = np.zeros((2, 32, 8, 64, 64), f32)
    cs_p = np.zeros((2, 16, 3, 768), f32); cs_s = np.zeros((2, 32, 3, 768), f32)
    k_p = np.zeros((2, 16, 128, 2, 64), f32); k_s = np.zeros((2, 32, 128, 2, 64), f32)
    v_p = np.zeros((2, 16, 128, 2, 64), f32); v_s = np.zeros((2, 32, 128, 2, 64), f32)
    cf_p = np.zeros((2, 16, 2, 5632), f32); cf_s = np.zeros((2, 32, 2, 5632), f32)
    for c in range(NCORES):
        r = rs[c]
        yt = np.asarray(r["yT"]).reshape(1024, NTOK).T
        ssmT = np.asarray(r["ssmT"]).reshape(2, 6, 2, 64, 4, 64)
        ssm = ssmT.transpose(0, 1, 2, 4, 5, 3).reshape(2, 6, 8, 64, 64)
        cso = np.asarray(r["cso"]).reshape(2, 6, 128, 6, 3).transpose(0, 1, 4, 3, 2).reshape(2, 6, 3, 768)
        kTo = np.asarray(r["kTo"]).reshape(2, 6, 128, 128).transpose(0, 1, 3, 2).reshape(2, 6, 128, 2, 64)
        vo = np.asarray(r["vo"]).reshape(2, 6, 128, 2, 64)
        cfo = np.asarray(r["cfo"]).reshape(2, 6, 128, 22, 2, 2).transpose(0, 1, 5, 4, 3, 2).reshape(2, 6, 2, 5632)
        for i in range(2):
            b = 2 * c + i
            y_p[b] = yt[i * 2048:(i + 1) * 2048]
            ssm_p[:, b] = ssm[:, i]; cs_p[:, b] = cso[:, i]; k_p[:, b] = kTo[:, i]; v_p[:, b] = vo[:, i]; cf_p[:, b] = cfo[:, i]
        for j in range(4):
            b = 4 * c + j
            y_s[b] = yt[4096 + 16 * j:4096 + 16 * (j + 1)]
            ssm_s[:, b] = ssm[:, 2 + j]; cs_s[:, b] = cso[:, 2 + j]; k_s[:, b] = kTo[:, 2 + j]; v_s[:, b] = vo[:, 2 + j]
            cf_s[:, b] = cfo[:, 2 + j]
    return (y_p, y_s, ssm_p, ssm_s, cs_p, cs_s, k_p, k_s, v_p, v_s, cf_p, cf_s)
```

```python
import contextlib
import numpy as np
import concourse.bass as bass
import concourse.mybir as mybir
from concourse.bass_utils import run_bass_kernel_spmd

F32 = mybir.dt.float32
BF16 = mybir.dt.bfloat16
I32 = mybir.dt.int32
AF = mybir.ActivationFunctionType
ALU = mybir.AluOpType

NCORES = 8
NTOK = 4160
NP = 1088
EPS = 1e-6
NEG = -30000.0
COMPUTE = ("pe", "act", "dve", "pool")
NDMA_SEMS = 24
TAGS = False


class Res:
    __slots__ = ("name", "w", "r")

    def __init__(self, name):
        self.name = name
        self.w = None
        self.r = []


class Prog:
    def __init__(self, nc, stack):
        self.nc = nc
        self.streams = {e: [] for e in ("pe", "act", "dve", "pool", "sp")}
        self.cnt = {e: 0 for e in COMPUTE}
        self.sems = {}
        for e in COMPUTE:
            self.sems[e] = stack.enter_context(nc.semaphore("sem_" + e))
        for q in ("sp", "pool"):
            for i in range(NDMA_SEMS):
                k = "d_%s_%d" % (q, i)
                self.sems[k] = stack.enter_context(nc.semaphore(k))
        self.dma_cnt = {}
        self.dma_rr = {"sp": 0, "pool": 0}
        self.waited = {e: {} for e in self.streams}
        self.n_ops = 0
        self.tag = ""
        self.tagmap = {}

    def res(self, name):
        return Res(name)

    def _deps(self, eng, reads, writes):
        out = {}
        for r in reads:
            if r.w is not None:
                k, v, e = r.w
                if not (e == "pe" and eng == "pe") and out.get(k, 0) < v:
                    out[k] = v
        for r in writes:
            if r.w is not None:
                k, v, e = r.w
                if not (e == "pe" and eng == "pe") and out.get(k, 0) < v:
                    out[k] = v
            for (k, v, e) in r.r:
                if not (e == "pe" and eng == "pe") and out.get(k, 0) < v:
                    out[k] = v
        w = self.waited[eng]
        res = []
        for k, v in out.items():
            if w.get(k, 0) < v:
                w[k] = v
                res.append((k, v))
        return res

    def _commit(self, tok, reads, writes):
        for r in writes:
            r.w = tok
            r.r = []
        for r in reads:
            r.r = [t for t in r.r if t[0] != tok[0]] + [tok]

    def op(self, eng, fn, reads=(), writes=()):
        ex = [r for r in reads if r.name.startswith("bk")]
        if ex:
            reads = [r for r in reads if not r.name.startswith("bk")]
            writes = list(writes) + ex
        waits = self._deps(eng, reads, writes)
        self.cnt[eng] += 1
        tok = (eng, self.cnt[eng], eng)
        self.streams[eng].append((waits, fn, (eng, 1), self.tag))
        self._commit(tok, reads, writes)
        self.n_ops += 1

    def dma(self, queue, out, in_, reads=(), writes=()):
        i = self.dma_rr[queue]
        self.dma_rr[queue] = (i + 1) % NDMA_SEMS
        k = "d_%s_%d" % (queue, i)
        prev = self.dma_cnt.get(k, 0)
        waits = self._deps(queue, reads, writes)
        w = self.waited[queue]
        if prev > 0 and w.get(k, 0) < prev * 16:
            w[k] = prev * 16
            waits.append((k, prev * 16))
        self.dma_cnt[k] = prev + 1
        tok = (k, (prev + 1) * 16, queue)

        def fn(eng, out=out, in_=in_):
            return eng.dma_start(out=out, in_=in_)
        self.streams[queue].append((waits, fn, (k, 16), self.tag))
        self._commit(tok, reads, writes)
        self.n_ops += 1

    def wait_all(self, eng, resources):
        out = {}
        for r in resources:
            toks = list(r.r)
            if r.w is not None:
                toks.append(r.w)
            for (k, v, e) in toks:
                if out.get(k, 0) < v:
                    out[k] = v
        self.streams[eng].append((list(out.items()), None, None, ""))

    def emit(self):
        nc = self.nc
        sems = self.sems
        streams = self.streams

        def run(engh, lst):
            for waits, fn, inc, tag in lst:
                for k, v in waits:
                    engh.wait_ge(sems[k], v)
                if fn is not None:
                    ins = fn(engh)
                    ins.then_inc(sems[inc[0]], inc[1])
                    if TAGS:
                        try:
                            self.tagmap[ins.ins.name] = (tag, [k for k, v in waits])
                        except Exception:
                            pass

        with nc.Block() as block:
            @block.tensor
            def _(e):
                run(e, streams["pe"])

            @block.scalar
            def _(e):
                run(e, streams["act"])

            @block.vector
            def _(e):
                run(e, streams["dve"])

            @block.gpsimd
            def _(e):
                run(e, streams["pool"])

            @block.sync
            def _(e):
                run(e, streams["sp"])


class Tile:
    pass


def build():
    nc = bass.Bass("TRN2", target_bir_lowering=False)

    def din(name, shape):
        return nc.dram_tensor(name, shape, F32, kind="ExternalInput").ap()

    def dout(name, shape):
        return nc.dram_tensor(name, shape, F32, kind="ExternalOutput").ap()

    xT_d = din("xT", [8, 128, NTOK])
    cT_d = din("cT", [128, 8, 6])
    wada_d = din("wada", [2, 128, 8, 6144])
    vec_d = din("vec", [2, 128, 272])
    tv_d = din("tv", [2, 544])
    win_d = din("win", [2, 128, 8, 2056])
    wout_d = din("wout", [2, 128, 8, 1024])
    wup_d = din("wup", [2, 128, 22, 8, 256])
    wdn_d = din("wdn", [2, 128, 2, 8, 11, 128])
    hT0_d = din("hT0", [2, 4, 128, 256])
    cs0_d = din("cs0", [2, 4, 128, 18])
    kcT_d = din("kcT", [2, 4, 128, 128])
    vc_d = din("vc", [2, 4, 128, 128])
    cf0_d = din("cf0", [2, 4, 128, 88])
    yT_d = dout("yT", [8, 128, NTOK])
    ssm_o = dout("ssmT", [2, 6, 128, 256])
    cs_o = dout("cso", [2, 6, 128, 18])
    k_o = dout("kTo", [2, 6, 128, 128])
    v_o = dout("vo", [2, 6, 128, 128])
    cf_o = dout("cfo", [2, 6, 128, 88])

    with contextlib.ExitStack() as st:
        P = Prog(nc, st)
        RO = P.res("outputs")

        def sb(name, shape, dt=F32):
            return st.enter_context(nc.sbuf_tensor(name, shape, dt))

        def MM(out, lhsT, rhs, start, stop, reads, writes, skip=False):
            P.op("pe", lambda e: e.matmul(out, lhsT=lhsT, rhs=rhs, start=start, stop=stop,
                                          skip_group_check=skip), reads, writes)

        def TR(out, in_, ident, reads, writes):
            P.op("pe", lambda e: e.transpose(out, in_, ident), reads, writes)

        def ACT(out, in_, func, reads, writes, scale=1.0, bias=0.0, accum=None):
            if accum is None:
                P.op("act", lambda e: e.activation(out=out, in_=in_, func=func, bias=bias, scale=scale),
                     reads, writes)
            else:
                P.op("act", lambda e: e.activation(out=out, in_=in_, func=func, bias=bias, scale=scale,
                                                   accum_out=accum), reads, writes)

        def ACP(out, in_, reads, writes):
            P.op("act", lambda e: e.copy(out=out, in_=in_), reads, writes)

        def VCP(out, in_, reads, writes):
            P.op("dve", lambda e: e.tensor_copy(out=out, in_=in_), reads, writes)

        def TT(out, in0, in1, op, reads, writes):
            P.op("dve", lambda e: e.tensor_tensor(out=out, in0=in0, in1=in1, op=op), reads, writes)

        def STT(out, in0, scalar, in1, op0, op1, reads, writes):
            P.op("dve", lambda e: e.scalar_tensor_tensor(out=out, in0=in0, scalar=scalar, in1=in1,
                                                         op0=op0, op1=op1), reads, writes)

        def TS(out, in0, s1, s2, op0, op1, reads, writes, eng="dve"):
            if s2 is None:
                P.op(eng, lambda e: e.tensor_scalar(out=out, in0=in0, scalar1=s1, scalar2=None, op0=op0),
                     reads, writes)
            else:
                P.op(eng, lambda e: e.tensor_scalar(out=out, in0=in0, scalar1=s1, scalar2=s2, op0=op0, op1=op1),
                     reads, writes)

        def MSET(eng, ap, val, writes):
            P.op(eng, lambda e: e.memset(ap, val), (), writes)

        BK = []
        BKr = []
        for i in range(8):
            BK.append(st.enter_context(nc.psum_tensor("bk%d" % i, [128, 512], F32)))
            BKr.append(P.res("bk%d" % i))
        B_D0, B_D1, B_GS, B_SM, B_Y, B_ST, B_T1, B_T2 = range(8)
        BT1v = BK[B_T1][:].bitcast(BF16)
        BT2v = BK[B_T2][:].bitcast(BF16)
        BSMv = BK[B_SM][:].bitcast(BF16)

        XT = sb("XT", [128, 8, NP])
        HTB = sb("HTB", [128, 8, NP], BF16)
        WIN = sb("WIN", [128, 8, 2056], BF16)
        WO = [sb("WO%d" % i, [128, 8, 128], BF16)[:] for i in range(3)]
        WU = [sb("WU%d" % i, [128, 8, 256], BF16) for i in range(3)]
        WD = [sb("WD%d" % i, [128, 11, 128], BF16) for i in range(2)]
        WO = WO + [WD[1][:, 0:8, :]]
        UNI = sb("UNI", [128, 11 * NP], BF16)
        ACTT = UNI[:].rearrange("p (f t) -> p f t", f=11)
        XS = UNI[:, 0:3072].rearrange("p (c t) -> p c t", c=6)
        QN = UNI[:, 3072:5120].rearrange("p (c t) -> p c t", c=4)
        XBCh = UNI[:, 5120:8210].rearrange("p (c t) -> p c t", c=6)
        LT = UNI[:, 8212:9236].rearrange("p (h t) -> p h t", h=8)
        MT = UNI[:, 9236:10260].rearrange("p (h t) -> p h t", h=8)
        PT = UNI[:, 10260:11284].rearrange("p (g a j q) -> p g a j q", g=2, a=2, j=4)
        UN2 = sb("UN2", [128, 4096])

        def bfv(a, b):
            return UN2[:, a:b].bitcast(BF16)
        UG = [bfv(0, 1100).rearrange("p (u t) -> p u t", u=2), bfv(1100, 2200).rearrange("p (u t) -> p u t", u=2)]
        SG = [UN2[:, 2200:2712], UN2[:, 2712:3224]]
        DG = [bfv(3224, 3608).rearrange("p (a c) -> p a c", a=6), bfv(3608, 3992).rearrange("p (a c) -> p a c", a=6)]
        SZ = [UN2[:, 0:512], UN2[:, 512:1024]]
        KN32 = UN2[:, 1024:1536]
        SQ2 = [bfv(1536, 1792), bfv(1792, 2048)]
        XDT = [bfv(2048, 2304), bfv(3584, 3840)]
        XW = [bfv(2304, 2560), bfv(3840, 4096)]
        XD = [bfv(2560, 2816), None]
        Y4 = [bfv(2816, 3072), UNI[:, 11284:11796]]
        MIX = sb("MIX", [128, 8, 512], BF16)
        SQN = sb("SQN", [128, 8, 512], BF16)
        PT1 = bfv(3072, 3584).rearrange("p (g a j q) -> p g a j q", g=2, a=2, j=4)
        KT = [[sb("KT%d_%d" % (l, g), [128, 640], BF16) for g in range(2)] for l in range(2)]
        VB = [sb("VB%d" % l, [128, 5, 2, 64], BF16) for l in range(2)]
        V32 = sb("V32", [128, 128])
        TMPF = [sb("TMPF%d" % i, [128, 512]) for i in range(3)]
        BTOK = [sb("BTOK", [128, 128], BF16), UNI[:, 11796:11924]]
        XD[1] = sb("XD1", [128, 512], BF16)
        MT = [MT, sb("MT1", [128, 8, 128], BF16)]
        HT32 = [sb("HT32%d" % l, [128, 256]) for l in range(2)]
        HTBF = [sb("HTBF%d" % l, [128, 512], BF16) for l in range(2)]
        CH = [sb("CH%d" % l, [128, 6, 3], BF16) for l in range(2)]
        CONVO = sb("CONVO", [128, 6, 3])
        CFFO = [sb("CFFO%d" % i, [128, 22, 2, 2]) for i in range(2)]
        CFS = sb("CFS", [128, 4, 22, 2, 2])
        CFFO = CFFO + [CFS[:, j_] for j_ in range(4)]
        FH = [sb("FH%d" % l, [128, 22, 2, 2], BF16) for l in range(2)]
        CFH = sb("CFH", [128, 2, 4, 88], BF16)
        DGX = [sb("DGX%d" % i, [128, 4, 128], BF16) for i in range(2)]
        FEN = sb("FEN", [128, 2])
        GSB = sb("GSB", [128, 2, 128], BF16)
        DTR = sb("DTR", [128, 32]); DTA = sb("DTA", [128, 32]); DT = sb("DT", [128, 32])
        ADT = sb("ADT", [128, 32]); ACS = sb("ACS", [128, 32]); NACS = sb("NACS", [128, 32])
        ACSH = sb("ACSH", [128, 32], BF16); ACSL = sb("ACSL", [128, 32], BF16)
        NACSH = sb("NACSH", [128, 32], BF16); NACSL = sb("NACSL", [128, 32], BF16)
        WDEC = sb("WDEC", [128, 32]); DTW = sb("DTW", [128, 32]); DECF = sb("DECF", [128, 32])
        EACS = sb("EACS", [128, 32]); DECSEL = sb("DECSEL", [128, 4, 4]); SSQ = sb("SSQ", [128, 2])
        IDF = sb("IDF", [128, 128]); IDB = sb("IDB", [128, 128], BF16)
        ONESB = sb("ONESB", [128, 128], BF16)
        BLK1 = sb("BLK1", [128, 128], BF16); TRI = sb("TRI", [128, 128])
        MASKN = sb("MASKN", [128, 128], BF16)
        SEL128 = sb("SEL128", [128, 128]); SEL16 = sb("SEL16", [128, 128])
        BIAS = [sb("BIAS%d" % i, [128, 8, 64], BF16) for i in range(4)]
        NSL = sb("NSL", [128, 8])
        VEC = sb("VEC", [128, 2, 272]); TVB = sb("TVB", [128, 2, 544])
        ANEG = sb("ANEG", [128, 2, 8]); ESK = sb("ESK", [128, 2, 8]); QG8 = sb("QG8", [128, 2])
        MOD = sb("MOD", [128, 2, 48, 6]); CS = sb("CS", [128, 8, 6]); CSB = sb("CSB", [128, 8, 6], BF16)

        R = {}
        for n in ("WIN GSB SQN PT1 MIX XS QN KN32 V32 LT PT CONVO CFH "
                  "DTR DTA DT ADT ACS NACS WDEC DTW DECF EACS DECSEL SSQ CONST VEC TVB MOD CS CSB").split():
            R[n] = P.res(n)
        for l in range(2):
            for n in ("KT", "VB", "HT32", "HTBF", "CH", "FH"):
                R["%s%d" % (n, l)] = P.res("%s%d" % (n, l))
        R_WD = [P.res("WD%d" % i) for i in range(2)]
        R_WO = [P.res("WO%d" % i) for i in range(3)] + [R_WD[1]]
        R_WU = [P.res("WU%d" % i) for i in range(3)]
        R_SZ = [P.res("SZ%d" % i) for i in range(2)]
        R_SQ2 = [P.res("SQ2%d" % i) for i in range(2)]
        R_TMPF = [P.res("TMPF%d" % i) for i in range(3)]
        R_UG = [P.res("UG%d" % i) for i in range(2)]
        R_SG = [P.res("SG%d" % i) for i in range(2)]
        R_DG = [P.res("DG%d" % i) for i in range(2)]
        R_XB = [P.res("XBCh%d" % c) for c in range(6)]
        R_X = [[P.res("X_%d_%d" % (i, k)) for k in range(8)] for i in range(6)]
        R_H = [P.res("H_%d" % i) for i in range(6)]
        R_AT = [P.res("AT_%d" % i) for i in range(6)]
        R_CFFO = [P.res("CFFO%d" % i) for i in range(6)]
        R_WAB = [P.res("WAB0"), P.res("WAB1")]
        R_XDT = [P.res("XDT%d" % i) for i in range(2)]
        R_XW = [P.res("XW%d" % i) for i in range(2)]
        R_XD = [P.res("XD%d" % i) for i in range(2)]
        R_Y4 = [P.res("Y4%d" % i) for i in range(2)]
        R_BTOK = [P.res("BTOK%d" % i) for i in range(2)]
        R_MT = [P.res("MT%d" % i) for i in range(2)]
        R_DGX = [P.res("DGX%d" % i) for i in range(2)]
        R_UGT = [[P.res("UG%d_%d" % (i, j)) for j in range(6)] for i in range(2)]
        ALIASED = ([R["XS"], R["QN"], R["LT"], R["PT"], R["PT1"], R["KN32"]] + R_MT + R_XDT + R_XW + R_XD + R_Y4 + R_BTOK
                   + R_XB + R_AT + R_SZ + R_SQ2 + R_UG + R_SG + R_DG + R_WAB + R_UGT[0] + R_UGT[1])

        def fence():
            MSET("dve", FEN[:, 0:1], 0.0, ALIASED)

        RC = [R["CONST"]]
        MSET("pool", IDF[:], 1.0, RC)
        P.op("pool", lambda e: e.affine_select(out=IDF[:], in_=IDF[:], pattern=[[-1, 128]],
                                               compare_op=ALU.is_equal, fill=0.0, base=0, channel_multiplier=1), RC, RC)
        MSET("pool", TRI[:], 1.0, RC)
        MSET("pool", SEL128[:], 1.0, RC)
        MSET("pool", SEL16[:], 1.0, RC)
        MSET("pool", MASKN[:], 0.0, RC)
        P.op("pool", lambda e: e.affine_select(out=TRI[:], in_=TRI[:], pattern=[[1, 128]],
                                               compare_op=ALU.is_ge, fill=0.0, base=0, channel_multiplier=-1), RC, RC)
        P.op("pool", lambda e: e.affine_select(out=MASKN[:], in_=MASKN[:], pattern=[[1, 128]],
                                               compare_op=ALU.is_ge, fill=NEG, base=0, channel_multiplier=-1), RC, RC)
        P.op("pool", lambda e: e.affine_select(out=SEL128[:], in_=SEL128[:], pattern=[[0, 128]],
                                               compare_op=ALU.is_equal, fill=0.0, base=-127, channel_multiplier=1), RC, RC)
        P.op("pool", lambda e: e.affine_select(out=SEL16[:], in_=SEL16[:], pattern=[[0, 128]],
                                               compare_op=ALU.is_equal, fill=0.0, base=-15, channel_multiplier=1), RC, RC)
        VCP(IDB[:], IDF[:], RC, RC)
        MSET("dve", ONESB[:], 1.0, RC)
        MSET("dve", BLK1[:], 0.0, RC)
        MSET("dve", BLK1[0:64, 0:64], 1.0, RC)
        MSET("dve", BLK1[64:128, 64:128], 1.0, RC)
        for h in range(8):
            MSET("dve", NSL[:, h:h + 1], -(2.0 ** (-(h + 1))), RC)
        IOT = TMPF[1][:].bitcast(I32)
        for idx, c0 in enumerate((128, 0, 192, 64)):
            P.op("pool", lambda e, c0=c0: e.iota(IOT.rearrange("p (h i) -> p h i", h=8), pattern=[[0, 8], [1, 64]],
                                                 base=c0, channel_multiplier=-1), RC, RC)
            VCP(TMPF[0][:], IOT, RC, RC)
            STT(TMPF[2][:], TMPF[0][:], -1.0, TMPF[0][:], ALU.mult, ALU.max, RC, RC)
            TT(BIAS[idx][:], TMPF[2][:].rearrange("p (h i) -> p h i", h=8),
               NSL[:].unsqueeze(2).to_broadcast([128, 8, 64]), ALU.mult, RC, RC)
        for l_ in range(2):
            for g_ in range(2):
                MSET("pool", KT[l_][g_][:], 0.0, [R["KT%d" % l_]])
        MSET("dve", BIAS[1][64:128, :, :], NEG, RC)
        MSET("dve", BIAS[2][0:64, :, :], NEG, RC)
        P.dma("sp", VEC[:], vec_d.rearrange("l p n -> p l n"), (), [R["VEC"]])
        for l in range(2):
            P.dma("sp", TVB[:, l, :], tv_d[l].partition_broadcast(128), (), [R["TVB"]])
        P.dma("sp", CS[:], cT_d, (), [R["CS"]])
        RV = [R["VEC"], R["TVB"]]
        ACT(ANEG[:], TVB[:, :, 8:16], AF.Exp, RV, RV)
        TS(ANEG[:], ANEG[:], -1.0, None, ALU.mult, None, RV, RV)
        ACT(ESK[:], TVB[:, :, 24:32], AF.Exp, RV, RV)
        TS(QG8[:], VEC[:, :, 270], 0.125, None, ALU.mult, None, RV, RV)
        ACT(CS[:], CS[:], AF.Silu, [R["CS"]], [R["CS"]])
        VCP(CSB[:], CS[:], [R["CS"]], [R["CSB"]])

        passes = []
        for s in range(2):
            for half in range(2):
                tl = []
                for i in range(2):
                    t = Tile()
                    t.seq = s; t.NT = 512; t.QB = 128; t.QA = 64; t.sample = False
                    t.tok0 = half * 1024 + i * 512
                    t.first = (t.tok0 == 0); t.last = (t.tok0 == 1536)
                    t.c0 = i * 512; t.idx = i; t.slot = i * 514
                    tl.append(t)
                passes.append(tl)
        for j in range(4):
            t = Tile()
            t.seq = 2 + j; t.sj = j; t.NT = 16; t.QB = 16; t.QA = 16; t.sample = True
            t.first = True; t.last = True; t.tok0 = 0
            t.c0 = 1024 + 16 * j; t.idx = 2 + j; t.slot = 2 * 514 + 18 * j
            passes[3].append(t)
        pass_dram0 = [0, 1024, 2048, 3072]

        def load_win_piece(l, k):
            P.dma("pool", WIN[:, k, :], win_d[l, :, k, :], (), [R["WIN"]])

        for k in range(8):
            load_win_piece(0, k)

        P.tag = "mod"
        WAB = [UNI[:, 0:4096].rearrange("p (k n) -> p k n", k=8), UNI[:, 4096:8192].rearrange("p (k n) -> p k n", k=8)]
        pi = 0
        for l in range(2):
            for piece in range(12):
                buf = pi % 2
                P.dma("pool", WAB[buf], wada_d[l, :, :, piece * 512:(piece + 1) * 512], (), [R_WAB[buf]])
                for cq in range(4):
                    cc = piece * 4 + cq
                    bank = B_D0 if cc % 2 == 0 else B_D1
                    for k in range(8):
                        MM(BK[bank][:, 0:6], WAB[buf][:, k, cq * 128:(cq + 1) * 128], CSB[:, k, :],
                           k == 0, k == 7, [R_WAB[buf], R["CSB"]], [BKr[bank]])
                    TS(MOD[:, l, cc, :], BK[bank][:, 0:6], VEC[:, l, 16 + cc:17 + cc], None, ALU.add, None,
                       [BKr[bank], R["VEC"]], [R["MOD"]])
                pi += 1
            for kind, nb in ((1, 0), (4, 8)):
                for k in range(8):
                    cc = kind * 8 + k
                    TS(MOD[:, l, cc, :], MOD[:, l, cc, :], 1.0, VEC[:, l, nb + k:nb + k + 1], ALU.add, ALU.mult,
                       [R["MOD"], R["VEC"]], [R["MOD"]])

        def load_wo(l, d):
            P.dma("pool", WO[d % 4], wout_d[l, :, d, :].rearrange("p (m c) -> p m c", m=8), (), [R_WO[d % 4]])

        def build_dgx(l):
            for c in range(6):
                for tap in range(4):
                    TS(DGX[:, c * 4 + tap, :], IDB[:], VEC[:, l, 64 + c * 4 + tap:65 + c * 4 + tap], None,
                       ALU.mult, None, [R["CONST"], R["VEC"]], [R["DGX"]])

        tmp_rr = [0]

        def tmpf():
            i = tmp_rr[0] % 3
            tmp_rr[0] += 1
            return TMPF[i], R_TMPF[i]

        def norm_stages(l, t, kA, kS):
            NT = t.NT
            cs = slice(t.c0, t.c0 + NT)
            s = t.seq
            tg = ("S:" if t.sample else "P:") + ("norm%d" % (1 if kA == 1 else 2))
            NB = B_ST

            def st1():
                old = P.tag; P.tag = tg
                for k in range(8):
                    ACT(SQN[:, k, :NT], XT[:, k, cs], AF.Square, [R_X[t.idx][k]], [R["SQN"]])
                P.tag = old

            def st2():
                old = P.tag; P.tag = tg
                for k in range(8):
                    MM(BK[NB][:, :NT], ONESB[:], SQN[:, k, :NT], k == 0, k == 7, [R["SQN"], R["CONST"]], [BKr[NB]])
                ACT(BK[NB][:, :NT], BK[NB][:, :NT], AF.Ln, [BKr[NB]], [BKr[NB]], scale=1.0 / 1024, bias=EPS)
                ACT(BK[NB][:, :NT], BK[NB][:, :NT], AF.Exp, [BKr[NB]], [BKr[NB]], scale=-0.5)
                P.tag = old

            def st3():
                old = P.tag; P.tag = tg
                for k in range(8):
                    tb, tr = tmpf()
                    STT(tb[:, :NT], BK[NB][:, :NT], MOD[:, l, kA * 8 + k, s:s + 1], XT[:, k, cs], ALU.mult, ALU.mult,
                        [BKr[NB], R["MOD"], R_X[t.idx][k]], [tr])
                    ACT(HTB[:, k, cs], tb[:, :NT], AF.Identity, [tr, R["MOD"]], [R_H[t.idx]],
                        bias=MOD[:, l, kS * 8 + k, s:s + 1])
                P.tag = old
            return [st1, st2, st3]

        def mixer(l, t, inj):
            NT, QB, QA = t.NT, t.QB, t.QA
            nblk = NT // QB
            nch = NT // QA
            nb8 = nblk * 8
            cs = slice(t.c0, t.c0 + NT)
            s = t.seq
            RH = R_H[t.idx]
            kt, vb = KT[l], VB[l]
            RKT, RVB = R["KT%d" % l], R["VB%d" % l]
            RHT, RHB = R["HT32%d" % l], R["HTBF%d" % l]
            mo = 16 * t.sj if t.sample else 0
            if t.sample:
                j = t.sj
                P.dma("pool", XBCh[:, :, 0:3], cs0_d[l, j].rearrange("p (c k) -> p c k", c=6), (), R_XB)
                P.dma("sp", HT32[l][:], hT0_d[l, j], (), [RHT])
                MSET("dve", HTBF[l][:], 0.0, [RHB])
                VCP(HTBF[l][0:64, 0:256], HT32[l][0:64, :], [RHT], [RHB])
                VCP(HTBF[l][64:128, 256:512], HT32[l][64:128, :], [RHT], [RHB])
                P.dma("pool", kt[0][0:64, 0:128], kcT_d[l, j, 0:64, :], (), [RKT])
                P.dma("pool", kt[1][64:128, 0:128], kcT_d[l, j, 64:128, :], (), [RKT])
                P.dma("pool", vb[:, 0, :, :], vc_d[l, j].rearrange("p (g d) -> p g d", g=2), (), [RVB])
                P.dma("sp", k_o[l, s, :, 0:112], kcT_d[l, j, :, 16:128], (), [RO])
                P.dma("sp", v_o[l, s, 0:112, :], vc_d[l, j, 16:128, :], (), [RO])
            elif t.first:
                MSET("dve", XBCh[:, :, 0:3], 0.0, R_XB)
                MSET("dve", HT32[l][:], 0.0, [RHT])
                MSET("dve", HTBF[l][:], 0.0, [RHB])
            else:
                VCP(XBCh[:, :, 0:3], CH[l][:], [R["CH%d" % l]], R_XB)

            if (not t.sample) or t.sj == 0:
                load_wo(l, 0)
                load_wo(l, 1)
                load_wo(l, 2)
                load_wo(l, 3)

            pre = "S:" if t.sample else "P:"

            def ph_xbc():
                P.tag = pre + "xbc"
                mi = 0
                pend = []
                for c in range(6):
                    bank = B_D0 if mi % 2 == 0 else B_D1
                    mi += 1
                    dgx, dgxr = DGX[c % 2], R_DGX[c % 2]
                    for tap in range(4):
                        TS(dgx[:, tap, :], IDB[:], VEC[:, l, 64 + c * 4 + tap:65 + c * 4 + tap], None,
                           ALU.mult, None, [R["CONST"], R["VEC"]], [dgxr])
                    for k in range(8):
                        MM(BK[bank][:, :NT], WIN[:, k, c * 128:(c + 1) * 128], HTB[:, k, cs], k == 0, k == 7,
                           [R["WIN"], RH], [BKr[bank]])
                    ACP(XBCh[:, c, 3:3 + NT], BK[bank][:, :NT], [BKr[bank]], [R_XB[c]])
                    if t.last:
                        VCP(CONVO[:, c, :], BK[bank][:, NT - 3:NT], [BKr[bank]], [R["CONVO"]])

                    def conv_part(c=c, dgx=dgx, dgxr=dgxr):
                        cb = B_GS if c % 2 == 0 else B_SM
                        for tap in range(4):
                            MM(BK[cb][:, :NT], dgx[:, tap, :], XBCh[:, c, tap:tap + NT], tap == 0, tap == 3,
                               [dgxr, R_XB[c]], [BKr[cb]])
                        ACT(XS[:, c, :NT], BK[cb][:, :NT], AF.Silu, [BKr[cb], R["VEC"]], [R["XS"]],
                            bias=VEC[:, l, 88 + c:89 + c])
                    if pend:
                        pend.pop(0)()
                    pend.append(conv_part)
                while pend:
                    pend.pop(0)()
                if not t.last:
                    VCP(CH[l][:], XBCh[:, :, NT:NT + 3], R_XB, [R["CH%d" % l]])
                else:
                    P.dma("sp", cs_o[l, s], CONVO[:].rearrange("p c k -> p (c k)"), [R["CONVO"]], [RO])

            def ph_qk():
                P.tag = pre + "qk"
                pend = []
                for j in range(5):
                    bank = (B_D0, B_D1, B_GS, B_SM)[j % 4]
                    col = 768 + j * 128
                    for k in range(8):
                        MM(BK[bank][:, :NT], WIN[:, k, col:col + 128], HTB[:, k, cs], k == 0, k == 7,
                           [R["WIN"], RH], [BKr[bank]])
                    sq, sqr = SQ2[j % 2], R_SQ2[j % 2]
                    ACT(sq[:, :NT], BK[bank][:, :NT], AF.Square, [BKr[bank]], [sqr])

                    def stat_part(j=j, bank=bank, sq=sq, sqr=sqr):
                        MM(BK[B_Y][:, :NT], BLK1[:], sq[:, :NT], True, True, [sqr, R["CONST"]], [BKr[B_Y]])
                        tb, tr = tmpf()
                        ACT(tb[:, :NT], BK[B_Y][:, :NT], AF.Ln, [BKr[B_Y]], [tr], scale=1.0 / 64, bias=EPS)
                        ACT(tb[:, :NT], tb[:, :NT], AF.Exp, [tr], [tr], scale=-0.5)
                        if j < 4:
                            STT(QN[:, j, :NT], BK[bank][:, :NT], QG8[:, l:l + 1], tb[:, :NT], ALU.mult, ALU.mult,
                                [BKr[bank], tr, R["VEC"]], [R["QN"]])
                        else:
                            STT(KN32[:, :NT], BK[bank][:, :NT], VEC[:, l, 271:272], tb[:, :NT], ALU.mult, ALU.mult,
                                [BKr[bank], tr, R["VEC"]], [R["KN32"]])
                            ACP(kt[0][0:64, 128:128 + NT], KN32[0:64, :NT], [R["KN32"]], [RKT])
                            VCP(kt[1][64:128, 128:128 + NT], KN32[64:128, :NT], [R["KN32"]], [RKT])
                            if t.last:
                                if t.sample:
                                    P.dma("sp", k_o[l, s, :, 112:128], KN32[:, 0:16], [R["KN32"]], [RO])
                                else:
                                    P.dma("sp", k_o[l, s], KN32[:, NT - 128:NT], [R["KN32"]], [RO])
                    if pend:
                        pend.pop(0)()
                    pend.append(stat_part)
                while pend:
                    pend.pop(0)()

            def ph_vdt():
                P.tag = pre + "vdt"
                for b in range(nblk):
                    bank = (B_Y, B_ST, B_GS, B_T1)[b % 4]
                    for k in range(8):
                        MM(BK[bank][:QB, 0:136], HTB[:, k, t.c0 + b * QB:t.c0 + (b + 1) * QB], WIN[:, k, 1920:2056],
                           k == 0, k == 7, [R["WIN"], RH], [BKr[bank]])
                    ACP(vb[:QB, 1 + b, :, :], BK[bank][:QB, 0:128].rearrange("p (g d) -> p g d", g=2), [BKr[bank]], [RVB])
                    if t.last and b == nblk - 1:
                        VCP(V32[:QB, :], BK[bank][:QB, 0:128], [BKr[bank]], [R["V32"]])
                        if t.sample:
                            P.dma("sp", v_o[l, s, 112:128, :], V32[0:16, :], [R["V32"]], [RO])
                        else:
                            P.dma("sp", v_o[l, s], V32[:, :], [R["V32"]], [RO])
                    TT(DTR[:QB, b * 8:(b + 1) * 8], BK[bank][:QB, 128:136], TVB[:QB, l, 0:8], ALU.add,
                       [BKr[bank], R["TVB"]], [R["DTR"]])

            def ph_dtp1():
                P.tag = pre + "dtp"
                STT(DTA[:QB, :nb8], DTR[:QB, :nb8], -1.0, DTR[:QB, :nb8], ALU.mult, ALU.max, [R["DTR"]], [R["DTA"]])
                ACT(DTA[:QB, :nb8], DTA[:QB, :nb8], AF.Exp, [R["DTA"]], [R["DTA"]], scale=-1.0)
                ACT(DTA[:QB, :nb8], DTA[:QB, :nb8], AF.Ln, [R["DTA"]], [R["DTA"]], bias=1.0)
                STT(DT[:QB, :nb8], DTR[:QB, :nb8], 0.0, DTA[:QB, :nb8], ALU.max, ALU.add, [R["DTR"], R["DTA"]], [R["DT"]])
                TT(ADT[:QB, :nb8].rearrange("p (b h) -> p b h", h=8), DT[:QB, :nb8].rearrange("p (b h) -> p b h", h=8),
                   ANEG[:QB, l, :].unsqueeze(1).to_broadcast([QB, nblk, 8]), ALU.mult, [R["DT"], R["TVB"]], [R["ADT"]])

            def ph_dtp2():
                P.tag = pre + "dtp"
                MM(BK[B_SM][:QB, 0:nb8], TRI[:QB, :QB], ADT[:QB, :nb8], True, True, [R["ADT"], R["CONST"]], [BKr[B_SM]])
                VCP(ACS[:QB, :nb8], BK[B_SM][:QB, 0:nb8], [BKr[B_SM]], [R["ACS"]])
                TS(NACS[:QB, :nb8], ACS[:QB, :nb8], -1.0, None, ALU.mult, None, [R["ACS"]], [R["NACS"]])
                VCP(ACSH[:QB, :nb8], ACS[:QB, :nb8], [R["ACS"]], [R["NACS"]])
                TT(ACSL[:QB, :nb8], ACS[:QB, :nb8], ACSH[:QB, :nb8], ALU.subtract, [R["ACS"], R["NACS"]], [R["NACS"]])
                TS(NACSH[:QB, :nb8], ACSH[:QB, :nb8], -1.0, None, ALU.mult, None, [R["NACS"]], [R["NACS"]])
                TS(NACSL[:QB, :nb8], ACSL[:QB, :nb8], -1.0, None, ALU.mult, None, [R["NACS"]], [R["NACS"]])
                SEL = SEL128 if QB == 128 else SEL16
                MM(BK[B_SM][:, 32:32 + nb8], SEL[:QB, :], ACS[:QB, :nb8], True, True, [R["ACS"], R["CONST"]], [BKr[B_SM]])
                TT(WDEC[:QB, :nb8], BK[B_SM][:QB, 32:32 + nb8], ACS[:QB, :nb8], ALU.subtract, [BKr[B_SM], R["ACS"]], [R["WDEC"]])
                ACT(DECF[:, :nb8], BK[B_SM][:, 32:32 + nb8], AF.Exp, [BKr[B_SM]], [R["DECF"]])

            def ph_dtp3():
                P.tag = pre + "dtp"
                ACT(WDEC[:QB, :nb8], WDEC[:QB, :nb8], AF.Exp, [R["WDEC"]], [R["WDEC"]])
                TT(DTW[:QB, :nb8], DT[:QB, :nb8], WDEC[:QB, :nb8], ALU.mult, [R["DT"], R["WDEC"]], [R["DTW"]])
                ACT(EACS[:QB, :nb8], ACS[:QB, :nb8], AF.Exp, [R["ACS"]], [R["EACS"]])
                dv = DECF[:, :nb8].rearrange("p (b h) -> p b h", h=8)
                VCP(DECSEL[0:64, :nblk, :], dv[0:64, :, 0:4], [R["DECF"]], [R["DECSEL"]])
                VCP(DECSEL[64:128, :nblk, :], dv[64:128, :, 4:8], [R["DECF"]], [R["DECSEL"]])


            def hook(k_):
                for fn_ in inj[k_]:
                    fn_()
            ph_vdt()
            hook("a")
            ph_dtp1()
            ph_xbc()
            hook("b")
            ph_dtp2()
            ph_qk()
            hook("c")
            ph_dtp3()
            hook("d")
            P.tag = pre + "ssd"
            W4 = 4 * QA

            def pieces_of(c):
                if t.sample:
                    pcs = [(0, 0, 128, 0, 0, 0), (1, 128, 16, 0, 1, 1)]
                elif c % 2 == 0:
                    pcs = [(0, c * 64, 128, 0, c // 2, 0), (1, 128 + c * 64, 128, 0, c // 2 + 1, 1)]
                else:
                    pcs = [(0, 128 + (c - 3) * 64, 128, 0, (c - 1) // 2, 2),
                           (1, 128 + (c - 1) * 64, 128, 0, (c - 1) // 2 + 1, 3)]
                if t.first and (not t.sample) and c < 2:
                    pcs = pcs[1:]
                return pcs

            def attn_A(c):
                q0 = c * QA
                pieces = pieces_of(c)
                sb_ = (B_D0, B_D1) if c % 2 == 0 else (B_ST, B_T1)
                pt, ptr = (PT, R["PT"]) if c % 2 == 0 else (PT1, R["PT1"])
                for g in range(2):
                    bank = sb_[g]
                    for (pi_, ktc, nk, pb, vbi, bi) in pieces:
                        o = BK[bank][pb:pb + nk, pi_ * W4:(pi_ + 1) * W4]
                        MM(o, kt[g][:, ktc:ktc + nk], QN[:, :, q0:q0 + QA], True, False,
                           [RKT, R["QN"]], [BKr[bank]])
                        MM(o, IDB[:, pb:pb + nk], BIAS[bi][:, g * 4:(g + 1) * 4, 0:QA], False, True,
                           [R["CONST"]], [BKr[bank]])
                for g in range(2):
                    bank = sb_[g]
                    ACT(pt[:, g, :, :, :QA], BK[bank][:, 0:2 * W4].rearrange("p (a j q) -> p a j q", a=2, j=4),
                        AF.Exp, [BKr[bank]], [ptr])

            def attn_C(c):
                q0 = c * QA
                pieces = pieces_of(c)
                pt, ptr = (PT, R["PT"]) if c % 2 == 0 else (PT1, R["PT1"])
                bden = B_GS if c % 2 == 0 else B_SM
                bo = B_Y if c % 2 == 0 else B_T2
                for g in range(2):
                    o = BK[bden][:, g * W4:(g + 1) * W4]
                    for ii, (pi_, ktc, nk, pb, vbi, bi) in enumerate(pieces):
                        MM(o, ONESB[pb:pb + nk, :], pt[pb:pb + nk, g, pi_, :, :QA], ii == 0, False,
                           [ptr, R["CONST"]], [BKr[bden]])
                    MM(o, SEL128[:, :], ESK[:, l, g * 4:(g + 1) * 4].unsqueeze(2).to_broadcast([128, 4, QA]),
                       False, True, [R["TVB"], R["CONST"]], [BKr[bden]])
                rd, rdr = tmpf()
                ACT(rd[:, :8 * QA], BK[bden][:, 0:8 * QA], AF.Ln, [BKr[bden]], [rdr])
                ACT(rd[:, :8 * QA], rd[:, :8 * QA], AF.Exp, [rdr], [rdr], scale=-1.0)
                for h in range(8):
                    g, r = h // 4, h % 4
                    o = BK[bo][(h % 2) * 64:(h % 2 + 1) * 64, (h // 2) * QA:(h // 2 + 1) * QA]
                    for ii, (pi_, ktc, nk, pb, vbi, bi) in enumerate(pieces):
                        MM(o, vb[pb:pb + nk, vbi, g, :], pt[pb:pb + nk, g, pi_, r, :QA], ii == 0, ii == len(pieces) - 1,
                           [RVB, ptr], [BKr[bo]])
                rd3 = rd[:, :8 * QA].rearrange("p (h q) -> p h q", h=8)
                TT(MIX[0:64, 4:8, mo + q0:mo + q0 + QA], BK[bo][0:64, 0:4 * QA].rearrange("p (m q) -> p m q", m=4),
                   rd3[0:64, 0:8:2, :], ALU.mult, [BKr[bo], rdr], [R["MIX"]])
                TT(MIX[64:128, 4:8, mo + q0:mo + q0 + QA], BK[bo][64:128, 0:4 * QA].rearrange("p (m q) -> p m q", m=4),
                   rd3[64:128, 1:8:2, :], ALU.mult, [BKr[bo], rdr], [R["MIX"]])

            def _attn_first():
                old = P.tag; P.tag = pre + "attn"
                attn_A(0)
                P.tag = old
            attn_first = [_attn_first]
            def ssd_E(b):
                bc = slice(b * QB, (b + 1) * QB)
                szb, szr = SZ[b % 2], R_SZ[b % 2]
                i2 = b % 2
                for k in range(8):
                    MM(BK[B_T2][:QB, :], HTB[:, k, t.c0 + b * QB:t.c0 + (b + 1) * QB], WIN[:, k, 1408:1920],
                       k == 0, k == 7, [R["WIN"], RH], [BKr[B_T2]])
                ACT(szb[:QB, :], BK[B_T2][:QB, :], AF.Exp, [BKr[B_T2]], [szr], scale=-1.0)
                ACT(szb[:QB, :], szb[:QB, :], AF.Ln, [szr], [szr], bias=1.0)
                ACT(szb[:QB, :], szb[:QB, :], AF.Exp, [szr], [szr], scale=-1.0)
                TT(szb[:QB, :], BK[B_T2][:QB, :], szb[:QB, :], ALU.mult, [BKr[B_T2], szr], [szr])
                MM(BK[B_T1][:QB, 384:384 + QB], XS[0:64, 4, bc], XS[0:64, 5, bc], True, True, [R["XS"]], [BKr[B_T1]])
                MM(BK[B_SM][:QB, 64:64 + QB], XS[64:128, 4, bc], XS[64:128, 5, bc], True, True, [R["XS"]], [BKr[B_SM]])
                for c in range(4):
                    TR(BT1v[:QB, c * 128:(c + 1) * 128], XS[:, c, bc], IDB[:], [R["XS"], R["CONST"]], [BKr[B_T1]])
                TR(BT1v[:QB, 512:640], XS[:, 4, bc], IDB[:], [R["XS"], R["CONST"]], [BKr[B_T1]])
                ACP(GSB[:QB, 0, :QB], BK[B_T1][:QB, 384:384 + QB], [BKr[B_T1]], [R["GSB"]])
                ACP(GSB[:QB, 1, :QB], BK[B_SM][:QB, 64:64 + QB], [BKr[B_SM]], [R["GSB"]])
                xt3 = BT1v[:QB, 0:512].rearrange("p (h d) -> p h d", h=8)

                def bc8(tile_):
                    return tile_[:QB, b * 8:(b + 1) * 8].unsqueeze(2).to_broadcast([QB, 8, 64])
                TT(XDT[i2][:QB, :].rearrange("p (h d) -> p h d", h=8), xt3, bc8(DT), ALU.mult, [BKr[B_T1], R["DT"]], [R_XDT[i2]])
                TT(XW[i2][:QB, :].rearrange("p (h d) -> p h d", h=8), xt3, bc8(DTW), ALU.mult, [BKr[B_T1], R["DTW"]], [R_XW[i2]])
                TT(XD[i2][:QB, :].rearrange("p (h d) -> p h d", h=8), xt3,
                   TVB[:QB, l, 16:24].unsqueeze(2).to_broadcast([QB, 8, 64]), ALU.mult, [BKr[B_T1], R["TVB"]], [R_XD[i2]])
                ACP(BTOK[i2][:QB, :], BT1v[:QB, 512:640], [BKr[B_T1]], [R_BTOK[i2]])
                for h in range(8):
                    bank = B_D0 if h < 4 else B_D1
                    r = h % 4
                    o = BK[bank][:QB, r * QB:(r + 1) * QB]
                    col = b * 8 + h
                    MM(o, ACSH[:QB, col:col + 1].to_broadcast([QB, QB]), IDB[:QB, :QB], True, False,
                       [R["NACS"], R["CONST"]], [BKr[bank]])
                    MM(o, ACSL[:QB, col:col + 1].to_broadcast([QB, QB]), IDB[:QB, :QB], False, False,
                       [R["NACS"], R["CONST"]], [BKr[bank]])
                    MM(o, IDB[:QB, :QB], NACSH[:QB, col:col + 1].to_broadcast([QB, QB]), False, False,
                       [R["NACS"], R["CONST"]], [BKr[bank]])
                    MM(o, IDB[:QB, :QB], NACSL[:QB, col:col + 1].to_broadcast([QB, QB]), False, False,
                       [R["NACS"], R["CONST"]], [BKr[bank]])
                    MM(o, IDB[:QB, :QB], MASKN[:QB, :QB], False, True, [R["CONST"]], [BKr[bank]])
                for g in range(2):
                    bank = B_D0 if g == 0 else B_D1
                    ACT(LT[:QB, 4 * g:4 * g + 4, :QB], BK[bank][:QB, 0:4 * QB].rearrange("p (h t) -> p h t", h=4),
                        AF.Exp, [BKr[bank]], [R["LT"]])
                for g in range(2):
                    TT(MT[i2][:QB, 4 * g:4 * g + 4, :QB], LT[:QB, 4 * g:4 * g + 4, :QB],
                       GSB[:QB, g, :QB].unsqueeze(1).to_broadcast([QB, 4, QB]), ALU.mult, [R["LT"], R["GSB"]], [R_MT[i2]])

            def ssd_M(b):
                bc = slice(b * QB, (b + 1) * QB)
                szb, szr = SZ[b % 2], R_SZ[b % 2]
                i2 = b % 2
                MM(BK[B_Y][:QB, :], IDB[:QB, :QB], XD[i2][:QB, :], True, False, [R_XD[i2], R["CONST"]], [BKr[B_Y]], skip=True)
                for h in range(8):
                    MM(BK[B_Y][:QB, h * 64:(h + 1) * 64], MT[i2][:QB, h, :QB], XDT[i2][:QB, h * 64:(h + 1) * 64], False, h == 7,
                       [R_MT[i2], R_XDT[i2]], [BKr[B_Y]], skip=True)
                MM(BK[B_ST][:QB, :], XS[:, 5, bc], HTBF[l][:, :], True, True, [R["XS"], RHB], [BKr[B_ST]])
                for g in range(2):
                    MM(BK[B_GS][g * 64:(g + 1) * 64, 256:512], BTOK[i2][:QB, g * 64:(g + 1) * 64],
                       XW[i2][:QB, g * 256:(g + 1) * 256], True, True, [R_BTOK[i2], R_XW[i2]], [BKr[B_GS]])
                hv = HT32[l][:].rearrange("p (h d) -> p h d", h=4)
                P.op("pool", lambda e, a=hv, b_=DECSEL[:, b, :].unsqueeze(2).to_broadcast([128, 4, 64]):
                     e.tensor_tensor(out=a, in0=a, in1=b_, op=ALU.mult), [RHT, R["DECSEL"]], [RHT])
                TT(HT32[l][:], HT32[l][:], BK[B_GS][:, 256:512], ALU.add, [RHT, BKr[B_GS]], [RHT])
                ACP(HTBF[l][0:64, 0:256], HT32[l][0:64, :], [RHT], [RHB])
                ACP(HTBF[l][64:128, 256:512], HT32[l][64:128, :], [RHT], [RHB])
                t1, t1r = tmpf()
                ev = EACS[:QB, b * 8:(b + 1) * 8]
                TT(t1[:QB, :].rearrange("p (h d) -> p h d", h=8), BK[B_ST][:QB, :].rearrange("p (h d) -> p h d", h=8),
                   ev.unsqueeze(2).to_broadcast([QB, 8, 64]), ALU.mult, [BKr[B_ST], R["EACS"]], [t1r])
                TT(t1[:QB, :], BK[B_Y][:QB, :], t1[:QB, :], ALU.add, [BKr[B_Y], t1r], [t1r])
                TT(t1[:QB, :], t1[:QB, :], szb[:QB, :], ALU.mult, [t1r, szr], [t1r])
                ACT(Y4[i2][:QB, :], t1[:QB, :], AF.Square, [t1r], [R_Y4[i2], R["SSQ"]], accum=SSQ[:QB, 0:1])
                ACT(SSQ[:QB, 1:2], SSQ[:QB, 0:1], AF.Ln, [R["SSQ"]], [R["SSQ"]], scale=1.0 / 512, bias=EPS)
                ACT(SSQ[:QB, 1:2], SSQ[:QB, 1:2], AF.Exp, [R["SSQ"]], [R["SSQ"]], scale=-0.5)
                STT(Y4[i2][:QB, :], t1[:QB, :], SSQ[:QB, 1:2], TVB[:QB, l, 32:544], ALU.mult, ALU.mult,
                    [t1r, R["SSQ"], R["TVB"]], [R_Y4[i2]])

            def ssd_L(b, t2=False):
                i2 = b % 2
                old_tag = P.tag
                P.tag = pre + "ssd"
                vw, off, br = (BT2v, 0, BKr[B_T2]) if t2 else (BSMv, 512, BKr[B_SM])
                for c in range(4):
                    TR(vw[:, off + c * QB:off + (c + 1) * QB], Y4[i2][:QB, c * 128:(c + 1) * 128], IDB[:QB, :QB],
                       [R_Y4[i2], R["CONST"]], [br])
                ACP(MIX[:, 0:4, mo + b * QB:mo + (b + 1) * QB], vw[:, off:off + 4 * QB].rearrange("p (c t) -> p c t", c=4),
                    [br], [R["MIX"]])
                P.tag = old_tag
            deferL = [nblk - 2, nblk - 1] if nblk >= 3 else []

            for step in range(nblk + 2):
                if step < nblk:
                    ssd_E(step)
                if step == nblk:
                    attn_first[0]()
                if 0 <= step - 2 < nblk and (step - 2) not in deferL:
                    ssd_L(step - 2)
                if 0 <= step - 1 < nblk:
                    ssd_M(step - 1)
            if t.last:
                P.dma("sp", ssm_o[l, s], HT32[l][:], [RHT], [RO])

            P.tag = pre + "attn"
            for c in range(nch):
                if c + 1 < nch:
                    attn_A(c + 1)
                if c == 0 and deferL:
                    ssd_L(deferL[0], t2=True)
                attn_C(c)
                if c == 0 and deferL:
                    ssd_L(deferL[1], t2=True)
            if not t.last:
                VCP(kt[0][0:64, 0:128], kt[0][0:64, NT:NT + 128], [RKT], [RKT])
                VCP(kt[1][64:128, 0:128], kt[1][64:128, NT:NT + 128], [RKT], [RKT])
                VCP(vb[:, 0, :, :], vb[:, nblk, :, :], [RVB], [RVB])
            if not t.sample:
                wout_phase(l, [t], inj.get("w"))

        def wout_phase(l, tl, nstages=None):
            P.tag = ("S:" if tl[0].sample else "P:") + "wout"
            wtag = P.tag
            c_lo = tl[0].c0
            ncol = sum(t_.NT for t_ in tl)
            for d in range(8):
                wo, wor = WO[d % 4], R_WO[d % 4]
                bank = B_D0 if d % 2 == 0 else B_D1
                for m in range(8):
                    MM(BK[bank][:, :ncol], wo[:, m, :], MIX[:, m, :ncol], m == 0, m == 7, [wor, R["MIX"]], [BKr[bank]])
                if d + 4 < 8:
                    load_wo(l, d + 4)
                if nstages is not None and d in (0, 2, 4):
                    nstages[d // 2]()
                    P.tag = wtag
                for t_ in tl:
                    o0 = t_.c0 - c_lo
                    cs_ = slice(t_.c0, t_.c0 + t_.NT)
                    STT(XT[:, d, cs_], BK[bank][:, o0:o0 + t_.NT], MOD[:, l, 2 * 8 + d, t_.seq:t_.seq + 1], XT[:, d, cs_],
                        ALU.mult, ALU.add, [BKr[bank], R["MOD"], R_X[t_.idx][d]], [R_X[t_.idx][d]])

        def load_wu(l, f):
            P.dma("pool", WU[f % 3][:], wup_d[l, :, f], (), [R_WU[f % 3]])

        def load_wd(l, n):
            P.dma("pool", WD[n % 2][:], wdn_d[l, :, n // 8, n % 8], (), [R_WD[n % 2]])

        JT = Tile()
        JT.joint = True; JT.sample = True; JT.NT = 64; JT.c0 = 1024; JT.slot = 2 * 514; JT.idx = 2
        JT.first = True; JT.last = True; JT.seq = -1

        def ffn(l, tiles_all, prefetch):
            fence()
            pending = []
            tiles = [t_ for t_ in tiles_all if not t_.sample]
            stl = [t_ for t_ in tiles_all if t_.sample]
            if stl:
                tiles = tiles + [JT]
            wd_seq = [(0, d_) for d_ in range(8)] + [(1, d_) for d_ in range(8)]
            wd_step = [-1]
            it = 0
            P.tag = "ffn_up"
            for hf in range(2):
                P.tag = "ffn_up"
                for fi in range(11):
                    f = hf * 11 + fi
                    if f + 2 < 22:
                        load_wu(l, f + 2)
                    if prefetch:
                        prefetch.pop(0)()
                    wu, wur = WU[f % 3], R_WU[f % 3]
                    dg, dgr = DG[f % 2], R_DG[f % 2]
                    for ug in range(2):
                        for tap in range(3):
                            col = 94 + (ug * 22 + f) * 3 + tap
                            TS(dg[:, ug * 3 + tap, :], IDB[:], VEC[:, l, col:col + 1], None, ALU.mult, None,
                               [R["CONST"], R["VEC"]], [dgr])
                    ugb = UG[f % 2]
                    for ti, t in enumerate(tiles):
                        NT = t.NT
                        cs = slice(t.c0, t.c0 + NT)
                        so = t.slot
                        bu, bg, bcu, bcg = (B_D0, B_D1, B_GS, B_SM) if it % 2 == 0 else (B_Y, B_ST, B_T1, B_T2)
                        sg, sgr = SG[it % 2], R_SG[it % 2]
                        ugr = R_UGT[f % 2][t.idx]
                        it += 1
                        if getattr(t, "joint", False):
                            RHj = [R_H[i_] for i_ in range(2, 6)]
                            uv = [ugb[:, u_, so:so + 72].rearrange("p (j w) -> p j w", w=18) for u_ in range(2)]
                            VCP(ugb[:, :, so:so + 72].rearrange("p u (j w) -> p u j w", w=18)[:, :, :, 0:2],
                                CFH[:, l, :, :].rearrange("p j (f u k) -> p f u j k", f=22, u=2)[:, f], [R["CFH"]], [ugr])
                            for k in range(8):
                                MM(BK[bu][:, :NT], wu[:, k, 0:128], HTB[:, k, cs], k == 0, k == 7, [wur] + RHj, [BKr[bu]])
                            for k in range(8):
                                MM(BK[bg][:, :NT], wu[:, k, 128:256], HTB[:, k, cs], k == 0, k == 7, [wur] + RHj, [BKr[bg]])
                            pu = BK[bu][:, 0:64].rearrange("p (j w) -> p j w", w=16)
                            pg = BK[bg][:, 0:64].rearrange("p (j w) -> p j w", w=16)
                            ACP(uv[0][:, :, 2:18], pu, [BKr[bu]], [ugr])
                            ACP(uv[1][:, :, 2:18], pg, [BKr[bg]], [ugr])
                            VCP(CFS[:, :, f, 0, :], pu[:, :, 14:16], [BKr[bu]], [R_CFFO[2]])
                            VCP(CFS[:, :, f, 1, :], pg[:, :, 14:16], [BKr[bg]], [R_CFFO[2]])

                            def part2j(NT=NT, cs=cs, bcu=bcu, bcg=bcg, sg=sg, sgr=sgr, dg=dg, dgr=dgr, uv=uv, ugr=ugr, f=f, fi=fi):
                                for tap in range(3):
                                    MM(BK[bcu][:, :NT], dg[:, tap, :], uv[0][:, :, tap:tap + 16], tap == 0, tap == 2,
                                       [dgr, ugr], [BKr[bcu]])
                                for tap in range(3):
                                    MM(BK[bcg][:, :NT], dg[:, 3 + tap, :], uv[1][:, :, tap:tap + 16], tap == 0, tap == 2,
                                       [dgr, ugr], [BKr[bcg]])
                                ACT(sg[:, :NT], BK[bcg][:, :NT], AF.Silu, [BKr[bcg], R["VEC"]], [sgr],
                                    bias=VEC[:, l, 226 + 22 + f:227 + 22 + f])
                                STT(ACTT[:, fi, cs], BK[bcu][:, :NT], VEC[:, l, 226 + f:227 + f], sg[:, :NT], ALU.add, ALU.mult,
                                    [BKr[bcu], R["VEC"], sgr], [R_AT[2]])
                            if pending:
                                pending.pop(0)()
                            pending.append(part2j)
                            continue
                        if t.sample:
                            VCP(ugb[:, :, so:so + 2],
                                CFH[:, l, t.sj, :].rearrange("p (f u k) -> p f u k", f=22, u=2)[:, f, :, :], [R["CFH"]], [ugr])
                        elif t.first:
                            MSET("dve", ugb[:, :, so:so + 2], 0.0, [ugr])
                        elif ti == 0:
                            VCP(ugb[:, :, so:so + 2], FH[l][:, f, :, :], [R["FH%d" % l]], [ugr])
                        else:
                            pso = tiles[ti - 1].slot + tiles[ti - 1].NT
                            VCP(ugb[:, :, so:so + 2], ugb[:, :, pso:pso + 2], [R_UGT[f % 2][tiles[ti - 1].idx]], [ugr])
                        for k in range(8):
                            MM(BK[bu][:, :NT], wu[:, k, 0:128], HTB[:, k, cs], k == 0, k == 7, [wur, R_H[t.idx]], [BKr[bu]])
                        for k in range(8):
                            MM(BK[bg][:, :NT], wu[:, k, 128:256], HTB[:, k, cs], k == 0, k == 7, [wur, R_H[t.idx]], [BKr[bg]])
                        ACP(ugb[:, 0, so + 2:so + 2 + NT], BK[bu][:, :NT], [BKr[bu]], [ugr])
                        ACP(ugb[:, 1, so + 2:so + 2 + NT], BK[bg][:, :NT], [BKr[bg]], [ugr])
                        if t.last:
                            VCP(CFFO[t.idx][:, f, 0, :], BK[bu][:, NT - 2:NT], [BKr[bu]], [R_CFFO[t.idx]])
                            VCP(CFFO[t.idx][:, f, 1, :], BK[bg][:, NT - 2:NT], [BKr[bg]], [R_CFFO[t.idx]])
                        elif ti == len(tiles) - 1 or tiles[ti + 1].sample:
                            VCP(FH[l][:, f, :, :], ugb[:, :, so + NT:so + NT + 2], [ugr], [R["FH%d" % l]])
                        def part2(NT=NT, cs=cs, so=so, bcu=bcu, bcg=bcg, sg=sg, sgr=sgr, dg=dg, dgr=dgr, ugb=ugb, ugr=ugr,
                                  f=f, fi=fi, t=t):
                            for tap in range(3):
                                MM(BK[bcu][:, :NT], dg[:, tap, :], ugb[:, 0, so + tap:so + tap + NT], tap == 0, tap == 2,
                                   [dgr, ugr], [BKr[bcu]])
                            for tap in range(3):
                                MM(BK[bcg][:, :NT], dg[:, 3 + tap, :], ugb[:, 1, so + tap:so + tap + NT], tap == 0, tap == 2,
                                   [dgr, ugr], [BKr[bcg]])
                            ACT(sg[:, :NT], BK[bcg][:, :NT], AF.Silu, [BKr[bcg], R["VEC"]], [sgr],
                                bias=VEC[:, l, 226 + 22 + f:227 + 22 + f])
                            STT(ACTT[:, fi, cs], BK[bcu][:, :NT], VEC[:, l, 226 + f:227 + f], sg[:, :NT], ALU.add, ALU.mult,
                                [BKr[bcu], R["VEC"], sgr], [R_AT[t.idx]])
                        if pending:
                            pending.pop(0)()
                        pending.append(part2)
                while pending:
                    pending.pop(0)()
                P.tag = "ffn_down"
                order = [(d_, t_) for d_ in range(8) for t_ in tiles]
                prev_piece = None
                for (d, t) in order:
                    piece = (hf, d)
                    if piece != prev_piece:
                        wd_step[0] += 1
                        nxt_i = wd_step[0] + 1
                        if nxt_i < len(wd_seq):
                            hf_n, d_n = wd_seq[nxt_i]
                            P.dma("pool", WD[nxt_i % 2][:], wdn_d[l, :, hf_n, d_n], (), [R_WD[nxt_i % 2]])
                        prev_piece = piece
                    n = wd_step[0]
                    wd, wdr = WD[n % 2], R_WD[n % 2]
                    NT = t.NT
                    cs = slice(t.c0, t.c0 + NT)
                    bank = it % 8
                    it += 1
                    for fi in range(11):
                        MM(BK[bank][:, :NT], wd[:, fi, :], ACTT[:, fi, cs], fi == 0, fi == 10,
                           [wdr, R_AT[t.idx]], [BKr[bank]])
                    if getattr(t, "joint", False):
                        for t_ in stl:
                            cs_ = slice(t_.c0, t_.c0 + 16)
                            o0 = t_.c0 - 1024
                            STT(XT[:, d, cs_], BK[bank][:, o0:o0 + 16], MOD[:, l, 5 * 8 + d, t_.seq:t_.seq + 1], XT[:, d, cs_],
                                ALU.mult, ALU.add, [BKr[bank], R["MOD"], R_X[t_.idx][d]], [R_X[t_.idx][d]])
                        continue
                    STT(XT[:, d, cs], BK[bank][:, :NT], MOD[:, l, 5 * 8 + d, t.seq:t.seq + 1], XT[:, d, cs],
                        ALU.mult, ALU.add, [BKr[bank], R["MOD"], R_X[t.idx][d]], [R_X[t.idx][d]])
            while prefetch:
                prefetch.pop(0)()
            for t in tiles_all:
                if t.last:
                    rr = R_CFFO[2] if t.sample else R_CFFO[t.idx]
                    P.dma("sp", cf_o[l, t.seq], CFFO[t.idx].rearrange("p f u k -> p (f u k)"), [rr], [RO])

        for pi_, tiles in enumerate(passes):
            c0d = pass_dram0[pi_]
            ncol = sum(t.NT for t in tiles)
            for t in tiles:
                P.dma("sp", XT[:, :, t.c0:t.c0 + t.NT],
                      xT_d[:, :, c0d + t.c0:c0d + t.c0 + t.NT].rearrange("c p t -> p c t"), (),
                      [R_X[t.idx][k] for k in range(8)])
            if pi_ == 3:
                P.dma("pool", CFH[:], cf0_d.rearrange("l j p n -> p l j n"), (), [R["CFH"]])
            for l in range(2):
                fence()
                load_wu(l, 0)
                load_wu(l, 1)
                load_wd(l, 0)
                n1 = norm_stages(l, tiles[0], 1, 0)
                for fn_ in n1:
                    fn_()
                n2prev = None
                for i, t in enumerate(tiles):
                    inj = {"a": [], "b": [], "c": [], "d": []}
                    if n2prev is not None:
                        inj["a"].append(n2prev[1])
                        inj["b"].append(n2prev[2])
                    if i + 1 < len(tiles):
                        n1n = norm_stages(l, tiles[i + 1], 1, 0)
                        if t.sample:
                            inj["b"].append(n1n[0])
                            inj["c"].append(n1n[1])
                            inj["d"].append(n1n[2])
                        else:
                            inj["w"] = n1n
                    mixer(l, t, inj)
                    if t.sample:
                        n2prev = None
                    else:
                        n2prev = norm_stages(l, t, 4, 3)
                        n2prev[0]()
                if n2prev is not None:
                    n2prev[1]()
                    n2prev[2]()
                stl = [t for t in tiles if t.sample]
                if stl:
                    wout_phase(l, stl)
                    for t in stl:
                        for fn_ in norm_stages(l, t, 4, 3):
                            fn_()
                nxt = pi_ * 2 + l + 1
                pf = []
                if nxt < 8:
                    pf = [(lambda k=k, ln=nxt % 2: load_win_piece(ln, k)) for k in range(8)]
                ffn(l, tiles, pf)
            for t in tiles:
                P.dma("sp", yT_d[:, :, c0d + t.c0:c0d + t.c0 + t.NT].rearrange("c p t -> p c t"),
                      XT[:, :, t.c0:t.c0 + t.NT], [R_X[t.idx][k] for k in range(8)], [RO])
        P.wait_all("sp", [RO])
        P.emit()
    return nc


_NC_CACHE = {}


def _fm(v):
    sh = v.shape
    c = sh[-1] // 128
    a = v.reshape(sh[:-1] + (c, 128))
    return np.moveaxis(a, -1, 0)


def kernel(x_prompt, x_sample, c_prompt, c_sample, state_ssm, cache_conv_ssd, cache_attn_k,
           cache_attn_v, cache_conv_ffn, w_ada, b_ada, norm_mix, w_in, conv_ssd_w, conv_ssd_b,
           dt_bias, a_log, d_skip, ssd_norm, q_norm, k_norm, sinks, w_out, norm_ffn, w_up,
           conv_ffn_w, conv_ffn_b, w_down):
    f32 = np.float32
    A = lambda v: np.asarray(v, dtype=f32)
    x_prompt, x_sample, c_prompt, c_sample = A(x_prompt), A(x_sample), A(c_prompt), A(c_sample)
    state_ssm, cache_conv_ssd, cache_attn_k, cache_attn_v, cache_conv_ffn = (
        A(state_ssm), A(cache_conv_ssd), A(cache_attn_k), A(cache_attn_v), A(cache_conv_ffn))
    w_ada, b_ada, norm_mix, w_in, conv_ssd_w, conv_ssd_b = A(w_ada), A(b_ada), A(norm_mix), A(w_in), A(conv_ssd_w), A(conv_ssd_b)
    dt_bias, a_log, d_skip, ssd_norm, q_norm, k_norm, sinks = A(dt_bias), A(a_log), A(d_skip), A(ssd_norm), A(q_norm), A(k_norm), A(sinks)
    w_out, norm_ffn, w_up, conv_ffn_w, conv_ffn_b, w_down = A(w_out), A(norm_ffn), A(w_up), A(conv_ffn_w), A(conv_ffn_b), A(w_down)

    wada = np.ascontiguousarray(w_ada.reshape(2, 8, 128, 6144).transpose(0, 2, 1, 3))
    qcols = []
    for j in range(4):
        qcols += list(range(1288 + j * 64, 1288 + (j + 1) * 64))
        qcols += list(range(1288 + (4 + j) * 64, 1288 + (5 + j) * 64))
    perm = (list(range(512, 1280)) + qcols + list(range(1800, 1928)) + list(range(0, 512))
            + list(range(1928, 2056)) + list(range(1280, 1288)))
    win = np.ascontiguousarray(w_in[:, :, perm].reshape(2, 8, 128, 2056).transpose(0, 2, 1, 3))
    wout = np.ascontiguousarray(w_out.reshape(2, 8, 128, 8, 128).transpose(0, 2, 3, 1, 4)).reshape(2, 128, 8, 1024)
    wu4 = w_up.reshape(2, 8, 128, 2, 22, 128)
    wup = np.ascontiguousarray(wu4.transpose(0, 2, 4, 1, 3, 5)).reshape(2, 128, 22, 8, 256)
    wd6 = w_down.reshape(2, 2, 11, 128, 8, 128)
    wdn = np.ascontiguousarray(wd6.transpose(0, 3, 1, 4, 2, 5))
    vec = np.zeros((2, 128, 272), f32)
    vec[:, :, 0:8] = np.moveaxis(_fm(norm_mix), 0, 1)
    vec[:, :, 8:16] = np.moveaxis(_fm(norm_ffn), 0, 1)
    vec[:, :, 16:64] = np.moveaxis(_fm(b_ada), 0, 1)
    csw = _fm(conv_ssd_w)
    vec[:, :, 64:88] = csw.transpose(1, 0, 3, 2).reshape(2, 128, 24)
    vec[:, :, 88:94] = np.moveaxis(_fm(conv_ssd_b), 0, 1)
    cfw = _fm(conv_ffn_w)
    vec[:, :, 94:226] = cfw.transpose(1, 0, 3, 2).reshape(2, 128, 132)
    vec[:, :, 226:270] = np.moveaxis(_fm(conv_ffn_b), 0, 1)
    vec[:, :, 270] = np.concatenate([q_norm, q_norm], axis=1)
    vec[:, :, 271] = np.concatenate([k_norm, k_norm], axis=1)
    tv = np.ascontiguousarray(np.concatenate([dt_bias, a_log, d_skip, sinks, ssd_norm], axis=1))

    in_maps = []
    for c in range(NCORES):
        bp = [2 * c, 2 * c + 1]
        bs = [4 * c + j for j in range(4)]
        xs = np.concatenate([x_prompt[bp[0]], x_prompt[bp[1]], x_sample[bs].reshape(64, 1024)], axis=0)
        xT = np.ascontiguousarray(xs.T.reshape(8, 128, NTOK))
        call = np.concatenate([c_prompt[bp], c_sample[bs]], axis=0)
        cT = np.ascontiguousarray(call.T.reshape(8, 128, 6).transpose(1, 0, 2))
        st_ = state_ssm[:, bs].reshape(2, 4, 2, 4, 64, 64)
        hT0 = np.ascontiguousarray(st_.transpose(0, 1, 2, 5, 3, 4)).reshape(2, 4, 128, 256)
        cc_ = cache_conv_ssd[:, bs].reshape(2, 4, 3, 6, 128)
        cs0 = np.ascontiguousarray(cc_.transpose(0, 1, 4, 3, 2)).reshape(2, 4, 128, 18)
        kc_ = cache_attn_k[:, bs].reshape(2, 4, 128, 128)
        kcT = np.ascontiguousarray(kc_.transpose(0, 1, 3, 2))
        vc = np.ascontiguousarray(cache_attn_v[:, bs].reshape(2, 4, 128, 128))
        cf_ = cache_conv_ffn[:, bs].reshape(2, 4, 2, 2, 22, 128)
        cf0 = np.ascontiguousarray(cf_.transpose(0, 1, 5, 4, 3, 2)).reshape(2, 4, 128, 88)
        in_maps.append(dict(xT=xT, cT=cT, wada=wada, vec=vec, tv=tv, win=win, wout=wout, wup=wup, wdn=wdn,
                            hT0=hT0, cs0=cs0, kcT=kcT, vc=vc, cf0=cf0))

    if "nc" not in _NC_CACHE:
        _NC_CACHE["nc"] = build()
    nc = _NC_CACHE["nc"]
    res = run_bass_kernel_spmd(nc, in_maps, core_ids=list(range(NCORES)))
    rs = res.results

    y_p = np.zeros((16, 2048, 1024), f32); y_s = np.zeros((32, 16, 1024), f32)
    ssm_p = np.zeros((2, 16, 8, 64, 64), f32); ssm_s = np.zeros((2, 32, 8, 64, 64), f32)
    cs_p = np.zeros((2, 16, 3, 768), f32); cs_s = np.zeros((2, 32, 3, 768), f32)
    k_p = np.zeros((2, 16, 128, 2, 64), f32); k_s = np.zeros((2, 32, 128, 2, 64), f32)
    v_p = np.zeros((2, 16, 128, 2, 64), f32); v_s = np.zeros((2, 32, 128, 2, 64), f32)
    cf_p = np.zeros((2, 16, 2, 5632), f32); cf_s = np.zeros((2, 32, 2, 5632), f32)
    for c in range(NCORES):
        r = rs[c]
        yt = np.asarray(r["yT"]).reshape(1024, NTOK).T
        ssmT = np.asarray(r["ssmT"]).reshape(2, 6, 2, 64, 4, 64)
        ssm = ssmT.transpose(0, 1, 2, 4, 5, 3).reshape(2, 6, 8, 64, 64)
        cso = np.asarray(r["cso"]).reshape(2, 6, 128, 6, 3).transpose(0, 1, 4, 3, 2).reshape(2, 6, 3, 768)
        kTo = np.asarray(r["kTo"]).reshape(2, 6, 128, 128).transpose(0, 1, 3, 2).reshape(2, 6, 128, 2, 64)
        vo = np.asarray(r["vo"]).reshape(2, 6, 128, 2, 64)
        cfo = np.asarray(r["cfo"]).reshape(2, 6, 128, 22, 2, 2).transpose(0, 1, 5, 4, 3, 2).reshape(2, 6, 2, 5632)
        for i in range(2):
            b = 2 * c + i
            y_p[b] = yt[i * 2048:(i + 1) * 2048]
            ssm_p[:, b] = ssm[:, i]; cs_p[:, b] = cso[:, i]; k_p[:, b] = kTo[:, i]; v_p[:, b] = vo[:, i]; cf_p[:, b] = cfo[:, i]
        for j in range(4):
            b = 4 * c + j
            y_s[b] = yt[4096 + 16 * j:4096 + 16 * (j + 1)]
            ssm_s[:, b] = ssm[:, 2 + j]; cs_s[:, b] = cso[:, 2 + j]; k_s[:, b] = kTo[:, 2 + j]; v_s[:, b] = vo[:, 2 + j]
            cf_s[:, b] = cfo[:, 2 + j]
    return (y_p, y_s, ssm_p, ssm_s, cs_p, cs_s, k_p, k_s, v_p, v_s, cf_p, cf_s)
```
